# Optimizing a Trainium2 kernel written in Bass

```python
import math
import jax, jax.numpy as jnp
from jax import lax
import numpy as np

D_MODEL = 1024
BATCH = 4
SEQ = 4096
DEPTH = 1

N_META = 16
EPS = 1e-6
D_FF = 2816
MLA_HEADS = 8
MLA_Q_RANK = 256
MLA_KV_RANK = 128
MLA_NOPE = 64
MLA_ROPE = 32
MLA_V = 64
ROPE_THETA = 10000.0
Q_BLOCK = 128
GDN_HEADS = 8
GDN_DK = 64
GDN_DV = 64
CONV_K = 4
CHUNK = 64
SPLITS = (MLA_Q_RANK, MLA_KV_RANK, MLA_ROPE,
          GDN_HEADS * GDN_DK, GDN_HEADS * GDN_DK, GDN_HEADS * GDN_DV,
          GDN_HEADS, GDN_HEADS, GDN_HEADS * GDN_DV,
          D_MODEL, D_MODEL)
D_IN = sum(SPLITS)
GDN_CONV_CH = 2 * GDN_HEADS * GDN_DK + GDN_HEADS * GDN_DV

kernel_name = 'hybrid_mla_gdn_macaron_block'


def rmsnorm(x, w):
    x32 = x.astype(jnp.float32)
    y = x32 * lax.rsqrt(jnp.mean(x32 * x32, axis=-1, keepdims=True) + EPS)
    return (y * w.astype(jnp.float32)).astype(x.dtype)


def swiglu(x, w_gate, w_up, w_down):
    return (jax.nn.silu(x @ w_gate) * (x @ w_up)) @ w_down


def rope_tables(length):
    pos = jnp.arange(length, dtype=jnp.float32)
    inv = ROPE_THETA ** (-jnp.arange(0, MLA_ROPE, 2, dtype=jnp.float32) / MLA_ROPE)
    ang = pos[:, None] * inv[None, :]
    return jnp.cos(ang), jnp.sin(ang)


def apply_rope(x, cos, sin):
    half = x.shape[-1] // 2
    x1 = x[..., :half].astype(jnp.float32)
    x2 = x[..., half:].astype(jnp.float32)
    out = jnp.concatenate([x1 * cos - x2 * sin, x2 * cos + x1 * sin], axis=-1)
    return out.astype(x.dtype)


def mla(c_q_raw, c_kv_raw, k_rope_raw, q_norm, w_uq, kv_norm, w_ukv):
    b, length, _ = c_q_raw.shape
    q = (rmsnorm(c_q_raw, q_norm) @ w_uq).reshape(b, length, MLA_HEADS, MLA_NOPE + MLA_ROPE)
    kv = (rmsnorm(c_kv_raw, kv_norm) @ w_ukv).reshape(b, length, MLA_HEADS, MLA_NOPE + MLA_V)
    q_nope, q_rope = q[..., :MLA_NOPE], q[..., MLA_NOPE:]
    k_nope, v = kv[..., :MLA_NOPE], kv[..., MLA_NOPE:]
    cos, sin = rope_tables(length)
    q_rope = apply_rope(q_rope, cos[:, None, :], sin[:, None, :])
    k_rope = apply_rope(k_rope_raw, cos, sin)
    scale = (MLA_NOPE + MLA_ROPE) ** -0.5
    n_blk = -(-length // Q_BLOCK)
    l_pad = n_blk * Q_BLOCK

    def to_blocks(t):
        t = jnp.pad(t, ((0, 0), (0, l_pad - length), (0, 0), (0, 0)))
        return t.reshape(b, n_blk, Q_BLOCK, MLA_HEADS, t.shape[-1]).transpose(1, 0, 2, 3, 4)

    qn_blocks = to_blocks(q_nope)
    qr_blocks = to_blocks(q_rope)
    starts = jnp.arange(n_blk, dtype=jnp.int32) * Q_BLOCK
    k_pos = jnp.arange(length, dtype=jnp.int32)

    def attend_block(args):
        qn_b, qr_b, start = args
        s = (jnp.einsum('bqhd,bkhd->bhqk', qn_b, k_nope)
             + jnp.einsum('bqhd,bkd->bhqk', qr_b, k_rope)).astype(jnp.float32) * scale
        q_pos = start + jnp.arange(Q_BLOCK, dtype=jnp.int32)
        causal = k_pos[None, :] <= q_pos[:, None]
        s = jnp.where(causal, s, -jnp.inf)
        p = jax.nn.softmax(s, axis=-1).astype(v.dtype)
        return jnp.einsum('bhqk,bkhd->bqhd', p, v)

    o = lax.map(attend_block, (qn_blocks, qr_blocks, starts))
    o = o.transpose(1, 0, 2, 3, 4).reshape(b, l_pad, MLA_HEADS * MLA_V)
    return o[:, :length]


def short_conv(x, w):
    ch = x.shape[-1]
    return lax.conv_general_dilated(
        x, w[:, None, :].astype(x.dtype), window_strides=(1,),
        padding=[(CONV_K - 1, 0)], dimension_numbers=('NWC', 'WIO', 'NWC'),
        feature_group_count=ch)


def l2norm(x):
    return x * lax.rsqrt(jnp.sum(x * x, axis=-1, keepdims=True) + EPS)


def gated_deltanet(q_raw, k_raw, v_raw, b_raw, a_raw, z_raw, conv_w, a_log, dt_bias, gdn_norm):
    b, length, _ = q_raw.shape
    dtype = q_raw.dtype
    qkv = jax.nn.silu(short_conv(jnp.concatenate([q_raw, k_raw, v_raw], axis=-1), conv_w))
    nq = GDN_HEADS * GDN_DK
    q = qkv[..., :nq].reshape(b, length, GDN_HEADS, GDN_DK).astype(jnp.float32)
    k = qkv[..., nq:2 * nq].reshape(b, length, GDN_HEADS, GDN_DK).astype(jnp.float32)
    v = qkv[..., 2 * nq:].reshape(b, length, GDN_HEADS, GDN_DV).astype(jnp.float32)
    q = l2norm(q) * (GDN_DK ** -0.5)
    k = l2norm(k)
    beta = jax.nn.sigmoid(b_raw.astype(jnp.float32))
    g = -jnp.exp(a_log.astype(jnp.float32)) * jax.nn.softplus(
        a_raw.astype(jnp.float32) + dt_bias.astype(jnp.float32))

    pad_front = (-N_META) % CHUNK
    lc = pad_front + length
    n_chunk = lc // CHUNK

    def to_chunks(t):
        t = jnp.pad(t, ((0, 0), (pad_front, 0)) + ((0, 0),) * (t.ndim - 2))
        t = t.reshape((b, n_chunk, CHUNK) + t.shape[2:])
        return jnp.moveaxis(t, 3, 1)

    qc, kc, vc = to_chunks(q), to_chunks(k), to_chunks(v)
    bc = to_chunks(beta)
    gc = jnp.cumsum(to_chunks(g), axis=-1)
    tri_incl = jnp.tril(jnp.ones((CHUNK, CHUNK), dtype=bool))
    tri_strict = jnp.tril(jnp.ones((CHUNK, CHUNK), dtype=bool), -1)
    diff = gc[..., :, None] - gc[..., None, :]
    decay = jnp.exp(jnp.where(tri_incl, diff, -jnp.inf))
    kb = kc * bc[..., None]
    lmat = jnp.where(tri_strict, jnp.einsum('bhncd,bhnsd->bhncs', kb, kc) * decay, 0.0)
    eye = jnp.eye(CHUNK, dtype=jnp.float32)
    tmat = lax.linalg.triangular_solve(eye + lmat, jnp.broadcast_to(eye, lmat.shape),
                                       left_side=True, lower=True)
    u_c = tmat @ (vc * bc[..., None])
    w_c = tmat @ (kb * jnp.exp(gc)[..., None])
    intra = jnp.einsum('bhncd,bhnsd->bhncs', qc, kc) * decay

    def step(state, inp):
        q_i, k_i, u_i, w_i, g_i, a_i = inp
        v_new = u_i - w_i @ state
        o_i = (q_i * jnp.exp(g_i)[..., None]) @ state + a_i @ v_new
        g_last = g_i[..., -1]
        k_dec = k_i * jnp.exp(g_last[..., None] - g_i)[..., None]
        state = state * jnp.exp(g_last)[..., None, None] + jnp.einsum('bhcd,bhce->bhde', k_dec, v_new)
        return state, o_i

    xs = tuple(jnp.moveaxis(t, 2, 0) for t in (qc, kc, u_c, w_c, gc, intra))
    s0 = jnp.zeros((b, GDN_HEADS, GDN_DK, GDN_DV), jnp.float32)
    _, o = lax.scan(step, s0, xs)
    o = o.transpose(1, 0, 3, 2, 4).reshape(b, lc, GDN_HEADS, GDN_DV)[:, pad_front:]
    z = z_raw.reshape(b, length, GDN_HEADS, GDN_DV).astype(jnp.float32)
    o = rmsnorm(o, gdn_norm) * jax.nn.silu(z)
    return o.reshape(b, length, GDN_HEADS * GDN_DV).astype(dtype)


def _w(k, shape, fan_in):
    return jax.random.normal(k, shape, jnp.float32) * (fan_in ** -0.5)


def _gain(k, shape):
    return 1.0 + 0.02 * jax.random.normal(k, shape, jnp.float32)


def setup_inputs(seed: int = 0) -> dict:
    key = jax.random.key(seed)
    ks = jax.random.split(key, 32)
    d = D_MODEL
    dt = jnp.exp(jax.random.uniform(ks[15], (DEPTH, GDN_HEADS), jnp.float32,
                                    minval=math.log(1e-3), maxval=math.log(1e-1)))
    return {
        'x': jax.random.normal(ks[0], (BATCH, SEQ, d), jnp.float32),
        'meta_tokens': jax.random.normal(ks[1], (N_META, d), jnp.float32),
        'ffn1_norm': _gain(ks[2], (DEPTH, d)),
        'ffn1_w_gate': _w(ks[3], (DEPTH, d, D_FF), d),
        'ffn1_w_up': _w(ks[4], (DEPTH, d, D_FF), d),
        'ffn1_w_down': _w(ks[5], (DEPTH, D_FF, d), D_FF),
        'mix_norm': _gain(ks[6], (DEPTH, d)),
        'w_in': _w(ks[7], (DEPTH, d, D_IN), d),
        'q_norm': _gain(ks[8], (DEPTH, MLA_Q_RANK)),
        'w_uq': _w(ks[9], (DEPTH, MLA_Q_RANK, MLA_HEADS * (MLA_NOPE + MLA_ROPE)), MLA_Q_RANK),
        'kv_norm': _gain(ks[10], (DEPTH, MLA_KV_RANK)),
        'w_ukv': _w(ks[11], (DEPTH, MLA_KV_RANK, MLA_HEADS * (MLA_NOPE + MLA_V)), MLA_KV_RANK),
        'w_mla_o': _w(ks[12], (DEPTH, MLA_HEADS * MLA_V, d), MLA_HEADS * MLA_V),
        'conv_w': _w(ks[13], (DEPTH, CONV_K, GDN_CONV_CH), CONV_K),
        'a_log': jnp.log(jax.random.uniform(ks[14], (DEPTH, GDN_HEADS), jnp.float32, minval=1.0, maxval=16.0)),
        'dt_bias': dt + jnp.log(-jnp.expm1(-dt)),
        'gdn_norm': _gain(ks[16], (DEPTH, GDN_DV)),
        'w_gdn_o': _w(ks[17], (DEPTH, GDN_HEADS * GDN_DV, d), GDN_HEADS * GDN_DV),
        'w_out': _w(ks[18], (DEPTH, d, d), d),
        'ffn2_norm': _gain(ks[19], (DEPTH, d)),
        'ffn2_w_gate': _w(ks[20], (DEPTH, d, D_FF), d),
        'ffn2_w_up': _w(ks[21], (DEPTH, d, D_FF), d),
        'ffn2_w_down': _w(ks[22], (DEPTH, D_FF, d), D_FF),
        'final_norm': _gain(ks[23], (d,)),
    }


def reference(x, meta_tokens, ffn1_norm, ffn1_w_gate, ffn1_w_up, ffn1_w_down, mix_norm, w_in,
              q_norm, w_uq, kv_norm, w_ukv, w_mla_o, conv_w, a_log, dt_bias, gdn_norm, w_gdn_o,
              w_out, ffn2_norm, ffn2_w_gate, ffn2_w_up, ffn2_w_down, final_norm):
    b = x.shape[0]
    meta = jnp.broadcast_to(meta_tokens[None].astype(x.dtype), (b, N_META, D_MODEL))
    h = jnp.concatenate([meta, x], axis=1)
    split_points = [int(p) for p in np.cumsum(SPLITS)[:-1]]
    for l in range(DEPTH):
        h = h + 0.5 * swiglu(rmsnorm(h, ffn1_norm[l]), ffn1_w_gate[l], ffn1_w_up[l], ffn1_w_down[l])
        u = rmsnorm(h, mix_norm[l])
        (c_q, c_kv, k_rope, g_q, g_k, g_v, g_b, g_a, g_z,
         gate_mla, gate_gdn) = jnp.split(u @ w_in[l], split_points, axis=-1)
        y_mla = mla(c_q, c_kv, k_rope, q_norm[l], w_uq[l], kv_norm[l], w_ukv[l]) @ w_mla_o[l]
        y_gdn = gated_deltanet(g_q, g_k, g_v, g_b, g_a, g_z, conv_w[l], a_log[l], dt_bias[l],
                               gdn_norm[l]) @ w_gdn_o[l]
        merged = jax.nn.sigmoid(gate_mla) * y_mla + jax.nn.sigmoid(gate_gdn) * y_gdn
        h = h + merged @ w_out[l]
        h = h + 0.5 * swiglu(rmsnorm(h, ffn2_norm[l]), ffn2_w_gate[l], ffn2_w_up[l], ffn2_w_down[l])
    return rmsnorm(h, final_norm)[:, N_META:]
```

```python
import numpy as np
from contextlib import ExitStack
import concourse.bass as bass
import concourse.mybir as mybir
from concourse.bass_utils import run_bass_kernel_spmd

F32 = mybir.dt.float32
BF16 = mybir.dt.bfloat16
AF = mybir.ActivationFunctionType
ALU = mybir.AluOpType
AX = mybir.AxisListType

ENGS = ("pe", "act", "dve", "pool", "sp")
SEM_LIMIT = 30000

D = 1024
DFF = 2816
NF = 22
SEQ = 4096
NMETA = 16
LREAL = SEQ + NMETA
LP = 4224
NB = 33
W = 384
NST = 11
EPS = 1e-6
DIN_EXT = 4560
NCORES = 4
GDN_F32 = True
GDN_F32R = False
F32R = mybir.dt.float32r


class Buf:
    __slots__ = ("name", "writers", "readers", "dsem", "dcount")

    def __init__(self, name):
        self.name = name
        self.writers = []
        self.readers = []
        self.dsem = None
        self.dcount = 0


class Op:
    __slots__ = ("eng", "fn", "deps", "is_dma", "sem", "count", "needed", "seq")


class TB:
    __slots__ = ("t", "b")

    def __init__(self, t, b):
        self.t = t
        self.b = b


class Sched:
    def __init__(self, nc, es, max_dma_sems=120):
        self.nc = nc
        self.es = es
        self.ops = {e: [] for e in ENGS}
        self.bufs = []
        self.dma_sems = []
        self.max_dma_sems = max_dma_sems
        self.eng_sems = {e: [] for e in ENGS}
        self.eng_cnt = {e: 0 for e in ENGS}
        self.last_op = {e: None for e in ENGS}
        self.dma_since = []
        self.waited = {e: {} for e in ENGS}
        self.nops = 0

    def buf(self, name=None):
        b = Buf(name or f"b{len(self.bufs)}")
        self.bufs.append(b)
        return b

    def _dsem(self, b):
        if b.dsem is None:
            assert len(self.dma_sems) < self.max_dma_sems, "too many dma sems"
            s = self.es.enter_context(self.nc.semaphore(f"d{len(self.dma_sems)}"))
            self.dma_sems.append(s)
            b.dsem = s
        return b.dsem

    def add(self, eng, fn, reads=(), writes=(), dma=False, own=None):
        op = Op()
        op.eng = eng
        op.fn = fn
        op.is_dma = dma
        op.seq = self.nops
        op.needed = False
        op.sem = None
        op.count = None
        deps = []
        seen = set()

        def push(d):
            if id(d) in seen:
                return
            seen.add(id(d))
            deps.append(d)

        for b in reads:
            for d in b.writers:
                push(d)
        owner = None
        if dma:
            assert len(writes) == 1
            owner = own if own is not None else writes[0]
            osem = self._dsem(owner)
        for b in writes:
            for d in b.writers:
                if dma and d.is_dma and d.sem is osem:
                    continue
                push(d)
            for d in b.readers:
                push(d)
        if eng == "pe" and not dma:
            deps = [d for d in deps if not (d.eng == "pe" and not d.is_dma)]
        latest = {}
        rest = []
        for d in deps:
            if d.is_dma:
                rest.append(d)
            elif d.eng not in latest or latest[d.eng].seq < d.seq:
                latest[d.eng] = d
        deps = rest + list(latest.values())
        op.deps = deps
        for d in deps:
            d.needed = True
        if dma:
            op.sem = osem
            owner.dcount += 16
            op.count = owner.dcount
            self.dma_since.append(op)
        for b in reads:
            b.readers.append(op)
        for b in writes:
            if dma and owner is not b:
                b.writers = [w for w in b.writers if w.is_dma] + [op]
            else:
                b.writers = [op]
            b.readers = []
        self.ops[eng].append(op)
        if fn is not None and not dma:
            self.last_op[eng] = op
        self.nops += 1
        return op

    def barrier(self):
        deps = [self.last_op[e] for e in ENGS if self.last_op[e] is not None] + list(self.dma_since)
        for e in ENGS:
            op = Op()
            op.eng = e
            op.fn = None
            op.seq = self.nops
            op.is_dma = False
            op.needed = False
            op.sem = None
            op.count = None
            op.deps = [d for d in deps if not (d.eng == e and not d.is_dma)]
            for d in op.deps:
                d.needed = True
            self.ops[e].append(op)
        for b in self.bufs:
            b.writers = []
            b.readers = []
        self.dma_since = []

    def emit(self):
        nc = self.nc
        for e in ENGS:
            for op in self.ops[e]:
                if op.is_dma or op.fn is None:
                    continue
                if op.needed:
                    self.eng_cnt[e] += 1
                    c = self.eng_cnt[e]
                    k = (c - 1) // SEM_LIMIT
                    while len(self.eng_sems[e]) <= k:
                        self.eng_sems[e].append(
                            self.es.enter_context(nc.semaphore(f"e_{e}_{len(self.eng_sems[e])}")))
                    op.sem = self.eng_sems[e][k]
                    op.count = c - k * SEM_LIMIT

        def run(e, engine):
            waited = self.waited[e]
            for op in self.ops[e]:
                need = {}
                for d in op.deps:
                    key = id(d.sem)
                    if waited.get(key, 0) >= d.count:
                        continue
                    if key not in need or need[key][1] < d.count:
                        need[key] = (d.sem, d.count)
                for key, (s, c) in need.items():
                    engine.wait_ge(s, c)
                    waited[key] = c
                if op.fn is None:
                    continue
                ins = op.fn(engine)
                if op.is_dma:
                    ins.then_inc(op.sem, 16)
                elif op.needed:
                    ins.then_inc(op.sem, 1)
            self.ops[e] = []

        with nc.Block() as block:
            @block.tensor
            def _(eng):
                run("pe", eng)

            @block.scalar
            def _(eng):
                run("act", eng)

            @block.vector
            def _(eng):
                run("dve", eng)

            @block.gpsimd
            def _(eng):
                run("pool", eng)

            @block.sync
            def _(eng):
                run("sp", eng)


FFN_COLGROUPS = [(0, 512), (512, 512), (1024, 512), (1536, 512), (2048, 512), (2560, 256)]
FFN_ROWGROUPS = [(0, 8), (8, 8), (16, 6)]


def build(dbg=False, stages=(1, 2, 3, 4), nb_lim=None, nst_lim=None, lvl=9):
    nc = bass.Bass("TRN2", target_bir_lowering=False)

    def din(name, shape, dt=F32):
        return nc.dram_tensor(name, shape, dt, kind="ExternalInput").ap()

    def dscr(name, shape, dt):
        return nc.dram_tensor(name, shape, dt, kind=("ExternalOutput" if dbg else "Internal")).ap()

    xin = din("xin", [LP, D])
    wg_f = [din("wg1", [D, DFF]), din("wg2", [D, DFF])]
    wu_f = [din("wu1", [D, DFF]), din("wu2", [D, DFF])]
    wd_f = [din("wd1", [DFF, D]), din("wd2", [DFF, D])]
    win_f = din("win", [D, DIN_EXT])
    wuq_f = din("wuq", [256, 768])
    wuqs_f = din("wuqs", [256, 768])
    wuk_f = din("wuk", [128, 512])
    wuv_f = din("wuv", [128, 512])
    wmo_f = din("wmo", [512, D])
    wgo_f = din("wgo", [512, D])
    wout_f = din("wout", [D, D])
    nrm_d = [din("nrm1", [128, 8]), din("nrmm", [128, 8]), din("nrm2", [128, 8])]
    qn_d = din("qn", [128, 2])
    kvn_d = din("kvn", [128, 1])
    cw_d = din("cw", [128, 48])
    alog_d = din("alog", [128, 8])
    dtb_d = din("dtb", [128, 8])
    gnorm_d = din("gnorm", [128, 64])
    fnorm_d = din("fnorm", [128, D])
    cosT_d = din("cosT", [32, LP])
    sinT_d = din("sinT", [32, LP])
    out_d = nc.dram_tensor("out", [LP, D], F32, kind="ExternalOutput").ap()

    wg_b = [nc.dram_tensor(f"wgb{i}", [6, 128, 8, 512], BF16, kind="Internal").ap() for i in range(2)]
    wu_b = [nc.dram_tensor(f"wub{i}", [6, 128, 8, 512], BF16, kind="Internal").ap() for i in range(2)]
    wd_b = [nc.dram_tensor(f"wdb{i}", [6, 128, 8, 512], BF16, kind="Internal").ap() for i in range(2)]
    win_b = nc.dram_tensor("winb", [9, 128, 8, 512], BF16, kind="Internal").ap()

    h1s = dscr("h1s", [LP, D], F32)
    cqnT = dscr("cqnT", [256, LP], BF16)
    ckvnT = dscr("ckvnT", [128, LP], BF16)
    kropeT = dscr("kropeT", [32, LP], BF16)
    gqT = dscr("gqT", [512, LP], BF16)
    gkT = dscr("gkT", [512, LP], BF16)
    gktok = dscr("gktok", [LP, 512], BF16)
    gvtok = dscr("gvtok", [LP, 512], BF16)
    gbs = dscr("gbs", [LP, 8], F32)
    ggs = dscr("ggs", [LP, 8], F32)
    zs = dscr("zs", [LP, 512], BF16)
    gates = dscr("gates", [LP, 2048], BF16)
    omla = dscr("omla", [LP, 512], BF16)
    ogdn = dscr("ogdn", [LP, 512], BF16)

    with ExitStack() as es:
        S = Sched(nc, es)

        def sbt(st, name, shape, dt):
            return st.enter_context(nc.sbuf_tensor(name, shape, dt))

        def tb(st, name, shape, dt):
            return TB(sbt(st, name, shape, dt), S.buf(name))

        def ring(st, name, shape, dt, n):
            tiles = [tb(st, f"{name}{i}", shape, dt) for i in range(n)]
            state = {"i": 0}

            def nxt():
                r = tiles[state["i"] % n]
                state["i"] += 1
                return r
            return nxt

        def A(eng, fn, r=(), w=()):
            return S.add(eng, fn, reads=list(r), writes=list(w))

        def DMA(eng, out, in_, r, w, own=None):
            return S.add(eng, lambda e: e.dma_start(out=out, in_=in_), reads=list(r), writes=[w], dma=True, own=own)

        banks = [TB(es.enter_context(nc.psum_tensor(f"ps{i}", [128, 512], F32)), S.buf(f"ps{i}"))
                 for i in range(8)]
        pstate = {"i": 0}

        pcfg = {"n": 8}

        def psum():
            r = banks[pstate["i"] % pcfg["n"]]
            pstate["i"] += 1
            return r

        accst = {"i": 0}

        def psum_acc():
            r = banks[6 + accst["i"] % 2]
            accst["i"] += 1
            return r

        def mm(ps, out, lhsT, rhs, start, stop, r, skip=False):
            if skip:
                return A("pe", lambda e: e.matmul(out=out, lhsT=lhsT, rhs=rhs, start=start, stop=stop,
                                                  skip_group_check=True), r=r, w=[ps.b])
            return A("pe", lambda e: e.matmul(out=out, lhsT=lhsT, rhs=rhs, start=start, stop=stop),
                     r=r, w=[ps.b])

        def tr(ps, out, in_, ident, r):
            return A("pe", lambda e: e.transpose(out=out, in_=in_, identity=ident), r=r, w=[ps.b])

        cst = S.buf("consts")
        identf = sbt(es, "identf", [128, 128], F32)
        identb = sbt(es, "identb", [128, 128], BF16)
        onesf = sbt(es, "onesf", [128, 128], F32)
        onesb = sbt(es, "onesb", [128, 128], BF16)
        onesblk = sbt(es, "onesblk", [128, 128], BF16)
        Uf = sbt(es, "Uf", [128, 128], F32)
        maskI = sbt(es, "maskI", [128, 128], F32)
        maskS = sbt(es, "maskS", [128, 128], F32)
        causb = sbt(es, "causb", [128, 128], BF16)
        A("pool", lambda e: e.memset(identf[:], 0.0), w=[cst])
        A("pool", lambda e: e.affine_select(out=identf[:], in_=identf[:], pattern=[[-1, 128]],
                                            compare_op=ALU.not_equal, fill=1.0, base=0,
                                            channel_multiplier=1), r=[cst], w=[cst])
        A("pool", lambda e: e.tensor_copy(out=identb[:], in_=identf[:]), r=[cst], w=[cst])
        A("pool", lambda e: e.memset(onesf[:], 1.0), w=[cst])
        A("pool", lambda e: e.memset(onesb[:], 1.0), w=[cst])
        A("pool", lambda e: e.memset(onesblk[:], 0.0), w=[cst])
        A("pool", lambda e: e.memset(onesblk[0:64, 0:64], 1.0), r=[cst], w=[cst])
        A("pool", lambda e: e.memset(onesblk[64:128, 64:128], 1.0), r=[cst], w=[cst])
        A("pool", lambda e: e.affine_select(out=Uf[:], in_=onesf[:], pattern=[[1, 128]],
                                            compare_op=ALU.is_ge, fill=0.0, base=0,
                                            channel_multiplier=-1), r=[cst], w=[cst])
        A("pool", lambda e: e.tensor_copy(out=maskI[:], in_=Uf[:]), r=[cst], w=[cst])
        A("pool", lambda e: e.affine_select(out=maskS[:], in_=onesf[:], pattern=[[1, 128]],
                                            compare_op=ALU.is_gt, fill=0.0, base=0,
                                            channel_multiplier=-1), r=[cst], w=[cst])
        A("pool", lambda e: e.tensor_copy(out=causb[:], in_=Uf[:]), r=[cst], w=[cst])
        zerob = sbt(es, "zerob", [128, 256], BF16)
        A("pool", lambda e: e.memset(zerob[:], 0.0), w=[cst])
        negmask = sbt(es, "negmask", [128, 128], BF16)
        A("pool", lambda e: e.tensor_scalar(out=negmask[:], in0=Uf[:], scalar1=-1.0, scalar2=30000.0,
                                            op0=ALU.add, op1=ALU.mult), r=[cst], w=[cst])

        nrm_t = [sbt(es, f"nrmt{i}", [128, 8], F32) for i in range(3)]
        qn_t = sbt(es, "qn_t", [128, 2], F32)
        kvn_t = sbt(es, "kvn_t", [128, 1], F32)
        cw_t = sbt(es, "cw_t", [128, 48], F32)
        alog_t = sbt(es, "alog_t", [128, 8], F32)
        negA_t = sbt(es, "negA_t", [128, 8], F32)
        dtb_t = sbt(es, "dtb_t", [128, 8], F32)
        gnorm_t = sbt(es, "gnorm_t", [128, 64], F32)
        prm = S.buf("prm")
        for t_, d_ in [(nrm_t[0], nrm_d[0]), (nrm_t[1], nrm_d[1]), (nrm_t[2], nrm_d[2]), (qn_t, qn_d),
                       (kvn_t, kvn_d), (cw_t, cw_d), (alog_t, alog_d), (dtb_t, dtb_d), (gnorm_t, gnorm_d)]:
            DMA("sp", t_[:], d_, [], prm)
        A("act", lambda e: e.activation(out=negA_t[:], in_=alog_t[:], func=AF.Exp), r=[prm], w=[cst])
        A("dve", lambda e: e.tensor_scalar(out=negA_t[:], in0=negA_t[:], scalar1=-1.0, scalar2=None,
                                           op0=ALU.mult), r=[cst], w=[cst])

        wgc = [[S.buf(f"wgc{i}_{g}") for g in range(6)] for i in range(2)]
        wuc = [[S.buf(f"wuc{i}_{g}") for g in range(6)] for i in range(2)]
        wdc = [S.buf(f"wdc{i}") for i in range(2)]
        winc = S.buf("winc")

        def tiled_cast(dst_g, dcol, src, r0, nchunk, c0, ncol, b, late=False):
            for ca in range(0, nchunk, 4):
                cb = min(nchunk, ca + 4)
                d_ = dst_g[:, ca:cb, dcol:dcol + ncol]
                s__ = src[(r0 + ca) * 128:(r0 + cb) * 128, c0:c0 + ncol].rearrange("(c p) n -> p c n", p=128)
                if late:
                    late_casts.append((d_, s__, b))
                else:
                    DMA("pool", d_, s__, [], b)

        late_casts = []
        WIN_GROUPS = [[(0, 416, 0), (4528, 32, 416), (1952, 16, 448)]] + \
            [[(416 + g_ * 512, 512, 0)] for g_ in range(3)] + \
            [[(c_, 512, 0)] for c_ in [1968, 2480, 2992, 3504, 4016]]

        def cast_ffn(i, late=False):
            for g, (c0, ncol) in enumerate(FFN_COLGROUPS):
                tiled_cast(wg_b[i][g], 0, wg_f[i], 0, 8, c0, ncol, wgc[i][g], late)
                tiled_cast(wu_b[i][g], 0, wu_f[i], 0, 8, c0, ncol, wuc[i][g], late)
            for ri, (r0, nr) in enumerate(FFN_ROWGROUPS):
                for half in range(2):
                    tiled_cast(wd_b[i][ri * 2 + half], 0, wd_f[i], r0, nr, half * 512, 512, wdc[i], late)

        cast_ffn(0)
        for gi_, parts in enumerate(WIN_GROUPS):
            for (c0, ncol, dcol) in parts:
                tiled_cast(win_b[gi_], dcol, win_f, 0, 8, c0, ncol, winc)
        cast_ffn(1, late=True)

        def issue_late_casts(k):
            for _ in range(k):
                if late_casts:
                    d_, s__, b_ = late_casts.pop(0)
                    DMA("pool", d_, s__, [], b_)

        def norm_chain(st_r, xt, nblk=3):
            xns = []
            for t in range(nblk):
                junk = st_r["junk"]()
                ss = st_r["ss"]()
                A("dve", lambda e, ss=ss: e.memset(ss.t[:], 0.0), w=[ss.b])
                A("act", lambda e, t=t, junk=junk, ss=ss: e.activation(
                    out=junk.t[:], in_=xt.t[:, t, :], func=AF.Square, accum_out=ss.t[:, 0:1]),
                  r=[xt.b, ss.b], w=[junk.b, ss.b])
                A("act", lambda e, ss=ss: e.activation(out=ss.t[:, 1:2], in_=ss.t[:, 0:1], func=AF.Sqrt,
                                                       scale=1.0 / D, bias=EPS), r=[ss.b], w=[ss.b])
                A("dve", lambda e, ss=ss: e.reciprocal(out=ss.t[:, 2:3], in_=ss.t[:, 1:2]), r=[ss.b], w=[ss.b])
                xn = st_r["xn"]()
                A("dve", lambda e, t=t, xn=xn, ss=ss: e.tensor_scalar(
                    out=xn.t[:], in0=xt.t[:, t, :], scalar1=ss.t[:, 2:3], scalar2=None, op0=ALU.mult),
                  r=[xt.b, ss.b], w=[xn.b])
                xns.append(xn)
            return xns

        def norm_tr(st_r, xns, nrm):
            xnT = st_r["xnT"]()
            for t, xn in enumerate(xns):
                ps = psum()
                pv = ps.t[:].bitcast(BF16)
                for c in range(8):
                    tr(ps, pv[:, c * 128:(c + 1) * 128], xn.t[:, c * 128:(c + 1) * 128], identb[:], [xn.b, cst])
                A("dve", lambda e, t=t, pv=pv, xnT=xnT: e.tensor_tensor(
                    out=xnT.t[:, :, t * 128:(t + 1) * 128],
                    in0=pv.rearrange("p (c n) -> p c n", c=8),
                    in1=nrm[:, :].unsqueeze(2).broadcast_to([128, 8, 128]), op=ALU.mult),
                  r=[prm], w=[ps.b, xnT.b])
            return xnT

        def norm_T(st_r, xt, nrm, nblk=3):
            xnT = norm_tr(st_r, norm_chain(st_r, xt, nblk), nrm)
            return xnT, [xnT.b] * nblk

        def ffn_gu(st_r, xnT, i, nblk=3):
            ncol = nblk * 128
            xb = [xnT.b]
            hid = st_r["hid"]()
            hb = st_r["hb"]
            for g, (c0, ncg) in enumerate(FFN_COLGROUPS):
                sg = st_r["wslot"]()
                su = st_r["wslot"]()
                DMA("sp", sg.t[:, :, 0:ncg], wg_b[i][g, :, :, 0:ncg], [wgc[i][g]], sg.b)
                DMA("sp", su.t[:, :, 0:ncg], wu_b[i][g, :, :, 0:ncg], [wuc[i][g]], su.b)
                for f in range(ncg // 128):
                    fi = c0 // 128 + f
                    pg = psum()
                    for k in range(8):
                        mm(pg, pg.t[:, 0:ncol], sg.t[:, k, f * 128:(f + 1) * 128], xnT.t[:, k, 0:ncol],
                           k == 0, k == 7, [sg.b] + xb)
                    pu = psum()
                    for k in range(8):
                        mm(pu, pu.t[:, 0:ncol], su.t[:, k, f * 128:(f + 1) * 128], xnT.t[:, k, 0:ncol],
                           k == 0, k == 7, [su.b] + xb)
                    sil = st_r["sil"]()
                    A("act", lambda e, pg=pg, sil=sil: e.activation(out=sil.t[:, 0:ncol], in_=pg.t[:, 0:ncol],
                                                                     func=AF.Silu), w=[pg.b, sil.b])
                    A("dve", lambda e, pu=pu, sil=sil, fi=fi: e.tensor_tensor(
                        out=hid.t[:, fi, 0:ncol], in0=sil.t[:, 0:ncol], in1=pu.t[:, 0:ncol], op=ALU.mult),
                      r=[sil.b], w=[pu.b, hb[fi]])
            return hid

        def ffn_down(st_r, hid, xt, i, nblk=3):
            hb = st_r["hb"]
            accs = [[psum() for _ in range(2)] for _ in range(nblk)]
            for ri, (r0, nr) in enumerate(FFN_ROWGROUPS):
                for half in range(2):
                    sd = st_r["wslot"]()
                    DMA("sp", sd.t[:, 0:nr, :], wd_b[i][ri * 2 + half, :, 0:nr, :], [wdc[i]], sd.b)
                    for t in range(nblk):
                        for f in range(nr):
                            fi = r0 + f
                            mm(accs[t][half], accs[t][half].t[:, :], hid.t[:, fi, t * 128:(t + 1) * 128],
                               sd.t[:, f, :], fi == 0, fi == NF - 1, [sd.b, hb[fi]])
            for t in range(nblk):
                for half in range(2):
                    ps = accs[t][half]
                    A("dve", lambda e, ps=ps, t=t, half=half: e.scalar_tensor_tensor(
                        out=xt.t[:, t, half * 512:(half + 1) * 512], in0=ps.t[:, :], scalar=0.5,
                        in1=xt.t[:, t, half * 512:(half + 1) * 512], op0=ALU.mult, op1=ALU.add),
                      r=[], w=[ps.b, xt.b])

        if 1 in stages:
            with ExitStack() as st:
                R = {
                    "xnT": ring(st, "xnT", [128, 8, W], BF16, 3),
                    "junk": ring(st, "junk", [128, D], BF16, 2),
                    "ss": ring(st, "ss", [128, 4], F32, 4),
                    "xn": ring(st, "xn", [128, D], BF16, 6),
                    "hid": ring(st, "hid", [128, NF, W], BF16, 1),
                    "wslot": ring(st, "wslot", [128, 8, 512], BF16, 5),
                    "sil": ring(st, "sil", [128, W], F32, 2),
                    "hb": [S.buf(f"hid{f}") for f in range(NF)],
                }
                xtr = ring(st, "xt", [128, 3, D], F32, 2)
                cqf = tb(st, "cqf", [128, 2, W], F32)
                ckvf = tb(st, "ckvf", [128, 1, W], F32)
                sqb = ring(st, "sqb", [128, W], BF16, 6)
                rr = ring(st, "rr", [128, W], F32, 2)
                ob16 = ring(st, "ob16", [128, W], BF16, 8)
                tabc = ring(st, "tabc", [32, W], F32, 2)
                tabs = ring(st, "tabs", [32, W], F32, 2)
                kr1 = ring(st, "kr1", [32, W], F32, 2)
                kr2 = ring(st, "kr2", [32, W], F32, 2)
                smallr = ring(st, "small", [128, 32], F32, 4)
                pre = [tb(st, f"pre{c}", [128, W + 3], BF16) for c in range(12)]
                dcw = sbt(st, "dcw", [128, 48, 128], BF16)
                dcwb = [S.buf("dcw0"), S.buf("dcw1")]
                for c_ in range(48):
                    A("dve", lambda e, c_=c_: e.tensor_scalar(
                        out=dcw[:, c_, :], in0=identf[:], scalar1=cw_t[:, c_:c_ + 1], scalar2=None,
                        op0=ALU.mult), r=[prm, cst], w=[dcwb[c_ % 2]])
                yr = ring(st, "yy", [128, W], F32, 5)
                ktst = tb(st, "ktst", [128, 3, 512], BF16)
                vtst = tb(st, "vtst", [128, 3, 512], BF16)
                zst = ring(st, "zst", [128, 512], BF16, 2)
                gst = ring(st, "gst", [128, 2048], BF16, 3)
                for c in range(12):
                    A("pool", lambda e, c=c: e.memset(pre[c].t[:, 0:3], 0.0), w=[pre[c].b])

                h1b = S.buf("h1s")
                sc = {k: S.buf(k) for k in ["cqnT", "ckvnT", "kropeT", "gqT", "gkT", "gktok", "gvtok",
                                            "gbs", "ggs", "zs", "gates"]}

                def inproj(s_, uT):
                    t0 = s_ * W
                    ub = [uT.b] * 3
                    sA = R["wslot"]()
                    DMA("sp", sA.t[:, :, 0:464], win_b[0, :, :, 0:464], [winc], sA.b)

                    def lowrank(cols, nch, nrmw, dst, dstb, inv_n, cqf):
                        sq = []
                        for c in range(nch):
                            ps = psum()
                            for k in range(8):
                                mm(ps, ps.t[:, 0:W], sA.t[:, k, cols + c * 128:cols + (c + 1) * 128],
                                   uT.t[:, k, :], k == 0, k == 7, [sA.b] + ub)
                            q = sqb()
                            A("act", lambda e, ps=ps, q=q: e.activation(out=q.t[:], in_=ps.t[:, 0:W], func=AF.Square),
                              w=[ps.b, q.b])
                            A("dve", lambda e, ps=ps, c=c: e.tensor_copy(out=cqf.t[:, c, :], in_=ps.t[:, 0:W]),
                              w=[ps.b, cqf.b])
                            sq.append(q)
                        yield
                        ps2 = psum()
                        for c in range(nch):
                            mm(ps2, ps2.t[:, 0:W], onesb[:], sq[c].t[:], c == 0, c == nch - 1, [sq[c].b, cst])
                        r_ = rr()
                        A("act", lambda e, ps2=ps2, r_=r_: e.activation(out=r_.t[:], in_=ps2.t[:, 0:W], func=AF.Sqrt,
                                                                        scale=inv_n, bias=EPS), w=[ps2.b, r_.b])
                        A("dve", lambda e, r_=r_: e.reciprocal(out=r_.t[:], in_=r_.t[:]), r=[r_.b], w=[r_.b])
                        for c in range(nch):
                            o = ob16()
                            A("dve", lambda e, c=c, o=o, r_=r_: e.scalar_tensor_tensor(
                                out=o.t[:], in0=cqf.t[:, c, :], scalar=nrmw[:, c:c + 1], in1=r_.t[:],
                                op0=ALU.mult, op1=ALU.mult), r=[cqf.b, r_.b, prm], w=[o.b])
                            DMA("pool", dst[c * 128:(c + 1) * 128, t0:t0 + W], o.t[:], [o.b], dstb, own=o.b)

                    lr1 = lowrank(0, 2, qn_t, cqnT, sc["cqnT"], 1.0 / 256, cqf)
                    lr2 = lowrank(256, 1, kvn_t, ckvnT, sc["ckvnT"], 1.0 / 128, ckvf)
                    next(lr1)
                    next(lr2)

                    tc_ = tabc()
                    ts_ = tabs()
                    DMA("sp", tc_.t[:], cosT_d[:, t0:t0 + W], [], tc_.b)
                    DMA("sp", ts_.t[:], sinT_d[:, t0:t0 + W], [], ts_.b)
                    psa = psum()
                    for k in range(8):
                        mm(psa, psa.t[0:32, 0:W], sA.t[:, k, 384:416], uT.t[:, k, :], k == 0, k == 7, [sA.b] + ub)
                    psb = psum()
                    for k in range(8):
                        mm(psb, psb.t[0:32, 0:W], sA.t[:, k, 416:448], uT.t[:, k, :], k == 0, k == 7, [sA.b] + ub)
                    k1 = kr1()
                    k2 = kr2()
                    A("dve", lambda e, psa=psa, k1=k1, tc_=tc_: e.tensor_tensor(
                        out=k1.t[:], in0=psa.t[0:32, 0:W], in1=tc_.t[:], op=ALU.mult), r=[tc_.b], w=[psa.b, k1.b])
                    A("dve", lambda e, psb=psb, k2=k2, ts_=ts_: e.tensor_tensor(
                        out=k2.t[:], in0=psb.t[0:32, 0:W], in1=ts_.t[:], op=ALU.mult), r=[ts_.b], w=[psb.b, k2.b])
                    o = ob16()
                    A("dve", lambda e, k1=k1, k2=k2, o=o: e.tensor_tensor(
                        out=o.t[0:32, :], in0=k1.t[:], in1=k2.t[:], op=ALU.add), r=[k1.b, k2.b], w=[o.b])
                    DMA("pool", kropeT[:, t0:t0 + W], o.t[0:32, :], [o.b], sc["kropeT"], own=o.b)

                    for t in range(3):
                        ps = psum()
                        for k in range(8):
                            mm(ps, ps.t[:, 0:16], uT.t[:, k, t * 128:(t + 1) * 128], sA.t[:, k, 448:464],
                               k == 0, k == 7, [sA.b, ub[t]])
                        sm = smallr()
                        A("act", lambda e, ps=ps, sm=sm: e.activation(out=sm.t[:, 0:8], in_=ps.t[:, 0:8],
                                                                      func=AF.Sigmoid), w=[ps.b, sm.b])
                        A("dve", lambda e, ps=ps, sm=sm: e.tensor_tensor(out=sm.t[:, 8:16], in0=ps.t[:, 8:16],
                                                                         in1=dtb_t[:], op=ALU.add),
                          r=[prm, sm.b], w=[ps.b, sm.b])
                        A("act", lambda e, sm=sm: e.activation(out=sm.t[:, 8:16], in_=sm.t[:, 8:16], func=AF.Exp),
                          r=[sm.b], w=[sm.b])
                        A("act", lambda e, sm=sm: e.activation(out=sm.t[:, 8:16], in_=sm.t[:, 8:16], func=AF.Ln,
                                                               bias=1.0), r=[sm.b], w=[sm.b])
                        A("dve", lambda e, sm=sm: e.tensor_tensor(out=sm.t[:, 16:24], in0=sm.t[:, 8:16],
                                                                  in1=negA_t[:], op=ALU.mult),
                          r=[sm.b, cst], w=[sm.b])
                        DMA("pool", gbs[t0 + t * 128:t0 + (t + 1) * 128, :], sm.t[:, 0:8], [sm.b], sc["gbs"], own=sm.b)
                        DMA("pool", ggs[t0 + t * 128:t0 + (t + 1) * 128, :], sm.t[:, 16:24], [sm.b], sc["ggs"], own=sm.b)

                    for lr_ in (lr1, lr2):
                        try:
                            next(lr_)
                        except StopIteration:
                            pass
                    def qkv_chunk(grp, f, sl):
                        ci = grp * 4 + f
                        ps = psum()
                        for k in range(8):
                            mm(ps, ps.t[:, 0:W], sl.t[:, k, f * 128:(f + 1) * 128], uT.t[:, k, :],
                               k == 0, k == 7, [sl.b] + ub)
                        pc = pre[ci]
                        A("act", lambda e, ps=ps, pc=pc: e.activation(out=pc.t[:, 3:W + 3], in_=ps.t[:, 0:W],
                                                                      func=AF.Copy), w=[ps.b, pc.b])
                        yield
                        psc = psum()
                        for j in range(4):
                            mm(psc, psc.t[:, 0:W], dcw[:, ci * 4 + j, :], pc.t[:, j:j + W], j == 0, j == 3, [pc.b] + dcwb)
                        A("pool", lambda e, pc=pc: e.tensor_copy(out=pc.t[:, 0:3], in_=pc.t[:, W:W + 3]),
                          r=[pc.b], w=[pc.b])
                        y = yr()
                        A("act", lambda e, psc=psc, y=y: e.activation(out=y.t[:], in_=psc.t[:, 0:W], func=AF.Silu),
                          w=[psc.b, y.b])
                        if grp < 2:
                            q = sqb()
                            A("act", lambda e, y=y, q=q: e.activation(out=q.t[:], in_=y.t[:], func=AF.Square),
                              r=[y.b], w=[q.b])
                        yield
                        o = ob16()
                        if grp < 2:
                            ps2 = psum()
                            mm(ps2, ps2.t[:, 0:W], onesblk[:], q.t[:], True, True, [q.b, cst])
                            r_ = rr()
                            sc_, bi_ = (64.0, 64.0 * EPS) if grp == 0 else (1.0, EPS)
                            A("act", lambda e, ps2=ps2, r_=r_, sc_=sc_, bi_=bi_: e.activation(
                                out=r_.t[:], in_=ps2.t[:, 0:W], func=AF.Sqrt, scale=sc_, bias=bi_),
                              w=[ps2.b, r_.b])
                            A("dve", lambda e, r_=r_: e.reciprocal(out=r_.t[:], in_=r_.t[:]), r=[r_.b], w=[r_.b])
                            A("dve", lambda e, y=y, r_=r_, o=o: e.tensor_tensor(out=o.t[:], in0=y.t[:], in1=r_.t[:],
                                                                               op=ALU.mult),
                              r=[y.b, r_.b], w=[o.b])
                            dstT, dstb = (gqT, sc["gqT"]) if grp == 0 else (gkT, sc["gkT"])
                            DMA("pool", dstT[f * 128:(f + 1) * 128, t0:t0 + W], o.t[:], [o.b], dstb, own=o.b)
                        else:
                            A("pool", lambda e, y=y, o=o: e.tensor_copy(out=o.t[:], in_=y.t[:]), r=[y.b], w=[o.b])
                        yield
                        if grp >= 1:
                            stg = ktst if grp == 1 else vtst
                            ps3 = psum()
                            pv = ps3.t[:].bitcast(BF16)
                            for t in range(3):
                                tr(ps3, pv[:, t * 128:(t + 1) * 128], o.t[:, t * 128:(t + 1) * 128], identb[:],
                                   [o.b, cst])
                            A("act", lambda e, pv=pv, stg=stg, f=f: e.activation(
                                out=stg.t[:, :, f * 128:(f + 1) * 128],
                                in_=pv[:, 0:384].rearrange("p (t n) -> p t n", t=3), func=AF.Copy),
                              w=[ps3.b, stg.b])
                            if f == 3:
                                dd, db = (gktok, sc["gktok"]) if grp == 1 else (gvtok, sc["gvtok"])
                                DMA("pool", dd[t0:t0 + W, :].rearrange("(t p) d -> p t d", p=128), stg.t[:],
                                    [stg.b], db, own=stg.b)
                        yield

                    gens = []

                    def adv(g):
                        try:
                            next(g)
                        except StopIteration:
                            pass

                    def pump():
                        n_ = len(gens)
                        for back in (3, 5, 7):
                            if n_ >= back and n_ - back < 12 and gens[n_ - back] is not None:
                                adv(gens[n_ - back])

                    for grp in range(3):
                        sl = R["wslot"]()
                        c_lo = 416 + grp * 512
                        DMA("sp", sl.t[:], win_b[1 + grp], [winc], sl.b)
                        for f in range(4):
                            g_ = qkv_chunk(grp, f, sl)
                            gens.append(g_)
                            adv(g_)
                            pump()
                    qkv_tail = {"gens": gens, "k": 0}

                    def qkv_drain_step():
                        gens.append(None)
                        pump()

                    gts = [gst() for _ in range(3)]
                    ndrain = [0]
                    for gi, c_lo in enumerate([1968, 2480, 2992, 3504, 4016]):
                        sl = R["wslot"]()
                        DMA("sp", sl.t[:], win_b[4 + gi], [winc], sl.b)
                        for t in range(3):
                            ps = psum()
                            for k in range(8):
                                mm(ps, ps.t[:, :], uT.t[:, k, t * 128:(t + 1) * 128], sl.t[:, k, :],
                                   k == 0, k == 7, [sl.b, ub[t]])
                            if ndrain[0] < 6:
                                qkv_drain_step()
                                ndrain[0] += 1
                            if gi == 0:
                                z = zst()
                                A("act", lambda e, ps=ps, z=z: e.activation(out=z.t[:], in_=ps.t[:, :], func=AF.Silu),
                                  w=[ps.b, z.b])
                                DMA("pool", zs[t0 + t * 128:t0 + (t + 1) * 128, :], z.t[:], [z.b], sc["zs"], own=z.b)
                            else:
                                g_ = gts[t]
                                A("act", lambda e, ps=ps, g_=g_, gi=gi: e.activation(
                                    out=g_.t[:, (gi - 1) * 512:gi * 512], in_=ps.t[:, :], func=AF.Sigmoid),
                                  w=[ps.b, g_.b])
                                if gi == 4:
                                    DMA("pool", gates[t0 + t * 128:t0 + (t + 1) * 128, :], g_.t[:], [g_.b], sc["gates"], own=g_.b)

                def load_x(s_):
                    xt = xtr()
                    DMA("sp", xt.t[:], xin[s_ * W:(s_ + 1) * W, :].rearrange("(t p) d -> p t d", p=128), [], xt.b)
                    return xt

                xts = {0: load_x(0)}
                xnTs = {0: norm_tr(R, norm_chain(R, xts[0]), nrm_t[0])}
                dxn = None
                for s_ in range(NST):
                    if s_ >= 1:
                        t0p = (s_ - 1) * W
                        xp = xts[s_ - 1]
                        DMA("pool", h1s[t0p:t0p + W, :].rearrange("(t p) d -> p t d", p=128), xp.t[:], [xp.b], h1b,
                            own=xp.b)
                        dxn = norm_chain(R, xp)
                    hid = ffn_gu(R, xnTs[s_], 0)
                    if s_ + 1 < NST:
                        xts[s_ + 1] = load_x(s_ + 1)
                    axn = norm_chain(R, xts[s_ + 1]) if s_ + 1 < NST else None
                    if s_ >= 1:
                        uT = norm_tr(R, dxn, nrm_t[1])
                        inproj(s_ - 1, uT)
                    issue_late_casts(5)
                    if axn is not None:
                        xnTs[s_ + 1] = norm_tr(R, axn, nrm_t[0])
                    ffn_down(R, hid, xts[s_], 0)
                xp = xts[NST - 1]
                t0p = (NST - 1) * W
                DMA("pool", h1s[t0p:t0p + W, :].rearrange("(t p) d -> p t d", p=128), xp.t[:], [xp.b], h1b, own=xp.b)
                uT = norm_tr(R, norm_chain(R, xp), nrm_t[1])
                inproj(NST - 1, uT)
                issue_late_casts(1000)
                S.barrier()
                S.emit()

        if 2 in stages:
            with ExitStack() as st:
                mst = {"i": 0}
                gst_ = {"i": 0}

                def psum_m():
                    r = banks[mst["i"] % 3]
                    mst["i"] += 1
                    return r

                def psum_g():
                    r = banks[3 + gst_["i"] % 3]
                    gst_["i"] += 1
                    return r

                gsl = []
                for i in range(2):
                    gsl.append({
                        "qT": sbt(st, f"g_qT{i}", [64, 8, 128], BF16),
                        "kT": sbt(st, f"g_kT{i}", [64, 8, 128], BF16),
                        "kt": sbt(st, f"g_kt{i}", [128, 512], BF16),
                        "vt": sbt(st, f"g_vt{i}", [128, 512], BF16),
                        "zt": sbt(st, f"g_zt{i}", [128, 512], BF16),
                        "bt": sbt(st, f"g_bt{i}", [128, 8], F32),
                        "gt": sbt(st, f"g_gt{i}", [128, 8], F32),
                        "b": S.buf(f"gslot{i}"),
                    })
                GT = F32 if GDN_F32 else BF16
                gclr = ring(st, "gcl", [128, 48], F32, 2)
                kdr = ring(st, "kdec", [128, 512], BF16, 2)
                gonr = ring(st, "gon", [128, 8, 128], F32, 2)
                xgr = ring(st, "xg", [128, 4, 128], F32, 1)
                etsr = ring(st, "ets", [128, 4, 128], F32, 1)
                etir = ring(st, "eti", [128, 4, 128], F32, 1)
                GL = BF16
                mtr = ring(st, "mtp", [128, 8, 128], GL, 4)
                nnr = ring(st, "nnp", [128, 8, 128], GL, 4)
                mt32r = ring(st, "mt32", [128, 8, 128], F32, 2)
                n0fr = ring(st, "n0f", [128, 8, 128], F32, 1)
                g32r = ring(st, "g32", [128, 8, 128], F32, 1)
                x0tr = ring(st, "x0t", [128, 8, 128], F32, 1)
                z32r = ring(st, "z32", [128, 8, 128], F32, 2)
                zbr = ring(st, "zb", [128, 8, 128], GL, 4)
                tfr = ring(st, "tfin", [128, 8, 128], GT, 2)
                itr = ring(st, "intraT", [128, 8, 128], BF16, 2)
                rtr = ring(st, "rt", [128, 8, 64], F32, 1)
                rbr = ring(st, "Rb", [128, 8, 64], GT, 1)
                vnr = ring(st, "vnew", [128, 8, 64], BF16, 1)
                o32r = ring(st, "o32", [128, 8, 64], F32, 2)
                sqr_ = ring(st, "osq", [128, 8, 64], F32, 1)
                ssr = ring(st, "oss", [128, 16], F32, 2)
                obr = ring(st, "gob", [128, 512], BF16, 2)
                S32 = tb(st, "S32", [64, 8, 64], F32)
                Sb = tb(st, "Sb", [64, 8, 64], BF16)
                A("pool", lambda e: e.memset(S32.t[:], 0.0), w=[S32.b])
                A("pool", lambda e: e.memset(Sb.t[:], 0.0), w=[Sb.b])
                ogb = S.buf("ogdn")

                def bc3(ap2, n):
                    k = ap2.shape[1]
                    p = ap2.shape[0]
                    return ap2.unsqueeze(2).broadcast_to([p, k, n])

                def v4(ps):
                    return ps.t[:].rearrange("p (h n) -> p h n", h=4)

                def fr(ap):
                    return ap.bitcast(F32R) if (GDN_F32 and GDN_F32R) else ap

                def v8(ps):
                    return ps.t[:].rearrange("p (h d) -> p h d", h=8)

                def gdn_gen(par):
                    for n in range(par, NB if nb_lim is None else nb_lim, 2):
                        z32 = z32r()
                        r0 = n * 128
                        sl = gsl[n % 2]
                        sb_ = sl["b"]
                        DMA("sp", sl["qT"][:], gqT[:, r0:r0 + 128].rearrange("(h d) t -> d h t", d=64), [], sb_)
                        DMA("sp", sl["kT"][:], gkT[:, r0:r0 + 128].rearrange("(h d) t -> d h t", d=64), [], sb_)
                        DMA("sp", sl["kt"][:], gktok[r0:r0 + 128, :], [], sb_)
                        DMA("sp", sl["vt"][:], gvtok[r0:r0 + 128, :], [], sb_)
                        DMA("sp", sl["zt"][:], zs[r0:r0 + 128, :], [], sb_)
                        DMA("sp", sl["bt"][:], gbs[r0:r0 + 128, :], [], sb_)
                        DMA("sp", sl["gt"][:], ggs[r0:r0 + 128, :], [], sb_)
                        qT, kT, kt, vt, zt, bt, gt = (sl[k] for k in ["qT", "kT", "kt", "vt", "zt", "bt", "gt"])

                        psg = psum_g()
                        mm(psg, psg.t[:, 0:8], Uf[:], gt[:], True, True, [sb_, cst])
                        mm(psg, psg.t[:, 8:16], onesf[:], gt[:], True, True, [sb_, cst])
                        gcl = gclr()
                        A("dve", lambda e, psg=psg, gcl=gcl: e.tensor_copy(out=gcl.t[:, 0:16], in_=psg.t[:, 0:16]),
                          w=[psg.b, gcl.b])
                        A("act", lambda e, gcl=gcl: e.activation(out=gcl.t[:, 16:24], in_=gcl.t[:, 0:8], func=AF.Exp),
                          r=[gcl.b], w=[gcl.b])
                        A("dve", lambda e, gcl=gcl: e.tensor_tensor(out=gcl.t[:, 24:32], in0=gcl.t[:, 8:16],
                                                                    in1=gcl.t[:, 0:8], op=ALU.subtract),
                          r=[gcl.b], w=[gcl.b])
                        A("act", lambda e, gcl=gcl: e.activation(out=gcl.t[:, 24:32], in_=gcl.t[:, 24:32],
                                                                 func=AF.Exp), r=[gcl.b], w=[gcl.b])
                        A("act", lambda e, gcl=gcl: e.activation(out=gcl.t[:, 32:40], in_=gcl.t[:, 8:16], func=AF.Exp),
                          r=[gcl.b], w=[gcl.b])
                        kd = kdr()
                        A("pool", lambda e, kd=kd, kt=kt, gcl=gcl: e.tensor_tensor(
                            out=kd.t[:].rearrange("p (h d) -> p h d", h=8),
                            in0=kt[:].rearrange("p (h d) -> p h d", h=8),
                            in1=bc3(gcl.t[:, 24:32], 64), op=ALU.mult), r=[sb_, gcl.b], w=[kd.b])
                        yield

                        mt32 = mt32r()
                        mt = mtr()
                        it_ = itr()
                        gon = gonr()
                        A("dve", lambda e, gon=gon, gt=gt: e.tensor_tensor(
                            out=gon.t[:], in0=onesf[:, :].unsqueeze(1).broadcast_to([128, 8, 128]),
                            in1=bc3(gt[:, 0:8], 128), op=ALU.mult), r=[sb_, cst], w=[gon.b])
                        for g in range(2):
                            psb = psum_g()
                            for hh in range(4):
                                h = 4 * g + hh
                                mm(psb, psb.t[:, hh * 128:(hh + 1) * 128], gon.t[:, h, :], Uf[:], True, True,
                                   [gon.b, cst])
                            xg = xgr()
                            A("dve", lambda e, psb=psb, xg=xg, gcl=gcl, g=g: e.tensor_tensor(
                                out=xg.t[:], in0=v4(psb), in1=bc3(gcl.t[:, 4 * g:4 * g + 4], 128), op=ALU.subtract),
                              r=[gcl.b], w=[psb.b, xg.b])
                            A("dve", lambda e, xg=xg: e.tensor_tensor(
                                out=xg.t[:], in0=xg.t[:], in1=maskI[:, :].unsqueeze(1).broadcast_to([128, 4, 128]),
                                op=ALU.mult), r=[xg.b, cst], w=[xg.b])
                            yield
                            A("act", lambda e, xg=xg: e.activation(out=xg.t[:], in_=xg.t[:], func=AF.Exp),
                              r=[xg.b], w=[xg.b])
                            ets = etsr()
                            eti = etir()
                            A("dve", lambda e, xg=xg, ets=ets: e.tensor_tensor(
                                out=ets.t[:], in0=xg.t[:], in1=maskS[:, :].unsqueeze(1).broadcast_to([128, 4, 128]),
                                op=ALU.mult), r=[xg.b, cst], w=[ets.b])
                            A("pool", lambda e, xg=xg, eti=eti: e.tensor_tensor(
                                out=eti.t[:], in0=xg.t[:], in1=maskI[:, :].unsqueeze(1).broadcast_to([128, 4, 128]),
                                op=ALU.mult), r=[xg.b, cst], w=[eti.b])
                            A("pool", lambda e, ets=ets, bt=bt, g=g: e.tensor_tensor(
                                out=ets.t[:], in0=ets.t[:], in1=bc3(bt[:, 4 * g:4 * g + 4], 128), op=ALU.mult),
                              r=[ets.b, sb_], w=[ets.b])
                            yield
                            psk = psum_g()
                            psq = psum_g()
                            for hh in range(4):
                                h = 4 * g + hh
                                mm(psk, psk.t[:, hh * 128:(hh + 1) * 128], kT[:, h, :], kT[:, h, :], True, True, [sb_])
                                mm(psq, psq.t[:, hh * 128:(hh + 1) * 128], kT[:, h, :], qT[:, h, :], True, True, [sb_])
                            A("dve", lambda e, psk=psk, ets=ets, mt32=mt32, g=g: e.tensor_tensor(
                                out=mt32.t[:, 4 * g:4 * g + 4, :], in0=v4(psk), in1=ets.t[:], op=ALU.mult),
                              r=[ets.b], w=[psk.b, mt32.b])
                            A("dve", lambda e, mt32=mt32, mt=mt, g=g: e.tensor_copy(
                                out=mt.t[:, 4 * g:4 * g + 4, :], in_=mt32.t[:, 4 * g:4 * g + 4, :]),
                              r=[mt32.b], w=[mt.b])
                            A("dve", lambda e, psq=psq, eti=eti, it_=it_, g=g: e.tensor_tensor(
                                out=it_.t[:, 4 * g:4 * g + 4, :], in0=v4(psq), in1=eti.t[:], op=ALU.mult),
                              r=[eti.b], w=[psq.b, it_.b])
                            yield

                        nn = nnr()
                        pst = psum_g()
                        pvb = pst.t[:].bitcast(BF16)
                        for h in range(8):
                            tr(pst, pvb[:, h * 128:(h + 1) * 128], mt.t[:, h, :], identb[:], [mt.b, cst])
                        A("dve", lambda e, pvb=pvb, nn=nn, pst=pst: e.tensor_copy(
                            out=nn.t[:], in_=pvb.rearrange("p (h n) -> p h n", h=8)), w=[pst.b, nn.b])
                        A("dve", lambda e, z32=z32, mt32=mt32: e.tensor_tensor(
                            out=z32.t[:], in0=identf[:, :].unsqueeze(1).broadcast_to([128, 8, 128]), in1=mt32.t[:],
                            op=ALU.subtract), r=[mt32.b, cst], w=[z32.b])
                        zb = zbr()
                        A("dve", lambda e, z32=z32, zb=zb: e.tensor_copy(out=zb.t[:], in_=z32.t[:]), r=[z32.b], w=[zb.b])
                        yield
                        mtp, np_ = mt, nn
                        for l in range(1, 7):
                            n2 = nnr()
                            psn = [psum_g(), psum_g()]
                            for h in range(8):
                                mm(psn[h // 4], psn[h // 4].t[:, (h % 4) * 128:(h % 4 + 1) * 128], fr(mtp.t[:, h, :]),
                                   fr(np_.t[:, h, :]), True, True, [mtp.b, np_.b])
                            for g in range(2):
                                A("dve", lambda e, g=g, psn=psn, n2=n2: e.tensor_copy(
                                    out=n2.t[:, 4 * g:4 * g + 4, :], in_=v4(psn[g])),
                                  w=[psn[g].b, n2.b])
                            if l <= 5:
                                m2 = mtr()
                                psm = [psum_g(), psum_g()]
                                for h in range(8):
                                    mm(psm[h // 4], psm[h // 4].t[:, (h % 4) * 128:(h % 4 + 1) * 128], fr(np_.t[:, h, :]),
                                       fr(mtp.t[:, h, :]), True, True, [mtp.b, np_.b])
                                for g in range(2):
                                    A("act", lambda e, g=g, psm=psm, m2=m2: e.activation(
                                        out=m2.t[:, 4 * g:4 * g + 4, :], in_=v4(psm[g]), func=AF.Copy),
                                      w=[psm[g].b, m2.b])
                            else:
                                m2 = None
                            yield
                            psz = [psum_g(), psum_g()]
                            for h in range(8):
                                mm(psz[h // 4], psz[h // 4].t[:, (h % 4) * 128:(h % 4 + 1) * 128], fr(n2.t[:, h, :]),
                                   fr(zb.t[:, h, :]), True, True, [n2.b, zb.b])
                            for g in range(2):
                                A("dve", lambda e, z32=z32, g=g, psz=psz: e.tensor_tensor(
                                    out=z32.t[:, 4 * g:4 * g + 4, :], in0=z32.t[:, 4 * g:4 * g + 4, :],
                                    in1=v4(psz[g]), op=ALU.add), r=[z32.b], w=[psz[g].b, z32.b])
                            if l < 6:
                                zb = zbr()
                                A("dve", lambda e, z32=z32, zb=zb: e.tensor_copy(out=zb.t[:], in_=z32.t[:]),
                                  r=[z32.b], w=[zb.b])
                            mtp, np_ = m2, n2
                            yield
                        n0f = n0fr()
                        for g in range(2):
                            pst = psum_g()
                            for hh in range(4):
                                tr(pst, pst.t[:, hh * 128:(hh + 1) * 128], mt32.t[:, 4 * g + hh, :], identf[:],
                                   [mt32.b, cst])
                            A("dve", lambda e, pst=pst, n0f=n0f, g=g: e.tensor_copy(
                                out=n0f.t[:, 4 * g:4 * g + 4, :], in_=v4(pst)), w=[pst.b, n0f.b])
                        g32 = g32r()
                        A("pool", lambda e, z32=z32, g32=g32: e.tensor_tensor(
                            out=g32.t[:], in0=identf[:, :].unsqueeze(1).broadcast_to([128, 8, 128]), in1=z32.t[:],
                            op=ALU.subtract), r=[z32.b, cst], w=[g32.b])
                        yield
                        pse = [psum_g(), psum_g()]
                        for h in range(8):
                            mm(pse[h // 4], pse[h // 4].t[:, (h % 4) * 128:(h % 4 + 1) * 128], n0f.t[:, h, :],
                               z32.t[:, h, :], True, True, [n0f.b, z32.b])
                        for g in range(2):
                            A("dve", lambda e, g=g, pse=pse, g32=g32: e.tensor_tensor(
                                out=g32.t[:, 4 * g:4 * g + 4, :], in0=g32.t[:, 4 * g:4 * g + 4, :], in1=v4(pse[g]),
                                op=ALU.subtract), r=[g32.b], w=[pse[g].b, g32.b])
                        x0t = x0tr()
                        for g in range(2):
                            pst = psum_g()
                            for hh in range(4):
                                tr(pst, pst.t[:, hh * 128:(hh + 1) * 128], z32.t[:, 4 * g + hh, :], identf[:],
                                   [z32.b, cst])
                            A("dve", lambda e, pst=pst, x0t=x0t, g=g: e.tensor_copy(
                                out=x0t.t[:, 4 * g:4 * g + 4, :], in_=v4(pst)), w=[pst.b, x0t.b])
                        yield
                        tf = tfr()
                        psx = [psum_g(), psum_g()]
                        for h in range(8):
                            mm(psx[h // 4], psx[h // 4].t[:, (h % 4) * 128:(h % 4 + 1) * 128], x0t.t[:, h, :],
                               g32.t[:, h, :], True, True, [x0t.b, g32.b])
                        for g in range(2):
                            A("dve", lambda e, z32=z32, g=g, psx=psx, tf=tf: e.tensor_tensor(
                                out=tf.t[:, 4 * g:4 * g + 4, :], in0=z32.t[:, 4 * g:4 * g + 4, :], in1=v4(psx[g]),
                                op=ALU.add), r=[z32.b], w=[psx[g].b, tf.b])
                        yield

                        psks = psum_g()
                        for h in range(8):
                            mm(psks, psks.t[:, h * 64:(h + 1) * 64], kT[:, h, :], Sb.t[:, h, :], True, True,
                               [sb_, Sb.b])
                        rt = rtr()
                        A("dve", lambda e, psks=psks, rt=rt, gcl=gcl: e.tensor_tensor(
                            out=rt.t[:], in0=v8(psks), in1=bc3(gcl.t[:, 16:24], 64), op=ALU.mult),
                          r=[gcl.b], w=[psks.b, rt.b])
                        psqs = psum_g()
                        for h in range(8):
                            mm(psqs, psqs.t[:, h * 64:(h + 1) * 64], qT[:, h, :], Sb.t[:, h, :], True, True,
                               [sb_, Sb.b])
                        o32 = o32r()
                        A("dve", lambda e, psqs=psqs, o32=o32, gcl=gcl: e.tensor_tensor(
                            out=o32.t[:], in0=v8(psqs), in1=bc3(gcl.t[:, 16:24], 64), op=ALU.mult),
                          r=[gcl.b], w=[psqs.b, o32.b])
                        Rb = rbr()
                        A("dve", lambda e, rt=rt, Rb=Rb, vt=vt: e.tensor_tensor(
                            out=Rb.t[:], in0=vt[:].rearrange("p (h d) -> p h d", h=8), in1=rt.t[:], op=ALU.subtract),
                          r=[rt.b, sb_], w=[Rb.b])
                        yield
                        pstr = psum_g()
                        for h in range(8):
                            mm(pstr, pstr.t[:, h * 64:(h + 1) * 64], fr(tf.t[:, h, :]), fr(Rb.t[:, h, :]), True, True,
                               [tf.b, Rb.b])
                        vn = vnr()
                        A("dve", lambda e, pstr=pstr, vn=vn, bt=bt: e.tensor_tensor(
                            out=vn.t[:], in0=v8(pstr), in1=bc3(bt[:, 0:8], 64), op=ALU.mult),
                          r=[sb_], w=[pstr.b, vn.b])
                        yield
                        psd = psum_g()
                        for h in range(8):
                            mm(psd, psd.t[0:64, h * 64:(h + 1) * 64], kd.t[:, h * 64:(h + 1) * 64], vn.t[:, h, :],
                               True, True, [kd.b, vn.b])
                        A("dve", lambda e, gcl=gcl: e.tensor_tensor(
                            out=S32.t[:], in0=S32.t[:], in1=gcl.t[0:64, 32:40].unsqueeze(2).broadcast_to([64, 8, 64]),
                            op=ALU.mult), r=[gcl.b, S32.b], w=[S32.b])
                        A("dve", lambda e, psd=psd: e.tensor_tensor(
                            out=S32.t[:], in0=S32.t[:], in1=psd.t[0:64, :].rearrange("p (h d) -> p h d", h=8),
                            op=ALU.add), r=[S32.b], w=[psd.b, S32.b])
                        A("dve", lambda e: e.tensor_copy(out=Sb.t[:], in_=S32.t[:]), r=[S32.b], w=[Sb.b])
                        pso = psum_g()
                        for h in range(8):
                            mm(pso, pso.t[:, h * 64:(h + 1) * 64], it_.t[:, h, :], vn.t[:, h, :], True, True,
                               [it_.b, vn.b])
                        A("dve", lambda e, pso=pso, o32=o32: e.tensor_tensor(
                            out=o32.t[:], in0=o32.t[:], in1=v8(pso), op=ALU.add), r=[o32.b], w=[pso.b, o32.b])
                        yield
                        sq = sqr_()
                        A("pool", lambda e, o32=o32, sq=sq: e.tensor_tensor(out=sq.t[:], in0=o32.t[:], in1=o32.t[:],
                                                                            op=ALU.mult), r=[o32.b], w=[sq.b])
                        ss = ssr()
                        A("dve", lambda e, sq=sq, ss=ss: e.tensor_reduce(out=ss.t[:, 0:8], in_=sq.t[:], axis=AX.X,
                                                                         op=ALU.add), r=[sq.b], w=[ss.b])
                        A("act", lambda e, ss=ss: e.activation(out=ss.t[:, 8:16], in_=ss.t[:, 0:8], func=AF.Sqrt,
                                                               scale=1.0 / 64, bias=EPS), r=[ss.b], w=[ss.b])
                        A("dve", lambda e, ss=ss: e.reciprocal(out=ss.t[:, 8:16], in_=ss.t[:, 8:16]),
                          r=[ss.b], w=[ss.b])
                        A("dve", lambda e, o32=o32, ss=ss: e.tensor_tensor(
                            out=o32.t[:], in0=o32.t[:], in1=bc3(ss.t[:, 8:16], 64), op=ALU.mult),
                          r=[ss.b, o32.b], w=[o32.b])
                        A("pool", lambda e, o32=o32: e.tensor_tensor(
                            out=o32.t[:], in0=o32.t[:], in1=gnorm_t[:, :].unsqueeze(1).broadcast_to([128, 8, 64]),
                            op=ALU.mult), r=[o32.b, prm], w=[o32.b])
                        ob = obr()
                        A("pool", lambda e, o32=o32, ob=ob, zt=zt: e.tensor_tensor(
                            out=ob.t[:].rearrange("p (h d) -> p h d", h=8), in0=o32.t[:],
                            in1=zt[:].rearrange("p (h d) -> p h d", h=8), op=ALU.mult), r=[o32.b, sb_], w=[ob.b])
                        DMA("pool", ogdn[r0:r0 + 128, :], ob.t[:], [ob.b], ogb, own=ob.b)
                        yield

                ggs_ = [gdn_gen(0), gdn_gen(1)]
                gstate = {"done": False, "d": [False, False], "i": 0}

                def gdn_step(k=2):
                    for _ in range(k):
                        if gstate["done"]:
                            return
                        i = gstate["i"]
                        gstate["i"] += 1
                        w = 0 if i < 11 else (i - 11 + 1) % 2
                        if gstate["d"][w]:
                            w = 1 - w
                        try:
                            next(ggs_[w])
                        except StopIteration:
                            gstate["d"][w] = True
                            gstate["done"] = all(gstate["d"])

                Kt = [tb(st, f"Kh{h}", [96, LP], BF16) for h in range(4)]
                Va = tb(st, "Vaug", [128, NB, 4, 65], BF16)
                ckv = tb(st, "ckv", [128, LP], BF16)
                wuk_t = sbt(st, "wuk_t", [128, 512], BF16)
                wuv_t = sbt(st, "wuv_t", [128, 512], BF16)
                wuq_t = sbt(st, "wuq_t", [128, 2, 768], BF16)
                wuqs_t = sbt(st, "wuqs_t", [128, 2, 768], BF16)
                wb = S.buf("mlaw")
                DMA("pool", wuk_t[:], wuk_f, [], wb)
                DMA("pool", wuv_t[:], wuv_f, [], wb)
                DMA("pool", wuq_t[:], wuq_f.rearrange("(c p) n -> p c n", p=128), [], wb)
                DMA("pool", wuqs_t[:], wuqs_f.rearrange("(c p) n -> p c n", p=128), [], wb)
                for s_ in range(3):
                    DMA("sp", ckv.t[:, s_ * 1408:(s_ + 1) * 1408], ckvnT[:, s_ * 1408:(s_ + 1) * 1408], [], ckv.b)
                A("pool", lambda e: e.memset(Va.t[:, :, :, 64:65], 1.0), w=[Va.b])
                cqr = ring(st, "cqs", [128, 2, W], BF16, 2)
                tcr = ring(st, "mtabc", [96, W], F32, 2)
                tsr = ring(st, "mtabs", [96, W], F32, 2)
                qtr = ring(st, "QT", [96, W], BF16, 3)
                t1r = ring(st, "qt1", [96, W], F32, 2)
                t2r = ring(st, "qt2", [96, W], F32, 2)
                ptr = ring(st, "PT", [128, W], BF16, 5)
                rcr = ring(st, "rc", [128, 4], F32, 2)
                omr = ring(st, "omst", [128, 3, 256], BF16, 2)
                omb = S.buf("omla")
                sm_scale = float(96 ** -0.5)
                do_mla = 3 in stages
                kbc = 0
                for half in range(2 if do_mla else 0):
                    ei = 0
                    for hl in range(4):
                        h = half * 4 + hl
                        DMA("sp", Kt[hl].t[64:96, :], kropeT[:, :], [], Kt[hl].b)
                        for s_ in range(NST):
                            ps = psum_m()
                            mm(ps, ps.t[0:64, 0:W], wuk_t[:, h * 64:(h + 1) * 64], ckv.t[:, s_ * W:(s_ + 1) * W],
                               True, True, [wb, ckv.b])
                            if ei % 2 == 0:
                                A("act", lambda e, ps=ps, hl=hl, s_=s_: e.activation(
                                    out=Kt[hl].t[0:64, s_ * W:(s_ + 1) * W], in_=ps.t[0:64, 0:W], func=AF.Copy),
                                  w=[ps.b, Kt[hl].b])
                            else:
                                A("dve", lambda e, ps=ps, hl=hl, s_=s_: e.tensor_copy(
                                    out=Kt[hl].t[0:64, s_ * W:(s_ + 1) * W], in_=ps.t[0:64, 0:W]),
                                  w=[ps.b, Kt[hl].b])
                            ei += 1
                            if ei % 4 == 0:
                                gdn_step()
                    for blk in range(NB):
                        ps = psum_m()
                        mm(ps, ps.t[:, 0:256], ckv.t[:, blk * 128:(blk + 1) * 128],
                           wuv_t[:, half * 256:(half + 1) * 256], True, True, [wb, ckv.b])
                        if blk % 2 == 0:
                            A("act", lambda e, ps=ps, blk=blk: e.activation(
                                out=Va.t[:, blk, :, 0:64], in_=ps.t[:, 0:256].rearrange("p (h d) -> p h d", h=4),
                                func=AF.Copy), w=[ps.b, Va.b])
                        else:
                            A("dve", lambda e, ps=ps, blk=blk: e.tensor_copy(
                                out=Va.t[:, blk, :, 0:64], in_=ps.t[:, 0:256].rearrange("p (h d) -> p h d", h=4)),
                              w=[ps.b, Va.b])
                        if blk % 4 == 3:
                            gdn_step()
                    for s_ in range(NST if nst_lim is None else nst_lim):
                        t0 = s_ * W
                        cq = cqr()
                        DMA("sp", cq.t[:], cqnT[:, t0:t0 + W].rearrange("(c p) t -> p c t", p=128), [], cq.b)
                        tc_ = tcr()
                        ts_ = tsr()
                        DMA("sp", tc_.t[64:96, :], cosT_d[:, t0:t0 + W], [], tc_.b)
                        DMA("sp", ts_.t[64:96, :], sinT_d[:, t0:t0 + W], [], ts_.b)
                        om = omr()

                        def prep_q(hl, cq=cq, tc_=tc_, ts_=ts_, half=half):
                            h = half * 4 + hl
                            psq = banks[7]
                            for c in range(2):
                                mm(psq, psq.t[0:96, 0:W], wuq_t[:, c, h * 96:(h + 1) * 96], cq.t[:, c, :], c == 0,
                                   c == 1, [wb, cq.b])
                            QT = qtr()
                            A("act", lambda e, psq=psq, QT=QT: e.activation(out=QT.t[0:64, :], in_=psq.t[0:64, 0:W],
                                                                            func=AF.Copy), w=[psq.b, QT.b])
                            t1 = t1r()
                            t2 = t2r()
                            A("dve", lambda e, psq=psq, t1=t1, tc_=tc_: e.tensor_tensor(
                                out=t1.t[64:96, :], in0=psq.t[64:96, 0:W], in1=tc_.t[64:96, :], op=ALU.mult),
                              r=[tc_.b], w=[psq.b, t1.b])
                            psq2 = banks[7]
                            for c in range(2):
                                mm(psq2, psq2.t[0:96, 0:W], wuqs_t[:, c, h * 96:(h + 1) * 96], cq.t[:, c, :], c == 0,
                                   c == 1, [wb, cq.b])
                            A("dve", lambda e, psq2=psq2, t2=t2, ts_=ts_: e.tensor_tensor(
                                out=t2.t[64:96, :], in0=psq2.t[64:96, 0:W], in1=ts_.t[64:96, :], op=ALU.mult),
                              r=[ts_.b], w=[psq2.b, t2.b])
                            A("dve", lambda e, t1=t1, t2=t2, QT=QT: e.tensor_tensor(
                                out=QT.t[64:96, :], in0=t1.t[64:96, :], in1=t2.t[64:96, :], op=ALU.add),
                              r=[t1.b, t2.b, QT.b], w=[QT.b])
                            return QT

                        QTn = prep_q(0)
                        for hl in range(4):
                            h = half * 4 + hl
                            QT = QTn
                            pso = banks[6]
                            mm(pso, pso.t[:, 0:195], zerob[:, 0:128], zerob[:, 0:195], True, True, [cst])
                            nkb = 3 * s_ + 3

                            def issue_pss(kb, hl=hl, QT=QT, s_=s_):
                                i0 = max(0, kb - 3 * s_)
                                ncols = (3 - i0) * 128
                                pss = psum_m()
                                diag = kb >= 3 * s_
                                mm(pss, pss.t[:, 0:ncols], Kt[hl].t[0:96, kb * 128:(kb + 1) * 128],
                                   QT.t[0:96, i0 * 128:W], True, not diag, [Kt[hl].b, QT.b])
                                if diag:
                                    mm(pss, pss.t[:, 0:128], identb[:], negmask[:], False, True, [cst])
                                return pss, i0, ncols

                            pend = [issue_pss(0)]
                            if nkb > 1:
                                pend.append(issue_pss(1))
                            for kb in range(nkb):
                                pss, i0, ncols = pend.pop(0)
                                if kb + 2 < nkb:
                                    pend.append(issue_pss(kb + 2))
                                if kb == min(2, nkb - 1) and hl + 1 < 4:
                                    QTn = prep_q(hl + 1)
                                PT = ptr()
                                A("act", lambda e, pss=pss, PT=PT, i0=i0, ncols=ncols: e.activation(
                                    out=PT.t[:, i0 * 128:W], in_=pss.t[:, 0:ncols], func=AF.Exp, scale=sm_scale),
                                  w=[pss.b, PT.b])
                                for i in range(i0, 3):
                                    mm(pso, pso.t[:, i * 65:(i + 1) * 65], PT.t[:, i * 128:(i + 1) * 128],
                                       Va.t[:, kb, hl, :], False, False, [PT.b, Va.b], skip=True)
                                kbc += 0.56
                                while kbc >= 1.0:
                                    gdn_step(1)
                                    kbc -= 1.0
                            rc = rcr()
                            A("dve", lambda e, pso=pso, rc=rc: e.reciprocal(
                                out=rc.t[:, 0:3], in_=pso.t[:, 0:195].rearrange("p (i d) -> p i d", i=3)[:, :, 64]),
                              w=[pso.b, rc.b])
                            A("dve", lambda e, pso=pso, rc=rc, om=om, hl=hl: e.tensor_tensor(
                                out=om.t[:, :, hl * 64:(hl + 1) * 64],
                                in0=pso.t[:, 0:195].rearrange("p (i d) -> p i d", i=3)[:, :, 0:64],
                                in1=rc.t[:, 0:3].unsqueeze(2).broadcast_to([128, 3, 64]), op=ALU.mult),
                              r=[rc.b], w=[pso.b, om.b])
                        DMA("pool", omla[t0:t0 + W, half * 256:(half + 1) * 256].rearrange("(t p) d -> p t d", p=128),
                            om.t[:], [om.b], omb, own=om.b)
                while not gstate["done"]:
                    gdn_step()
                S.barrier()
                S.emit()

        if 4 in stages:
            with ExitStack() as st:
                R = {
                    "xnT": ring(st, "xnT3", [128, 8, W], BF16, 2),
                    "junk": ring(st, "junk3", [128, D], BF16, 2),
                    "ss": ring(st, "ss3", [128, 4], F32, 4),
                    "xn": ring(st, "xn3", [128, D], BF16, 3),
                    "hid": ring(st, "hid3", [128, NF, W], BF16, 1),
                    "wslot": ring(st, "wslot3", [128, 8, 512], BF16, 6),
                    "sil": ring(st, "sil3", [128, W], F32, 2),
                    "hb": [S.buf(f"hid3_{f}") for f in range(NF)],
                }
                wmo_t = sbt(st, "wmo_t", [128, 4, D], BF16)
                wgo_t = sbt(st, "wgo_t", [128, 4, D], BF16)
                wout_t = sbt(st, "wout_t", [128, 8, D], BF16)
                fnorm_t = sbt(st, "fnorm_t", [128, D], F32)
                w3 = S.buf("w3")
                DMA("pool", wmo_t[:], wmo_f.rearrange("(c p) n -> p c n", p=128), [], w3)
                DMA("pool", wgo_t[:], wgo_f.rearrange("(c p) n -> p c n", p=128), [], w3)
                for c in range(8):
                    DMA("pool", wout_t[:, c, :], wout_f[c * 128:(c + 1) * 128, :], [], w3)
                DMA("sp", fnorm_t[:], fnorm_d, [], w3)
                htr = ring(st, "ht", [128, 3, D], F32, 2)
                omr3 = ring(st, "om3", [128, 3, 512], BF16, 2)
                ogr3 = ring(st, "og3", [128, 3, 512], BF16, 2)
                gtr3 = ring(st, "gt3", [128, 2048], BF16, 2)
                oTr = ring(st, "oT3", [128, 8, 128], BF16, 2)
                mar = ring(st, "ma3", [128, 512], F32, 2)
                mbr = ring(st, "mb3", [128, 512], F32, 2)
                mrr = ring(st, "mrg3", [128, D], BF16, 2)
                mTr = ring(st, "mT3", [128, 8, 128], BF16, 2)
                outr = ring(st, "ost3", [128, D], F32, 2)
                outb = S.buf("out")
                def merge(s_):
                    t0 = s_ * W
                    ht = htr()
                    DMA("sp", ht.t[:], h1s[t0:t0 + W, :].rearrange("(t p) d -> p t d", p=128), [], ht.b)
                    om = omr3()
                    og = ogr3()
                    DMA("sp", om.t[:], omla[t0:t0 + W, :].rearrange("(t p) d -> p t d", p=128), [], om.b)
                    DMA("sp", og.t[:], ogdn[t0:t0 + W, :].rearrange("(t p) d -> p t d", p=128), [], og.b)
                    for t in range(3):
                        gt_ = gtr3()
                        DMA("sp", gt_.t[:], gates[t0 + t * 128:t0 + (t + 1) * 128, :], [], gt_.b)
                        ps = psum()
                        pv = ps.t[:].bitcast(BF16)
                        for c in range(4):
                            tr(ps, pv[:, c * 128:(c + 1) * 128], om.t[:, t, c * 128:(c + 1) * 128], identb[:],
                               [om.b, cst])
                        for c in range(4):
                            tr(ps, pv[:, (4 + c) * 128:(5 + c) * 128], og.t[:, t, c * 128:(c + 1) * 128], identb[:],
                               [og.b, cst])
                        oT = oTr()
                        A("act", lambda e, pv=pv, oT=oT: e.activation(
                            out=oT.t[:], in_=pv.rearrange("p (c n) -> p c n", c=8), func=AF.Copy), w=[ps.b, oT.b])
                        mrg = mrr()
                        for half in range(2):
                            hs = slice(half * 512, (half + 1) * 512)
                            psm = psum()
                            for c in range(4):
                                mm(psm, psm.t[:, :], oT.t[:, c, :], wmo_t[:, c, hs], c == 0, c == 3, [oT.b, w3])
                            psg = psum()
                            for c in range(4):
                                mm(psg, psg.t[:, :], oT.t[:, 4 + c, :], wgo_t[:, c, hs], c == 0, c == 3, [oT.b, w3])
                            ma = mar()
                            mb = mbr()
                            A("dve", lambda e, psm=psm, ma=ma, gt_=gt_, half=half: e.tensor_tensor(
                                out=ma.t[:], in0=psm.t[:, :], in1=gt_.t[:, half * 512:(half + 1) * 512], op=ALU.mult),
                              r=[gt_.b], w=[psm.b, ma.b])
                            A("dve", lambda e, psg=psg, mb=mb, gt_=gt_, half=half: e.tensor_tensor(
                                out=mb.t[:], in0=psg.t[:, :], in1=gt_.t[:, 1024 + half * 512:1024 + (half + 1) * 512],
                                op=ALU.mult), r=[gt_.b], w=[psg.b, mb.b])
                            A("pool", lambda e, ma=ma, mb=mb, mrg=mrg, half=half: e.tensor_tensor(
                                out=mrg.t[:, half * 512:(half + 1) * 512], in0=ma.t[:], in1=mb.t[:], op=ALU.add),
                              r=[ma.b, mb.b], w=[mrg.b])
                        ps2 = psum()
                        pv2 = ps2.t[:].bitcast(BF16)
                        for c in range(8):
                            tr(ps2, pv2[:, c * 128:(c + 1) * 128], mrg.t[:, c * 128:(c + 1) * 128], identb[:],
                               [mrg.b, cst])
                        mT = mTr()
                        A("act", lambda e, pv2=pv2, mT=mT: e.activation(
                            out=mT.t[:], in_=pv2.rearrange("p (c n) -> p c n", c=8), func=AF.Copy), w=[ps2.b, mT.b])
                        for half in range(2):
                            pso = psum()
                            for c in range(8):
                                mm(pso, pso.t[:, :], mT.t[:, c, :], wout_t[:, c, half * 512:(half + 1) * 512], c == 0,
                                   c == 7, [mT.b, w3])
                            A("dve", lambda e, pso=pso, ht=ht, t=t, half=half: e.tensor_tensor(
                                out=ht.t[:, t, half * 512:(half + 1) * 512], in0=ht.t[:, t, half * 512:(half + 1) * 512],
                                in1=pso.t[:, :], op=ALU.add), r=[ht.b], w=[pso.b, ht.b])
                    return ht

                def final(s_, ht):
                    t0 = s_ * W
                    for t in range(3):
                        junk = R["junk"]()
                        ss = R["ss"]()
                        A("dve", lambda e, ss=ss: e.memset(ss.t[:], 0.0), w=[ss.b])
                        A("act", lambda e, t=t, junk=junk, ss=ss, ht=ht: e.activation(
                            out=junk.t[:], in_=ht.t[:, t, :], func=AF.Square, accum_out=ss.t[:, 0:1]),
                          r=[ht.b, ss.b], w=[junk.b, ss.b])
                        A("act", lambda e, ss=ss: e.activation(out=ss.t[:, 1:2], in_=ss.t[:, 0:1], func=AF.Sqrt,
                                                               scale=1.0 / D, bias=EPS), r=[ss.b], w=[ss.b])
                        A("dve", lambda e, ss=ss: e.reciprocal(out=ss.t[:, 2:3], in_=ss.t[:, 1:2]), r=[ss.b], w=[ss.b])
                        ot = outr()
                        A("dve", lambda e, ot=ot, ht=ht, t=t, ss=ss: e.scalar_tensor_tensor(
                            out=ot.t[:], in0=ht.t[:, t, :], scalar=ss.t[:, 2:3], in1=fnorm_t[:], op0=ALU.mult,
                            op1=ALU.mult), r=[ht.b, ss.b, w3], w=[ot.b])
                        DMA("pool", out_d[t0 + t * 128:t0 + (t + 1) * 128, :], ot.t[:], [ot.b], outb, own=ot.b)

                hts = {0: merge(0)}
                xnT2 = {0: norm_tr(R, norm_chain(R, hts[0]), nrm_t[2])}
                for s_ in range(NST):
                    hid = ffn_gu(R, xnT2[s_], 1)
                    axn = None
                    if s_ + 1 < NST:
                        hts[s_ + 1] = merge(s_ + 1)
                        axn = norm_chain(R, hts[s_ + 1])
                    ffn_down(R, hid, hts[s_], 1)
                    final(s_, hts[s_])
                    if axn is not None:
                        xnT2[s_ + 1] = norm_tr(R, axn, nrm_t[2])
                S.barrier()
                S.emit()

        S.barrier()
        S.emit()
    return nc


def host_inputs(inp, b):
    f32 = np.float32
    x = np.asarray(inp["x"], f32)
    xin = np.zeros((LP, D), f32)
    xin[:NMETA] = np.asarray(inp["meta_tokens"], f32)
    xin[NMETA:LREAL] = x[b]
    win = np.asarray(inp["w_in"], f32)[0]
    kr = win[:, 384:416]
    win_ext = np.concatenate([win, kr[:, 16:32], kr[:, 0:16]], axis=1)
    wuq = np.asarray(inp["w_uq"], f32)[0].reshape(256, 8, 96)
    wuqs = np.zeros_like(wuq)
    wuqs[:, :, 64:80] = wuq[:, :, 80:96]
    wuqs[:, :, 80:96] = wuq[:, :, 64:80]
    wukv = np.asarray(inp["w_ukv"], f32)[0].reshape(128, 8, 128)

    def pc(v, c):
        return np.ascontiguousarray(np.asarray(v, f32).reshape(c, 128).T)

    pos = np.arange(LP, dtype=f32)
    inv = (np.float32(10000.0) ** (-np.arange(0, 32, 2, dtype=f32) / np.float32(32))).astype(f32)
    ang = (pos[:, None] * inv[None, :]).astype(f32)
    cos = np.cos(ang).astype(f32).T
    sin = np.sin(ang).astype(f32).T
    cosT = np.concatenate([cos, cos], 0)
    sinT = np.concatenate([-sin, sin], 0)
    cwv = np.asarray(inp["conv_w"], f32)[0]
    cw = np.ascontiguousarray(cwv.reshape(4, 12, 128).transpose(2, 1, 0).reshape(128, 48))
    m = {
        "xin": xin,
        "wg1": np.asarray(inp["ffn1_w_gate"], f32)[0], "wu1": np.asarray(inp["ffn1_w_up"], f32)[0],
        "wd1": np.asarray(inp["ffn1_w_down"], f32)[0],
        "wg2": np.asarray(inp["ffn2_w_gate"], f32)[0], "wu2": np.asarray(inp["ffn2_w_up"], f32)[0],
        "wd2": np.asarray(inp["ffn2_w_down"], f32)[0],
        "win": win_ext,
        "wuq": wuq.reshape(256, 768), "wuqs": wuqs.reshape(256, 768),
        "wuk": np.ascontiguousarray(wukv[:, :, 0:64].reshape(128, 512)),
        "wuv": np.ascontiguousarray(wukv[:, :, 64:128].reshape(128, 512)),
        "wmo": np.asarray(inp["w_mla_o"], f32)[0], "wgo": np.asarray(inp["w_gdn_o"], f32)[0],
        "wout": np.asarray(inp["w_out"], f32)[0],
        "nrm1": pc(inp["ffn1_norm"][0], 8), "nrmm": pc(inp["mix_norm"][0], 8), "nrm2": pc(inp["ffn2_norm"][0], 8),
        "qn": pc(inp["q_norm"][0], 2), "kvn": pc(inp["kv_norm"][0], 1),
        "cw": cw,
        "alog": np.ascontiguousarray(np.broadcast_to(np.asarray(inp["a_log"], f32)[0][None, :], (128, 8))),
        "dtb": np.ascontiguousarray(np.broadcast_to(np.asarray(inp["dt_bias"], f32)[0][None, :], (128, 8))),
        "gnorm": np.ascontiguousarray(np.broadcast_to(np.asarray(inp["gdn_norm"], f32)[0][None, :], (128, 64))),
        "fnorm": np.ascontiguousarray(np.broadcast_to(np.asarray(inp["final_norm"], f32)[None, :], (128, D))),
        "cosT": np.ascontiguousarray(cosT), "sinT": np.ascontiguousarray(sinT),
    }
    return {k: np.ascontiguousarray(v, dtype=f32) for k, v in m.items()}


def kernel(**inputs):
    nc = build()
    in_maps = [host_inputs(inputs, b) for b in range(NCORES)]
    res = run_bass_kernel_spmd(nc, in_maps, core_ids=list(range(NCORES)))
    out = np.stack([np.asarray(res.results[b]["out"])[NMETA:LREAL] for b in range(NCORES)], axis=0)
    return out.astype(np.float32)
```

```python
import numpy as np
from contextlib import ExitStack
import concourse.bass as bass
import concourse.mybir as mybir
from concourse.bass_utils import run_bass_kernel_spmd

F32 = mybir.dt.float32
BF16 = mybir.dt.bfloat16
AF = mybir.ActivationFunctionType
ALU = mybir.AluOpType
AX = mybir.AxisListType

ENGS = ("pe", "act", "dve", "pool", "sp")
SEM_LIMIT = 30000

D = 1024
DFF = 2816
NF = 22
SEQ = 4096
NMETA = 16
LREAL = SEQ + NMETA
LP = 4224
NB = 33
W = 384
NST = 11
EPS = 1e-6
DIN_EXT = 4560
NCORES = 4
GDN_F32 = True
GDN_F32R = False
F32R = mybir.dt.float32r


class Buf:
    __slots__ = ("name", "writers", "readers", "dsem", "dcount")

    def __init__(self, name):
        self.name = name
        self.writers = []
        self.readers = []
        self.dsem = None
        self.dcount = 0


class Op:
    __slots__ = ("eng", "fn", "deps", "is_dma", "sem", "count", "needed", "seq")


class TB:
    __slots__ = ("t", "b")

    def __init__(self, t, b):
        self.t = t
        self.b = b


class Sched:
    def __init__(self, nc, es, max_dma_sems=120):
        self.nc = nc
        self.es = es
        self.ops = {e: [] for e in ENGS}
        self.bufs = []
        self.dma_sems = []
        self.max_dma_sems = max_dma_sems
        self.eng_sems = {e: [] for e in ENGS}
        self.eng_cnt = {e: 0 for e in ENGS}
        self.last_op = {e: None for e in ENGS}
        self.dma_since = []
        self.waited = {e: {} for e in ENGS}
        self.nops = 0

    def buf(self, name=None):
        b = Buf(name or f"b{len(self.bufs)}")
        self.bufs.append(b)
        return b

    def _dsem(self, b):
        if b.dsem is None:
            assert len(self.dma_sems) < self.max_dma_sems, "too many dma sems"
            s = self.es.enter_context(self.nc.semaphore(f"d{len(self.dma_sems)}"))
            self.dma_sems.append(s)
            b.dsem = s
        return b.dsem

    def add(self, eng, fn, reads=(), writes=(), dma=False, own=None):
        op = Op()
        op.eng = eng
        op.fn = fn
        op.is_dma = dma
        op.seq = self.nops
        op.needed = False
        op.sem = None
        op.count = None
        deps = []
        seen = set()

        def push(d):
            if id(d) in seen:
                return
            seen.add(id(d))
            deps.append(d)

        for b in reads:
            for d in b.writers:
                push(d)
        owner = None
        if dma:
            assert len(writes) == 1
            owner = own if own is not None else writes[0]
            osem = self._dsem(owner)
        for b in writes:
            for d in b.writers:
                if dma and d.is_dma and d.sem is osem:
                    continue
                push(d)
            for d in b.readers:
                push(d)
        if eng == "pe" and not dma:
            deps = [d for d in deps if not (d.eng == "pe" and not d.is_dma)]
        latest = {}
        rest = []
        for d in deps:
            if d.is_dma:
                rest.append(d)
            elif d.eng not in latest or latest[d.eng].seq < d.seq:
                latest[d.eng] = d
        deps = rest + list(latest.values())
        op.deps = deps
        for d in deps:
            d.needed = True
        if dma:
            op.sem = osem
            owner.dcount += 16
            op.count = owner.dcount
            self.dma_since.append(op)
        for b in reads:
            b.readers.append(op)
        for b in writes:
            if dma and owner is not b:
                b.writers = [w for w in b.writers if w.is_dma] + [op]
            else:
                b.writers = [op]
            b.readers = []
        self.ops[eng].append(op)
        if fn is not None and not dma:
            self.last_op[eng] = op
        self.nops += 1
        return op

    def barrier(self):
        deps = [self.last_op[e] for e in ENGS if self.last_op[e] is not None] + list(self.dma_since)
        for e in ENGS:
            op = Op()
            op.eng = e
            op.fn = None
            op.seq = self.nops
            op.is_dma = False
            op.needed = False
            op.sem = None
            op.count = None
            op.deps = [d for d in deps if not (d.eng == e and not d.is_dma)]
            for d in op.deps:
                d.needed = True
            self.ops[e].append(op)
        for b in self.bufs:
            b.writers = []
            b.readers = []
        self.dma_since = []

    def emit(self):
        nc = self.nc
        for e in ENGS:
            for op in self.ops[e]:
                if op.is_dma or op.fn is None:
                    continue
                if op.needed:
                    self.eng_cnt[e] += 1
                    c = self.eng_cnt[e]
                    k = (c - 1) // SEM_LIMIT
                    while len(self.eng_sems[e]) <= k:
                        self.eng_sems[e].append(
                            self.es.enter_context(nc.semaphore(f"e_{e}_{len(self.eng_sems[e])}")))
                    op.sem = self.eng_sems[e][k]
                    op.count = c - k * SEM_LIMIT

        def run(e, engine):
            waited = self.waited[e]
            for op in self.ops[e]:
                need = {}
                for d in op.deps:
                    key = id(d.sem)
                    if waited.get(key, 0) >= d.count:
                        continue
                    if key not in need or need[key][1] < d.count:
                        need[key] = (d.sem, d.count)
                for key, (s, c) in need.items():
                    engine.wait_ge(s, c)
                    waited[key] = c
                if op.fn is None:
                    continue
                ins = op.fn(engine)
                if op.is_dma:
                    ins.then_inc(op.sem, 16)
                elif op.needed:
                    ins.then_inc(op.sem, 1)
            self.ops[e] = []

        with nc.Block() as block:
            @block.tensor
            def _(eng):
                run("pe", eng)

            @block.scalar
            def _(eng):
                run("act", eng)

            @block.vector
            def _(eng):
                run("dve", eng)

            @block.gpsimd
            def _(eng):
                run("pool", eng)

            @block.sync
            def _(eng):
                run("sp", eng)


FFN_COLGROUPS = [(0, 512), (512, 512), (1024, 512), (1536, 512), (2048, 512), (2560, 256)]
FFN_ROWGROUPS = [(0, 8), (8, 8), (16, 6)]


def build(dbg=False, stages=(1, 2, 3, 4), nb_lim=None, nst_lim=None, lvl=9):
    nc = bass.Bass("TRN2", target_bir_lowering=False)

    def din(name, shape, dt=F32):
        return nc.dram_tensor(name, shape, dt, kind="ExternalInput").ap()

    def dscr(name, shape, dt):
        return nc.dram_tensor(name, shape, dt, kind=("ExternalOutput" if dbg else "Internal")).ap()

    xin = din("xin", [LP, D])
    wg_f = [din("wg1", [D, DFF]), din("wg2", [D, DFF])]
    wu_f = [din("wu1", [D, DFF]), din("wu2", [D, DFF])]
    wd_f = [din("wd1", [DFF, D]), din("wd2", [DFF, D])]
    win_f = din("win", [D, DIN_EXT])
    wuq_f = din("wuq", [256, 768])
    wuqs_f = din("wuqs", [256, 768])
    wuk_f = din("wuk", [128, 512])
    wuv_f = din("wuv", [128, 512])
    wmo_f = din("wmo", [512, D])
    wgo_f = din("wgo", [512, D])
    wout_f = din("wout", [D, D])
    nrm_d = [din("nrm1", [128, 8]), din("nrmm", [128, 8]), din("nrm2", [128, 8])]
    qn_d = din("qn", [128, 2])
    kvn_d = din("kvn", [128, 1])
    cw_d = din("cw", [128, 48])
    alog_d = din("alog", [128, 8])
    dtb_d = din("dtb", [128, 8])
    gnorm_d = din("gnorm", [128, 64])
    fnorm_d = din("fnorm", [128, D])
    cosT_d = din("cosT", [32, LP])
    sinT_d = din("sinT", [32, LP])
    out_d = nc.dram_tensor("out", [LP, D], F32, kind="ExternalOutput").ap()

    wg_b = [nc.dram_tensor(f"wgb{i}", [6, 128, 8, 512], BF16, kind="Internal").ap() for i in range(2)]
    wu_b = [nc.dram_tensor(f"wub{i}", [6, 128, 8, 512], BF16, kind="Internal").ap() for i in range(2)]
    wd_b = [nc.dram_tensor(f"wdb{i}", [6, 128, 8, 512], BF16, kind="Internal").ap() for i in range(2)]
    win_b = nc.dram_tensor("winb", [9, 128, 8, 512], BF16, kind="Internal").ap()

    h1s = dscr("h1s", [LP, D], F32)
    cqnT = dscr("cqnT", [256, LP], BF16)
    ckvnT = dscr("ckvnT", [128, LP], BF16)
    kropeT = dscr("kropeT", [32, LP], BF16)
    gqT = dscr("gqT", [512, LP], BF16)
    gkT = dscr("gkT", [512, LP], BF16)
    gktok = dscr("gktok", [LP, 512], BF16)
    gvtok = dscr("gvtok", [LP, 512], BF16)
    gbs = dscr("gbs", [LP, 8], F32)
    ggs = dscr("ggs", [LP, 8], F32)
    zs = dscr("zs", [LP, 512], BF16)
    gates = dscr("gates", [LP, 2048], BF16)
    omla = dscr("omla", [LP, 512], BF16)
    ogdn = dscr("ogdn", [LP, 512], BF16)

    with ExitStack() as es:
        S = Sched(nc, es)

        def sbt(st, name, shape, dt):
            return st.enter_context(nc.sbuf_tensor(name, shape, dt))

        def tb(st, name, shape, dt):
            return TB(sbt(st, name, shape, dt), S.buf(name))

        def ring(st, name, shape, dt, n):
            tiles = [tb(st, f"{name}{i}", shape, dt) for i in range(n)]
            state = {"i": 0}

            def nxt():
                r = tiles[state["i"] % n]
                state["i"] += 1
                return r
            return nxt

        def A(eng, fn, r=(), w=()):
            return S.add(eng, fn, reads=list(r), writes=list(w))

        def DMA(eng, out, in_, r, w, own=None):
            return S.add(eng, lambda e: e.dma_start(out=out, in_=in_), reads=list(r), writes=[w], dma=True, own=own)

        banks = [TB(es.enter_context(nc.psum_tensor(f"ps{i}", [128, 512], F32)), S.buf(f"ps{i}"))
                 for i in range(8)]
        pstate = {"i": 0}

        pcfg = {"n": 8}

        def psum():
            r = banks[pstate["i"] % pcfg["n"]]
            pstate["i"] += 1
            return r

        accst = {"i": 0}

        def psum_acc():
            r = banks[6 + accst["i"] % 2]
            accst["i"] += 1
            return r

        def mm(ps, out, lhsT, rhs, start, stop, r, skip=False):
            if skip:
                return A("pe", lambda e: e.matmul(out=out, lhsT=lhsT, rhs=rhs, start=start, stop=stop,
                                                  skip_group_check=True), r=r, w=[ps.b])
            return A("pe", lambda e: e.matmul(out=out, lhsT=lhsT, rhs=rhs, start=start, stop=stop),
                     r=r, w=[ps.b])

        def tr(ps, out, in_, ident, r):
            return A("pe", lambda e: e.transpose(out=out, in_=in_, identity=ident), r=r, w=[ps.b])

        cst = S.buf("consts")
        identf = sbt(es, "identf", [128, 128], F32)
        identb = sbt(es, "identb", [128, 128], BF16)
        onesf = sbt(es, "onesf", [128, 128], F32)
        onesb = sbt(es, "onesb", [128, 128], BF16)
        onesblk = sbt(es, "onesblk", [128, 128], BF16)
        Uf = sbt(es, "Uf", [128, 128], F32)
        maskI = sbt(es, "maskI", [128, 128], F32)
        maskS = sbt(es, "maskS", [128, 128], F32)
        causb = sbt(es, "causb", [128, 128], BF16)
        A("pool", lambda e: e.memset(identf[:], 0.0), w=[cst])
        A("pool", lambda e: e.affine_select(out=identf[:], in_=identf[:], pattern=[[-1, 128]],
                                            compare_op=ALU.not_equal, fill=1.0, base=0,
                                            channel_multiplier=1), r=[cst], w=[cst])
        A("pool", lambda e: e.tensor_copy(out=identb[:], in_=identf[:]), r=[cst], w=[cst])
        A("pool", lambda e: e.memset(onesf[:], 1.0), w=[cst])
        A("pool", lambda e: e.memset(onesb[:], 1.0), w=[cst])
        A("pool", lambda e: e.memset(onesblk[:], 0.0), w=[cst])
        A("pool", lambda e: e.memset(onesblk[0:64, 0:64], 1.0), r=[cst], w=[cst])
        A("pool", lambda e: e.memset(onesblk[64:128, 64:128], 1.0), r=[cst], w=[cst])
        A("pool", lambda e: e.affine_select(out=Uf[:], in_=onesf[:], pattern=[[1, 128]],
                                            compare_op=ALU.is_ge, fill=0.0, base=0,
                                            channel_multiplier=-1), r=[cst], w=[cst])
        A("pool", lambda e: e.tensor_copy(out=maskI[:], in_=Uf[:]), r=[cst], w=[cst])
        A("pool", lambda e: e.affine_select(out=maskS[:], in_=onesf[:], pattern=[[1, 128]],
                                            compare_op=ALU.is_gt, fill=0.0, base=0,
                                            channel_multiplier=-1), r=[cst], w=[cst])
        A("pool", lambda e: e.tensor_copy(out=causb[:], in_=Uf[:]), r=[cst], w=[cst])
        zerob = sbt(es, "zerob", [128, 256], BF16)
        A("pool", lambda e: e.memset(zerob[:], 0.0), w=[cst])
        negmask = sbt(es, "negmask", [128, 128], BF16)
        A("pool", lambda e: e.tensor_scalar(out=negmask[:], in0=Uf[:], scalar1=-1.0, scalar2=30000.0,
                                            op0=ALU.add, op1=ALU.mult), r=[cst], w=[cst])

        nrm_t = [sbt(es, f"nrmt{i}", [128, 8], F32) for i in range(3)]
        qn_t = sbt(es, "qn_t", [128, 2], F32)
        kvn_t = sbt(es, "kvn_t", [128, 1], F32)
        cw_t = sbt(es, "cw_t", [128, 48], F32)
        alog_t = sbt(es, "alog_t", [128, 8], F32)
        negA_t = sbt(es, "negA_t", [128, 8], F32)
        dtb_t = sbt(es, "dtb_t", [128, 8], F32)
        gnorm_t = sbt(es, "gnorm_t", [128, 64], F32)
        prm = S.buf("prm")
        for t_, d_ in [(nrm_t[0], nrm_d[0]), (nrm_t[1], nrm_d[1]), (nrm_t[2], nrm_d[2]), (qn_t, qn_d),
                       (kvn_t, kvn_d), (cw_t, cw_d), (alog_t, alog_d), (dtb_t, dtb_d), (gnorm_t, gnorm_d)]:
            DMA("sp", t_[:], d_, [], prm)
        A("act", lambda e: e.activation(out=negA_t[:], in_=alog_t[:], func=AF.Exp), r=[prm], w=[cst])
        A("dve", lambda e: e.tensor_scalar(out=negA_t[:], in0=negA_t[:], scalar1=-1.0, scalar2=None,
                                           op0=ALU.mult), r=[cst], w=[cst])

        wgc = [[S.buf(f"wgc{i}_{g}") for g in range(6)] for i in range(2)]
        wuc = [[S.buf(f"wuc{i}_{g}") for g in range(6)] for i in range(2)]
        wdc = [S.buf(f"wdc{i}") for i in range(2)]
        winc = S.buf("winc")

        def tiled_cast(dst_g, dcol, src, r0, nchunk, c0, ncol, b, late=False):
            for ca in range(0, nchunk, 4):
                cb = min(nchunk, ca + 4)
                d_ = dst_g[:, ca:cb, dcol:dcol + ncol]
                s__ = src[(r0 + ca) * 128:(r0 + cb) * 128, c0:c0 + ncol].rearrange("(c p) n -> p c n", p=128)
                if late:
                    late_casts.append((d_, s__, b))
                else:
                    DMA("pool", d_, s__, [], b)

        late_casts = []
        WIN_GROUPS = [[(0, 416, 0), (4528, 32, 416), (1952, 16, 448)]] + \
            [[(416 + g_ * 512, 512, 0)] for g_ in range(3)] + \
            [[(c_, 512, 0)] for c_ in [1968, 2480, 2992, 3504, 4016]]

        def cast_ffn(i, late=False):
            for g, (c0, ncol) in enumerate(FFN_COLGROUPS):
                tiled_cast(wg_b[i][g], 0, wg_f[i], 0, 8, c0, ncol, wgc[i][g], late)
                tiled_cast(wu_b[i][g], 0, wu_f[i], 0, 8, c0, ncol, wuc[i][g], late)
            for ri, (r0, nr) in enumerate(FFN_ROWGROUPS):
                for half in range(2):
                    tiled_cast(wd_b[i][ri * 2 + half], 0, wd_f[i], r0, nr, half * 512, 512, wdc[i], late)

        cast_ffn(0)
        for gi_, parts in enumerate(WIN_GROUPS):
            for (c0, ncol, dcol) in parts:
                tiled_cast(win_b[gi_], dcol, win_f, 0, 8, c0, ncol, winc)
        cast_ffn(1, late=True)

        def issue_late_casts(k):
            for _ in range(k):
                if late_casts:
                    d_, s__, b_ = late_casts.pop(0)
                    DMA("pool", d_, s__, [], b_)

        def norm_chain(st_r, xt, nblk=3):
            xns = []
            for t in range(nblk):
                junk = st_r["junk"]()
                ss = st_r["ss"]()
                A("dve", lambda e, ss=ss: e.memset(ss.t[:], 0.0), w=[ss.b])
                A("act", lambda e, t=t, junk=junk, ss=ss: e.activation(
                    out=junk.t[:], in_=xt.t[:, t, :], func=AF.Square, accum_out=ss.t[:, 0:1]),
                  r=[xt.b, ss.b], w=[junk.b, ss.b])
                A("act", lambda e, ss=ss: e.activation(out=ss.t[:, 1:2], in_=ss.t[:, 0:1], func=AF.Sqrt,
                                                       scale=1.0 / D, bias=EPS), r=[ss.b], w=[ss.b])
                A("dve", lambda e, ss=ss: e.reciprocal(out=ss.t[:, 2:3], in_=ss.t[:, 1:2]), r=[ss.b], w=[ss.b])
                xn = st_r["xn"]()
                A("dve", lambda e, t=t, xn=xn, ss=ss: e.tensor_scalar(
                    out=xn.t[:], in0=xt.t[:, t, :], scalar1=ss.t[:, 2:3], scalar2=None, op0=ALU.mult),
                  r=[xt.b, ss.b], w=[xn.b])
                xns.append(xn)
            return xns

        def norm_tr(st_r, xns, nrm):
            xnT = st_r["xnT"]()
            for t, xn in enumerate(xns):
                ps = psum()
                pv = ps.t[:].bitcast(BF16)
                for c in range(8):
                    tr(ps, pv[:, c * 128:(c + 1) * 128], xn.t[:, c * 128:(c + 1) * 128], identb[:], [xn.b, cst])
                A("dve", lambda e, t=t, pv=pv, xnT=xnT: e.tensor_tensor(
                    out=xnT.t[:, :, t * 128:(t + 1) * 128],
                    in0=pv.rearrange("p (c n) -> p c n", c=8),
                    in1=nrm[:, :].unsqueeze(2).broadcast_to([128, 8, 128]), op=ALU.mult),
                  r=[prm], w=[ps.b, xnT.b])
            return xnT

        def norm_T(st_r, xt, nrm, nblk=3):
            xnT = norm_tr(st_r, norm_chain(st_r, xt, nblk), nrm)
            return xnT, [xnT.b] * nblk

        def ffn_gu(st_r, xnT, i, nblk=3):
            ncol = nblk * 128
            xb = [xnT.b]
            hid = st_r["hid"]()
            hb = st_r["hb"]
            for g, (c0, ncg) in enumerate(FFN_COLGROUPS):
                sg = st_r["wslot"]()
                su = st_r["wslot"]()
                DMA("sp", sg.t[:, :, 0:ncg], wg_b[i][g, :, :, 0:ncg], [wgc[i][g]], sg.b)
                DMA("sp", su.t[:, :, 0:ncg], wu_b[i][g, :, :, 0:ncg], [wuc[i][g]], su.b)
                for f in range(ncg // 128):
                    fi = c0 // 128 + f
                    pg = psum()
                    for k in range(8):
                        mm(pg, pg.t[:, 0:ncol], sg.t[:, k, f * 128:(f + 1) * 128], xnT.t[:, k, 0:ncol],
                           k == 0, k == 7, [sg.b] + xb)
                    pu = psum()
                    for k in range(8):
                        mm(pu, pu.t[:, 0:ncol], su.t[:, k, f * 128:(f + 1) * 128], xnT.t[:, k, 0:ncol],
                           k == 0, k == 7, [su.b] + xb)
                    sil = st_r["sil"]()
                    A("act", lambda e, pg=pg, sil=sil: e.activation(out=sil.t[:, 0:ncol], in_=pg.t[:, 0:ncol],
                                                                     func=AF.Silu), w=[pg.b, sil.b])
                    A("dve", lambda e, pu=pu, sil=sil, fi=fi: e.tensor_tensor(
                        out=hid.t[:, fi, 0:ncol], in0=sil.t[:, 0:ncol], in1=pu.t[:, 0:ncol], op=ALU.mult),
                      r=[sil.b], w=[pu.b, hb[fi]])
            return hid

        def ffn_down(st_r, hid, xt, i, nblk=3):
            hb = st_r["hb"]
            accs = [[psum() for _ in range(2)] for _ in range(nblk)]
            for ri, (r0, nr) in enumerate(FFN_ROWGROUPS):
                for half in range(2):
                    sd = st_r["wslot"]()
                    DMA("sp", sd.t[:, 0:nr, :], wd_b[i][ri * 2 + half, :, 0:nr, :], [wdc[i]], sd.b)
                    for t in range(nblk):
                        for f in range(nr):
                            fi = r0 + f
                            mm(accs[t][half], accs[t][half].t[:, :], hid.t[:, fi, t * 128:(t + 1) * 128],
                               sd.t[:, f, :], fi == 0, fi == NF - 1, [sd.b, hb[fi]])
            for t in range(nblk):
                for half in range(2):
                    ps = accs[t][half]
                    A("dve", lambda e, ps=ps, t=t, half=half: e.scalar_tensor_tensor(
                        out=xt.t[:, t, half * 512:(half + 1) * 512], in0=ps.t[:, :], scalar=0.5,
                        in1=xt.t[:, t, half * 512:(half + 1) * 512], op0=ALU.mult, op1=ALU.add),
                      r=[], w=[ps.b, xt.b])

        if 1 in stages:
            with ExitStack() as st:
                R = {
                    "xnT": ring(st, "xnT", [128, 8, W], BF16, 3),
                    "junk": ring(st, "junk", [128, D], BF16, 2),
                    "ss": ring(st, "ss", [128, 4], F32, 4),
                    "xn": ring(st, "xn", [128, D], BF16, 6),
                    "hid": ring(st, "hid", [128, NF, W], BF16, 1),
                    "wslot": ring(st, "wslot", [128, 8, 512], BF16, 5),
                    "sil": ring(st, "sil", [128, W], F32, 2),
                    "hb": [S.buf(f"hid{f}") for f in range(NF)],
                }
                xtr = ring(st, "xt", [128, 3, D], F32, 2)
                cqf = tb(st, "cqf", [128, 2, W], F32)
                ckvf = tb(st, "ckvf", [128, 1, W], F32)
                sqb = ring(st, "sqb", [128, W], BF16, 6)
                rr = ring(st, "rr", [128, W], F32, 2)
                ob16 = ring(st, "ob16", [128, W], BF16, 8)
                tabc = ring(st, "tabc", [32, W], F32, 2)
                tabs = ring(st, "tabs", [32, W], F32, 2)
                kr1 = ring(st, "kr1", [32, W], F32, 2)
                kr2 = ring(st, "kr2", [32, W], F32, 2)
                smallr = ring(st, "small", [128, 32], F32, 4)
                pre = [tb(st, f"pre{c}", [128, W + 3], BF16) for c in range(12)]
                dcw = sbt(st, "dcw", [128, 48, 128], BF16)
                dcwb = [S.buf("dcw0"), S.buf("dcw1")]
                for c_ in range(48):
                    A("dve", lambda e, c_=c_: e.tensor_scalar(
                        out=dcw[:, c_, :], in0=identf[:], scalar1=cw_t[:, c_:c_ + 1], scalar2=None,
                        op0=ALU.mult), r=[prm, cst], w=[dcwb[c_ % 2]])
                yr = ring(st, "yy", [128, W], F32, 5)
                ktst = tb(st, "ktst", [128, 3, 512], BF16)
                vtst = tb(st, "vtst", [128, 3, 512], BF16)
                zst = ring(st, "zst", [128, 512], BF16, 2)
                gst = ring(st, "gst", [128, 2048], BF16, 3)
                for c in range(12):
                    A("pool", lambda e, c=c: e.memset(pre[c].t[:, 0:3], 0.0), w=[pre[c].b])

                h1b = S.buf("h1s")
                sc = {k: S.buf(k) for k in ["cqnT", "ckvnT", "kropeT", "gqT", "gkT", "gktok", "gvtok",
                                            "gbs", "ggs", "zs", "gates"]}

                def inproj(s_, uT):
                    t0 = s_ * W
                    ub = [uT.b] * 3
                    sA = R["wslot"]()
                    DMA("sp", sA.t[:, :, 0:464], win_b[0, :, :, 0:464], [winc], sA.b)

                    def lowrank(cols, nch, nrmw, dst, dstb, inv_n, cqf):
                        sq = []
                        for c in range(nch):
                            ps = psum()
                            for k in range(8):
                                mm(ps, ps.t[:, 0:W], sA.t[:, k, cols + c * 128:cols + (c + 1) * 128],
                                   uT.t[:, k, :], k == 0, k == 7, [sA.b] + ub)
                            q = sqb()
                            A("act", lambda e, ps=ps, q=q: e.activation(out=q.t[:], in_=ps.t[:, 0:W], func=AF.Square),
                              w=[ps.b, q.b])
                            A("dve", lambda e, ps=ps, c=c: e.tensor_copy(out=cqf.t[:, c, :], in_=ps.t[:, 0:W]),
                              w=[ps.b, cqf.b])
                            sq.append(q)
                        yield
                        ps2 = psum()
                        for c in range(nch):
                            mm(ps2, ps2.t[:, 0:W], onesb[:], sq[c].t[:], c == 0, c == nch - 1, [sq[c].b, cst])
                        r_ = rr()
                        A("act", lambda e, ps2=ps2, r_=r_: e.activation(out=r_.t[:], in_=ps2.t[:, 0:W], func=AF.Sqrt,
                                                                        scale=inv_n, bias=EPS), w=[ps2.b, r_.b])
                        A("dve", lambda e, r_=r_: e.reciprocal(out=r_.t[:], in_=r_.t[:]), r=[r_.b], w=[r_.b])
                        for c in range(nch):
                            o = ob16()
                            A("dve", lambda e, c=c, o=o, r_=r_: e.scalar_tensor_tensor(
                                out=o.t[:], in0=cqf.t[:, c, :], scalar=nrmw[:, c:c + 1], in1=r_.t[:],
                                op0=ALU.mult, op1=ALU.mult), r=[cqf.b, r_.b, prm], w=[o.b])
                            DMA("pool", dst[c * 128:(c + 1) * 128, t0:t0 + W], o.t[:], [o.b], dstb, own=o.b)

                    lr1 = lowrank(0, 2, qn_t, cqnT, sc["cqnT"], 1.0 / 256, cqf)
                    lr2 = lowrank(256, 1, kvn_t, ckvnT, sc["ckvnT"], 1.0 / 128, ckvf)
                    next(lr1)
                    next(lr2)

                    tc_ = tabc()
                    ts_ = tabs()
                    DMA("sp", tc_.t[:], cosT_d[:, t0:t0 + W], [], tc_.b)
                    DMA("sp", ts_.t[:], sinT_d[:, t0:t0 + W], [], ts_.b)
                    psa = psum()
                    for k in range(8):
                        mm(psa, psa.t[0:32, 0:W], sA.t[:, k, 384:416], uT.t[:, k, :], k == 0, k == 7, [sA.b] + ub)
                    psb = psum()
                    for k in range(8):
                        mm(psb, psb.t[0:32, 0:W], sA.t[:, k, 416:448], uT.t[:, k, :], k == 0, k == 7, [sA.b] + ub)
                    k1 = kr1()
                    k2 = kr2()
                    A("dve", lambda e, psa=psa, k1=k1, tc_=tc_: e.tensor_tensor(
                        out=k1.t[:], in0=psa.t[0:32, 0:W], in1=tc_.t[:], op=ALU.mult), r=[tc_.b], w=[psa.b, k1.b])
                    A("dve", lambda e, psb=psb, k2=k2, ts_=ts_: e.tensor_tensor(
                        out=k2.t[:], in0=psb.t[0:32, 0:W], in1=ts_.t[:], op=ALU.mult), r=[ts_.b], w=[psb.b, k2.b])
                    o = ob16()
                    A("dve", lambda e, k1=k1, k2=k2, o=o: e.tensor_tensor(
                        out=o.t[0:32, :], in0=k1.t[:], in1=k2.t[:], op=ALU.add), r=[k1.b, k2.b], w=[o.b])
                    DMA("pool", kropeT[:, t0:t0 + W], o.t[0:32, :], [o.b], sc["kropeT"], own=o.b)

                    for t in range(3):
                        ps = psum()
                        for k in range(8):
                            mm(ps, ps.t[:, 0:16], uT.t[:, k, t * 128:(t + 1) * 128], sA.t[:, k, 448:464],
                               k == 0, k == 7, [sA.b, ub[t]])
                        sm = smallr()
                        A("act", lambda e, ps=ps, sm=sm: e.activation(out=sm.t[:, 0:8], in_=ps.t[:, 0:8],
                                                                      func=AF.Sigmoid), w=[ps.b, sm.b])
                        A("dve", lambda e, ps=ps, sm=sm: e.tensor_tensor(out=sm.t[:, 8:16], in0=ps.t[:, 8:16],
                                                                         in1=dtb_t[:], op=ALU.add),
                          r=[prm, sm.b], w=[ps.b, sm.b])
                        A("act", lambda e, sm=sm: e.activation(out=sm.t[:, 8:16], in_=sm.t[:, 8:16], func=AF.Exp),
                          r=[sm.b], w=[sm.b])
                        A("act", lambda e, sm=sm: e.activation(out=sm.t[:, 8:16], in_=sm.t[:, 8:16], func=AF.Ln,
                                                               bias=1.0), r=[sm.b], w=[sm.b])
                        A("dve", lambda e, sm=sm: e.tensor_tensor(out=sm.t[:, 16:24], in0=sm.t[:, 8:16],
                                                                  in1=negA_t[:], op=ALU.mult),
                          r=[sm.b, cst], w=[sm.b])
                        DMA("pool", gbs[t0 + t * 128:t0 + (t + 1) * 128, :], sm.t[:, 0:8], [sm.b], sc["gbs"], own=sm.b)
                        DMA("pool", ggs[t0 + t * 128:t0 + (t + 1) * 128, :], sm.t[:, 16:24], [sm.b], sc["ggs"], own=sm.b)

                    for lr_ in (lr1, lr2):
                        try:
                            next(lr_)
                        except StopIteration:
                            pass
                    def qkv_chunk(grp, f, sl):
                        ci = grp * 4 + f
                        ps = psum()
                        for k in range(8):
                            mm(ps, ps.t[:, 0:W], sl.t[:, k, f * 128:(f + 1) * 128], uT.t[:, k, :],
                               k == 0, k == 7, [sl.b] + ub)
                        pc = pre[ci]
                        A("act", lambda e, ps=ps, pc=pc: e.activation(out=pc.t[:, 3:W + 3], in_=ps.t[:, 0:W],
                                                                      func=AF.Copy), w=[ps.b, pc.b])
                        yield
                        psc = psum()
                        for j in range(4):
                            mm(psc, psc.t[:, 0:W], dcw[:, ci * 4 + j, :], pc.t[:, j:j + W], j == 0, j == 3, [pc.b] + dcwb)
                        A("pool", lambda e, pc=pc: e.tensor_copy(out=pc.t[:, 0:3], in_=pc.t[:, W:W + 3]),
                          r=[pc.b], w=[pc.b])
                        y = yr()
                        A("act", lambda e, psc=psc, y=y: e.activation(out=y.t[:], in_=psc.t[:, 0:W], func=AF.Silu),
                          w=[psc.b, y.b])
                        if grp < 2:
                            q = sqb()
                            A("act", lambda e, y=y, q=q: e.activation(out=q.t[:], in_=y.t[:], func=AF.Square),
                              r=[y.b], w=[q.b])
                        yield
                        o = ob16()
                        if grp < 2:
                            ps2 = psum()
                            mm(ps2, ps2.t[:, 0:W], onesblk[:], q.t[:], True, True, [q.b, cst])
                            r_ = rr()
                            sc_, bi_ = (64.0, 64.0 * EPS) if grp == 0 else (1.0, EPS)
                            A("act", lambda e, ps2=ps2, r_=r_, sc_=sc_, bi_=bi_: e.activation(
                                out=r_.t[:], in_=ps2.t[:, 0:W], func=AF.Sqrt, scale=sc_, bias=bi_),
                              w=[ps2.b, r_.b])
                            A("dve", lambda e, r_=r_: e.reciprocal(out=r_.t[:], in_=r_.t[:]), r=[r_.b], w=[r_.b])
                            A("dve", lambda e, y=y, r_=r_, o=o: e.tensor_tensor(out=o.t[:], in0=y.t[:], in1=r_.t[:],
                                                                               op=ALU.mult),
                              r=[y.b, r_.b], w=[o.b])
                            dstT, dstb = (gqT, sc["gqT"]) if grp == 0 else (gkT, sc["gkT"])
                            DMA("pool", dstT[f * 128:(f + 1) * 128, t0:t0 + W], o.t[:], [o.b], dstb, own=o.b)
                        else:
                            A("pool", lambda e, y=y, o=o: e.tensor_copy(out=o.t[:], in_=y.t[:]), r=[y.b], w=[o.b])
                        yield
                        if grp >= 1:
                            stg = ktst if grp == 1 else vtst
                            ps3 = psum()
                            pv = ps3.t[:].bitcast(BF16)
                            for t in range(3):
                                tr(ps3, pv[:, t * 128:(t + 1) * 128], o.t[:, t * 128:(t + 1) * 128], identb[:],
                                   [o.b, cst])
                            A("act", lambda e, pv=pv, stg=stg, f=f: e.activation(
                                out=stg.t[:, :, f * 128:(f + 1) * 128],
                                in_=pv[:, 0:384].rearrange("p (t n) -> p t n", t=3), func=AF.Copy),
                              w=[ps3.b, stg.b])
                            if f == 3:
                                dd, db = (gktok, sc["gktok"]) if grp == 1 else (gvtok, sc["gvtok"])
                                DMA("pool", dd[t0:t0 + W, :].rearrange("(t p) d -> p t d", p=128), stg.t[:],
                                    [stg.b], db, own=stg.b)
                        yield

                    gens = []

                    def adv(g):
                        try:
                            next(g)
                        except StopIteration:
                            pass

                    def pump():
                        n_ = len(gens)
                        for back in (3, 5, 7):
                            if n_ >= back and n_ - back < 12 and gens[n_ - back] is not None:
                                adv(gens[n_ - back])

                    for grp in range(3):
                        sl = R["wslot"]()
                        c_lo = 416 + grp * 512
                        DMA("sp", sl.t[:], win_b[1 + grp], [winc], sl.b)
                        for f in range(4):
                            g_ = qkv_chunk(grp, f, sl)
                            gens.append(g_)
                            adv(g_)
                            pump()
                    qkv_tail = {"gens": gens, "k": 0}

                    def qkv_drain_step():
                        gens.append(None)
                        pump()

                    gts = [gst() for _ in range(3)]
                    ndrain = [0]
                    for gi, c_lo in enumerate([1968, 2480, 2992, 3504, 4016]):
                        sl = R["wslot"]()
                        DMA("sp", sl.t[:], win_b[4 + gi], [winc], sl.b)
                        for t in range(3):
                            ps = psum()
                            for k in range(8):
                                mm(ps, ps.t[:, :], uT.t[:, k, t * 128:(t + 1) * 128], sl.t[:, k, :],
                                   k == 0, k == 7, [sl.b, ub[t]])
                            if ndrain[0] < 6:
                                qkv_drain_step()
                                ndrain[0] += 1
                            if gi == 0:
                                z = zst()
                                A("act", lambda e, ps=ps, z=z: e.activation(out=z.t[:], in_=ps.t[:, :], func=AF.Silu),
                                  w=[ps.b, z.b])
                                DMA("pool", zs[t0 + t * 128:t0 + (t + 1) * 128, :], z.t[:], [z.b], sc["zs"], own=z.b)
                            else:
                                g_ = gts[t]
                                A("act", lambda e, ps=ps, g_=g_, gi=gi: e.activation(
                                    out=g_.t[:, (gi - 1) * 512:gi * 512], in_=ps.t[:, :], func=AF.Sigmoid),
                                  w=[ps.b, g_.b])
                                if gi == 4:
                                    DMA("pool", gates[t0 + t * 128:t0 + (t + 1) * 128, :], g_.t[:], [g_.b], sc["gates"], own=g_.b)

                def load_x(s_):
                    xt = xtr()
                    DMA("sp", xt.t[:], xin[s_ * W:(s_ + 1) * W, :].rearrange("(t p) d -> p t d", p=128), [], xt.b)
                    return xt

                xts = {0: load_x(0)}
                xnTs = {0: norm_tr(R, norm_chain(R, xts[0]), nrm_t[0])}
                dxn = None
                for s_ in range(NST):
                    if s_ >= 1:
                        t0p = (s_ - 1) * W
                        xp = xts[s_ - 1]
                        DMA("pool", h1s[t0p:t0p + W, :].rearrange("(t p) d -> p t d", p=128), xp.t[:], [xp.b], h1b,
                            own=xp.b)
                        dxn = norm_chain(R, xp)
                    hid = ffn_gu(R, xnTs[s_], 0)
                    if s_ + 1 < NST:
                        xts[s_ + 1] = load_x(s_ + 1)
                    axn = norm_chain(R, xts[s_ + 1]) if s_ + 1 < NST else None
                    if s_ >= 1:
                        uT = norm_tr(R, dxn, nrm_t[1])
                        inproj(s_ - 1, uT)
                    issue_late_casts(5)
                    if axn is not None:
                        xnTs[s_ + 1] = norm_tr(R, axn, nrm_t[0])
                    ffn_down(R, hid, xts[s_], 0)
                xp = xts[NST - 1]
                t0p = (NST - 1) * W
                DMA("pool", h1s[t0p:t0p + W, :].rearrange("(t p) d -> p t d", p=128), xp.t[:], [xp.b], h1b, own=xp.b)
                uT = norm_tr(R, norm_chain(R, xp), nrm_t[1])
                inproj(NST - 1, uT)
                issue_late_casts(1000)
                S.barrier()
                S.emit()

        if 2 in stages:
            with ExitStack() as st:
                mst = {"i": 0}
                gst_ = {"i": 0}

                def psum_m():
                    r = banks[mst["i"] % 3]
                    mst["i"] += 1
                    return r

                def psum_g():
                    r = banks[3 + gst_["i"] % 3]
                    gst_["i"] += 1
                    return r

                gsl = []
                for i in range(2):
                    gsl.append({
                        "qT": sbt(st, f"g_qT{i}", [64, 8, 128], BF16),
                        "kT": sbt(st, f"g_kT{i}", [64, 8, 128], BF16),
                        "kt": sbt(st, f"g_kt{i}", [128, 512], BF16),
                        "vt": sbt(st, f"g_vt{i}", [128, 512], BF16),
                        "zt": sbt(st, f"g_zt{i}", [128, 512], BF16),
                        "bt": sbt(st, f"g_bt{i}", [128, 8], F32),
                        "gt": sbt(st, f"g_gt{i}", [128, 8], F32),
                        "b": S.buf(f"gslot{i}"),
                    })
                GT = F32 if GDN_F32 else BF16
                gclr = ring(st, "gcl", [128, 48], F32, 2)
                kdr = ring(st, "kdec", [128, 512], BF16, 2)
                gonr = ring(st, "gon", [128, 8, 128], F32, 2)
                xgr = ring(st, "xg", [128, 4, 128], F32, 1)
                etsr = ring(st, "ets", [128, 4, 128], F32, 1)
                etir = ring(st, "eti", [128, 4, 128], F32, 1)
                GL = BF16
                mtr = ring(st, "mtp", [128, 8, 128], GL, 4)
                nnr = ring(st, "nnp", [128, 8, 128], GL, 4)
                mt32r = ring(st, "mt32", [128, 8, 128], F32, 2)
                n0fr = ring(st, "n0f", [128, 8, 128], F32, 1)
                g32r = ring(st, "g32", [128, 8, 128], F32, 1)
                x0tr = ring(st, "x0t", [128, 8, 128], F32, 1)
                z32r = ring(st, "z32", [128, 8, 128], F32, 2)
                zbr = ring(st, "zb", [128, 8, 128], GL, 4)
                tfr = ring(st, "tfin", [128, 8, 128], GT, 2)
                itr = ring(st, "intraT", [128, 8, 128], BF16, 2)
                rtr = ring(st, "rt", [128, 8, 64], F32, 1)
                rbr = ring(st, "Rb", [128, 8, 64], GT, 1)
                vnr = ring(st, "vnew", [128, 8, 64], BF16, 1)
                o32r = ring(st, "o32", [128, 8, 64], F32, 2)
                sqr_ = ring(st, "osq", [128, 8, 64], F32, 1)
                ssr = ring(st, "oss", [128, 16], F32, 2)
                obr = ring(st, "gob", [128, 512], BF16, 2)
                S32 = tb(st, "S32", [64, 8, 64], F32)
                Sb = tb(st, "Sb", [64, 8, 64], BF16)
                A("pool", lambda e: e.memset(S32.t[:], 0.0), w=[S32.b])
                A("pool", lambda e: e.memset(Sb.t[:], 0.0), w=[Sb.b])
                ogb = S.buf("ogdn")

                def bc3(ap2, n):
                    k = ap2.shape[1]
                    p = ap2.shape[0]
                    return ap2.unsqueeze(2).broadcast_to([p, k, n])

                def v4(ps):
                    return ps.t[:].rearrange("p (h n) -> p h n", h=4)

                def fr(ap):
                    return ap.bitcast(F32R) if (GDN_F32 and GDN_F32R) else ap

                def v8(ps):
                    return ps.t[:].rearrange("p (h d) -> p h d", h=8)

                def gdn_gen(par):
                    for n in range(par, NB if nb_lim is None else nb_lim, 2):
                        z32 = z32r()
                        r0 = n * 128
                        sl = gsl[n % 2]
                        sb_ = sl["b"]
                        DMA("sp", sl["qT"][:], gqT[:, r0:r0 + 128].rearrange("(h d) t -> d h t", d=64), [], sb_)
                        DMA("sp", sl["kT"][:], gkT[:, r0:r0 + 128].rearrange("(h d) t -> d h t", d=64), [], sb_)
                        DMA("sp", sl["kt"][:], gktok[r0:r0 + 128, :], [], sb_)
                        DMA("sp", sl["vt"][:], gvtok[r0:r0 + 128, :], [], sb_)
                        DMA("sp", sl["zt"][:], zs[r0:r0 + 128, :], [], sb_)
                        DMA("sp", sl["bt"][:], gbs[r0:r0 + 128, :], [], sb_)
                        DMA("sp", sl["gt"][:], ggs[r0:r0 + 128, :], [], sb_)
                        qT, kT, kt, vt, zt, bt, gt = (sl[k] for k in ["qT", "kT", "kt", "vt", "zt", "bt", "gt"])

                        psg = psum_g()
                        mm(psg, psg.t[:, 0:8], Uf[:], gt[:], True, True, [sb_, cst])
                        mm(psg, psg.t[:, 8:16], onesf[:], gt[:], True, True, [sb_, cst])
                        gcl = gclr()
                        A("dve", lambda e, psg=psg, gcl=gcl: e.tensor_copy(out=gcl.t[:, 0:16], in_=psg.t[:, 0:16]),
                          w=[psg.b, gcl.b])
                        A("act", lambda e, gcl=gcl: e.activation(out=gcl.t[:, 16:24], in_=gcl.t[:, 0:8], func=AF.Exp),
                          r=[gcl.b], w=[gcl.b])
                        A("dve", lambda e, gcl=gcl: e.tensor_tensor(out=gcl.t[:, 24:32], in0=gcl.t[:, 8:16],
                                                                    in1=gcl.t[:, 0:8], op=ALU.subtract),
                          r=[gcl.b], w=[gcl.b])
                        A("act", lambda e, gcl=gcl: e.activation(out=gcl.t[:, 24:32], in_=gcl.t[:, 24:32],
                                                                 func=AF.Exp), r=[gcl.b], w=[gcl.b])
                        A("act", lambda e, gcl=gcl: e.activation(out=gcl.t[:, 32:40], in_=gcl.t[:, 8:16], func=AF.Exp),
                          r=[gcl.b], w=[gcl.b])
                        kd = kdr()
                        A("pool", lambda e, kd=kd, kt=kt, gcl=gcl: e.tensor_tensor(
                            out=kd.t[:].rearrange("p (h d) -> p h d", h=8),
                            in0=kt[:].rearrange("p (h d) -> p h d", h=8),
                            in1=bc3(gcl.t[:, 24:32], 64), op=ALU.mult), r=[sb_, gcl.b], w=[kd.b])
                        yield

                        mt32 = mt32r()
                        mt = mtr()
                        it_ = itr()
                        gon = gonr()
                        A("dve", lambda e, gon=gon, gt=gt: e.tensor_tensor(
                            out=gon.t[:], in0=onesf[:, :].unsqueeze(1).broadcast_to([128, 8, 128]),
                            in1=bc3(gt[:, 0:8], 128), op=ALU.mult), r=[sb_, cst], w=[gon.b])
                        for g in range(2):
                            psb = psum_g()
                            for hh in range(4):
                                h = 4 * g + hh
                                mm(psb, psb.t[:, hh * 128:(hh + 1) * 128], gon.t[:, h, :], Uf[:], True, True,
                                   [gon.b, cst])
                            xg = xgr()
                            A("dve", lambda e, psb=psb, xg=xg, gcl=gcl, g=g: e.tensor_tensor(
                                out=xg.t[:], in0=v4(psb), in1=bc3(gcl.t[:, 4 * g:4 * g + 4], 128), op=ALU.subtract),
                              r=[gcl.b], w=[psb.b, xg.b])
                            A("dve", lambda e, xg=xg: e.tensor_tensor(
                                out=xg.t[:], in0=xg.t[:], in1=maskI[:, :].unsqueeze(1).broadcast_to([128, 4, 128]),
                                op=ALU.mult), r=[xg.b, cst], w=[xg.b])
                            yield
                            A("act", lambda e, xg=xg: e.activation(out=xg.t[:], in_=xg.t[:], func=AF.Exp),
                              r=[xg.b], w=[xg.b])
                            ets = etsr()
                            eti = etir()
                            A("dve", lambda e, xg=xg, ets=ets: e.tensor_tensor(
                                out=ets.t[:], in0=xg.t[:], in1=maskS[:, :].unsqueeze(1).broadcast_to([128, 4, 128]),
                                op=ALU.mult), r=[xg.b, cst], w=[ets.b])
                            A("pool", lambda e, xg=xg, eti=eti: e.tensor_tensor(
                                out=eti.t[:], in0=xg.t[:], in1=maskI[:, :].unsqueeze(1).broadcast_to([128, 4, 128]),
                                op=ALU.mult), r=[xg.b, cst], w=[eti.b])
                            A("pool", lambda e, ets=ets, bt=bt, g=g: e.tensor_tensor(
                                out=ets.t[:], in0=ets.t[:], in1=bc3(bt[:, 4 * g:4 * g + 4], 128), op=ALU.mult),
                              r=[ets.b, sb_], w=[ets.b])
                            yield
                            psk = psum_g()
                            psq = psum_g()
                            for hh in range(4):
                                h = 4 * g + hh
                                mm(psk, psk.t[:, hh * 128:(hh + 1) * 128], kT[:, h, :], kT[:, h, :], True, True, [sb_])
                                mm(psq, psq.t[:, hh * 128:(hh + 1) * 128], kT[:, h, :], qT[:, h, :], True, True, [sb_])
                            A("dve", lambda e, psk=psk, ets=ets, mt32=mt32, g=g: e.tensor_tensor(
                                out=mt32.t[:, 4 * g:4 * g + 4, :], in0=v4(psk), in1=ets.t[:], op=ALU.mult),
                              r=[ets.b], w=[psk.b, mt32.b])
                            A("dve", lambda e, mt32=mt32, mt=mt, g=g: e.tensor_copy(
                                out=mt.t[:, 4 * g:4 * g + 4, :], in_=mt32.t[:, 4 * g:4 * g + 4, :]),
                              r=[mt32.b], w=[mt.b])
                            A("dve", lambda e, psq=psq, eti=eti, it_=it_, g=g: e.tensor_tensor(
                                out=it_.t[:, 4 * g:4 * g + 4, :], in0=v4(psq), in1=eti.t[:], op=ALU.mult),
                              r=[eti.b], w=[psq.b, it_.b])
                            yield

                        nn = nnr()
                        pst = psum_g()
                        pvb = pst.t[:].bitcast(BF16)
                        for h in range(8):
                            tr(pst, pvb[:, h * 128:(h + 1) * 128], mt.t[:, h, :], identb[:], [mt.b, cst])
                        A("dve", lambda e, pvb=pvb, nn=nn, pst=pst: e.tensor_copy(
                            out=nn.t[:], in_=pvb.rearrange("p (h n) -> p h n", h=8)), w=[pst.b, nn.b])
                        zb = zbr()
                        A("dve", lambda e, zb=zb, mt32=mt32: e.tensor_tensor(
                            out=zb.t[:], in0=identf[:, :].unsqueeze(1).broadcast_to([128, 8, 128]), in1=mt32.t[:],
                            op=ALU.subtract), r=[mt32.b, cst], w=[zb.b])
                        yield
                        mtp, np_ = mt, nn
                        for l in range(1, 7):
                            n2 = nnr()
                            psn = [psum_g(), psum_g()]
                            for h in range(8):
                                mm(psn[h // 4], psn[h // 4].t[:, (h % 4) * 128:(h % 4 + 1) * 128], fr(mtp.t[:, h, :]),
                                   fr(np_.t[:, h, :]), True, True, [mtp.b, np_.b])
                            for g in range(2):
                                A("dve", lambda e, g=g, psn=psn, n2=n2: e.tensor_copy(
                                    out=n2.t[:, 4 * g:4 * g + 4, :], in_=v4(psn[g])),
                                  w=[psn[g].b, n2.b])
                            if l <= 5:
                                m2 = mtr()
                                psm = [psum_g(), psum_g()]
                                for h in range(8):
                                    mm(psm[h // 4], psm[h // 4].t[:, (h % 4) * 128:(h % 4 + 1) * 128], fr(np_.t[:, h, :]),
                                       fr(mtp.t[:, h, :]), True, True, [mtp.b, np_.b])
                                for g in range(2):
                                    A("dve", lambda e, g=g, psm=psm, m2=m2: e.tensor_copy(
                                        out=m2.t[:, 4 * g:4 * g + 4, :], in_=v4(psm[g])), w=[psm[g].b, m2.b])
                            else:
                                m2 = None
                            yield
                            psz = [psum_g(), psum_g()]
                            for h in range(8):
                                mm(psz[h // 4], psz[h // 4].t[:, (h % 4) * 128:(h % 4 + 1) * 128], fr(n2.t[:, h, :]),
                                   fr(zb.t[:, h, :]), True, True, [n2.b, zb.b])
                            zbn = zbr()
                            for g in range(2):
                                A("dve", lambda e, zb=zb, zbn=zbn, g=g, psz=psz: e.tensor_tensor(
                                    out=zbn.t[:, 4 * g:4 * g + 4, :], in0=zb.t[:, 4 * g:4 * g + 4, :],
                                    in1=v4(psz[g]), op=ALU.add), r=[zb.b], w=[psz[g].b, zbn.b])
                            zb = zbn
                            if l == 6:
                                A("act", lambda e, z32=z32, zb=zb: e.activation(out=z32.t[:], in_=zb.t[:],
                                                                                func=AF.Copy),
                                  r=[zb.b], w=[z32.b])
                            mtp, np_ = m2, n2
                            yield
                        n0f = n0fr()
                        for g in range(2):
                            pst = psum_g()
                            for hh in range(4):
                                tr(pst, pst.t[:, hh * 128:(hh + 1) * 128], mt32.t[:, 4 * g + hh, :], identf[:],
                                   [mt32.b, cst])
                            A("dve", lambda e, pst=pst, n0f=n0f, g=g: e.tensor_copy(
                                out=n0f.t[:, 4 * g:4 * g + 4, :], in_=v4(pst)), w=[pst.b, n0f.b])
                        g32 = g32r()
                        A("pool", lambda e, z32=z32, g32=g32: e.tensor_tensor(
                            out=g32.t[:], in0=identf[:, :].unsqueeze(1).broadcast_to([128, 8, 128]), in1=z32.t[:],
                            op=ALU.subtract), r=[z32.b, cst], w=[g32.b])
                        yield
                        pse = [psum_g(), psum_g()]
                        for h in range(8):
                            mm(pse[h // 4], pse[h // 4].t[:, (h % 4) * 128:(h % 4 + 1) * 128], n0f.t[:, h, :],
                               z32.t[:, h, :], True, True, [n0f.b, z32.b])
                        for g in range(2):
                            A("dve", lambda e, g=g, pse=pse, g32=g32: e.tensor_tensor(
                                out=g32.t[:, 4 * g:4 * g + 4, :], in0=g32.t[:, 4 * g:4 * g + 4, :], in1=v4(pse[g]),
                                op=ALU.subtract), r=[g32.b], w=[pse[g].b, g32.b])
                        x0t = x0tr()
                        for g in range(2):
                            pst = psum_g()
                            for hh in range(4):
                                tr(pst, pst.t[:, hh * 128:(hh + 1) * 128], z32.t[:, 4 * g + hh, :], identf[:],
                                   [z32.b, cst])
                            A("dve", lambda e, pst=pst, x0t=x0t, g=g: e.tensor_copy(
                                out=x0t.t[:, 4 * g:4 * g + 4, :], in_=v4(pst)), w=[pst.b, x0t.b])
                        yield
                        tf = tfr()
                        psx = [psum_g(), psum_g()]
                        for h in range(8):
                            mm(psx[h // 4], psx[h // 4].t[:, (h % 4) * 128:(h % 4 + 1) * 128], x0t.t[:, h, :],
                               g32.t[:, h, :], True, True, [x0t.b, g32.b])
                        for g in range(2):
                            A("dve", lambda e, z32=z32, g=g, psx=psx, tf=tf: e.tensor_tensor(
                                out=tf.t[:, 4 * g:4 * g + 4, :], in0=z32.t[:, 4 * g:4 * g + 4, :], in1=v4(psx[g]),
                                op=ALU.add), r=[z32.b], w=[psx[g].b, tf.b])
                        yield

                        psks = psum_g()
                        for h in range(8):
                            mm(psks, psks.t[:, h * 64:(h + 1) * 64], kT[:, h, :], Sb.t[:, h, :], True, True,
                               [sb_, Sb.b])
                        rt = rtr()
                        A("dve", lambda e, psks=psks, rt=rt, gcl=gcl: e.tensor_tensor(
                            out=rt.t[:], in0=v8(psks), in1=bc3(gcl.t[:, 16:24], 64), op=ALU.mult),
                          r=[gcl.b], w=[psks.b, rt.b])
                        psqs = psum_g()
                        for h in range(8):
                            mm(psqs, psqs.t[:, h * 64:(h + 1) * 64], qT[:, h, :], Sb.t[:, h, :], True, True,
                               [sb_, Sb.b])
                        o32 = o32r()
                        A("dve", lambda e, psqs=psqs, o32=o32, gcl=gcl: e.tensor_tensor(
                            out=o32.t[:], in0=v8(psqs), in1=bc3(gcl.t[:, 16:24], 64), op=ALU.mult),
                          r=[gcl.b], w=[psqs.b, o32.b])
                        Rb = rbr()
                        A("dve", lambda e, rt=rt, Rb=Rb, vt=vt: e.tensor_tensor(
                            out=Rb.t[:], in0=vt[:].rearrange("p (h d) -> p h d", h=8), in1=rt.t[:], op=ALU.subtract),
                          r=[rt.b, sb_], w=[Rb.b])
                        yield
                        pstr = psum_g()
                        for h in range(8):
                            mm(pstr, pstr.t[:, h * 64:(h + 1) * 64], fr(tf.t[:, h, :]), fr(Rb.t[:, h, :]), True, True,
                               [tf.b, Rb.b])
                        vn = vnr()
                        A("dve", lambda e, pstr=pstr, vn=vn, bt=bt: e.tensor_tensor(
                            out=vn.t[:], in0=v8(pstr), in1=bc3(bt[:, 0:8], 64), op=ALU.mult),
                          r=[sb_], w=[pstr.b, vn.b])
                        yield
                        psd = psum_g()
                        for h in range(8):
                            mm(psd, psd.t[0:64, h * 64:(h + 1) * 64], kd.t[:, h * 64:(h + 1) * 64], vn.t[:, h, :],
                               True, True, [kd.b, vn.b])
                        A("dve", lambda e, gcl=gcl: e.tensor_tensor(
                            out=S32.t[:], in0=S32.t[:], in1=gcl.t[0:64, 32:40].unsqueeze(2).broadcast_to([64, 8, 64]),
                            op=ALU.mult), r=[gcl.b, S32.b], w=[S32.b])
                        A("dve", lambda e, psd=psd: e.tensor_tensor(
                            out=S32.t[:], in0=S32.t[:], in1=psd.t[0:64, :].rearrange("p (h d) -> p h d", h=8),
                            op=ALU.add), r=[S32.b], w=[psd.b, S32.b])
                        A("dve", lambda e: e.tensor_copy(out=Sb.t[:], in_=S32.t[:]), r=[S32.b], w=[Sb.b])
                        pso = psum_g()
                        for h in range(8):
                            mm(pso, pso.t[:, h * 64:(h + 1) * 64], it_.t[:, h, :], vn.t[:, h, :], True, True,
                               [it_.b, vn.b])
                        A("dve", lambda e, pso=pso, o32=o32: e.tensor_tensor(
                            out=o32.t[:], in0=o32.t[:], in1=v8(pso), op=ALU.add), r=[o32.b], w=[pso.b, o32.b])
                        yield
                        sq = sqr_()
                        A("pool", lambda e, o32=o32, sq=sq: e.tensor_tensor(out=sq.t[:], in0=o32.t[:], in1=o32.t[:],
                                                                            op=ALU.mult), r=[o32.b], w=[sq.b])
                        ss = ssr()
                        A("dve", lambda e, sq=sq, ss=ss: e.tensor_reduce(out=ss.t[:, 0:8], in_=sq.t[:], axis=AX.X,
                                                                         op=ALU.add), r=[sq.b], w=[ss.b])
                        A("act", lambda e, ss=ss: e.activation(out=ss.t[:, 8:16], in_=ss.t[:, 0:8], func=AF.Sqrt,
                                                               scale=1.0 / 64, bias=EPS), r=[ss.b], w=[ss.b])
                        A("dve", lambda e, ss=ss: e.reciprocal(out=ss.t[:, 8:16], in_=ss.t[:, 8:16]),
                          r=[ss.b], w=[ss.b])
                        A("dve", lambda e, o32=o32, ss=ss: e.tensor_tensor(
                            out=o32.t[:], in0=o32.t[:], in1=bc3(ss.t[:, 8:16], 64), op=ALU.mult),
                          r=[ss.b, o32.b], w=[o32.b])
                        A("pool", lambda e, o32=o32: e.tensor_tensor(
                            out=o32.t[:], in0=o32.t[:], in1=gnorm_t[:, :].unsqueeze(1).broadcast_to([128, 8, 64]),
                            op=ALU.mult), r=[o32.b, prm], w=[o32.b])
                        ob = obr()
                        A("pool", lambda e, o32=o32, ob=ob, zt=zt: e.tensor_tensor(
                            out=ob.t[:].rearrange("p (h d) -> p h d", h=8), in0=o32.t[:],
                            in1=zt[:].rearrange("p (h d) -> p h d", h=8), op=ALU.mult), r=[o32.b, sb_], w=[ob.b])
                        DMA("pool", ogdn[r0:r0 + 128, :], ob.t[:], [ob.b], ogb, own=ob.b)
                        yield

                ggs_ = [gdn_gen(0), gdn_gen(1)]
                gstate = {"done": False, "d": [False, False], "i": 0}

                def gdn_step(k=2):
                    for _ in range(k):
                        if gstate["done"]:
                            return
                        i = gstate["i"]
                        gstate["i"] += 1
                        w = 0 if i < 11 else (i - 11 + 1) % 2
                        if gstate["d"][w]:
                            w = 1 - w
                        try:
                            next(ggs_[w])
                        except StopIteration:
                            gstate["d"][w] = True
                            gstate["done"] = all(gstate["d"])

                Kt = [tb(st, f"Kh{h}", [96, LP], BF16) for h in range(4)]
                Va = tb(st, "Vaug", [128, NB, 4, 65], BF16)
                ckv = tb(st, "ckv", [128, LP], BF16)
                wuk_t = sbt(st, "wuk_t", [128, 512], BF16)
                wuv_t = sbt(st, "wuv_t", [128, 512], BF16)
                wuq_t = sbt(st, "wuq_t", [128, 2, 768], BF16)
                wuqs_t = sbt(st, "wuqs_t", [128, 2, 768], BF16)
                wb = S.buf("mlaw")
                DMA("pool", wuk_t[:], wuk_f, [], wb)
                DMA("pool", wuv_t[:], wuv_f, [], wb)
                DMA("pool", wuq_t[:], wuq_f.rearrange("(c p) n -> p c n", p=128), [], wb)
                DMA("pool", wuqs_t[:], wuqs_f.rearrange("(c p) n -> p c n", p=128), [], wb)
                for s_ in range(3):
                    DMA("sp", ckv.t[:, s_ * 1408:(s_ + 1) * 1408], ckvnT[:, s_ * 1408:(s_ + 1) * 1408], [], ckv.b)
                A("pool", lambda e: e.memset(Va.t[:, :, :, 64:65], 1.0), w=[Va.b])
                cqr = ring(st, "cqs", [128, 2, W], BF16, 2)
                tcr = ring(st, "mtabc", [96, W], F32, 2)
                tsr = ring(st, "mtabs", [96, W], F32, 2)
                qtr = ring(st, "QT", [96, W], BF16, 3)
                t1r = ring(st, "qt1", [96, W], F32, 2)
                t2r = ring(st, "qt2", [96, W], F32, 2)
                ptr = ring(st, "PT", [128, W], BF16, 5)
                rcr = ring(st, "rc", [128, 4], F32, 2)
                omr = ring(st, "omst", [128, 3, 256], BF16, 2)
                omb = S.buf("omla")
                sm_scale = float(96 ** -0.5)
                do_mla = 3 in stages
                kbc = 0
                for half in range(2 if do_mla else 0):
                    ei = 0
                    for hl in range(4):
                        h = half * 4 + hl
                        DMA("sp", Kt[hl].t[64:96, :], kropeT[:, :], [], Kt[hl].b)
                        for s_ in range(NST):
                            ps = psum_m()
                            mm(ps, ps.t[0:64, 0:W], wuk_t[:, h * 64:(h + 1) * 64], ckv.t[:, s_ * W:(s_ + 1) * W],
                               True, True, [wb, ckv.b])
                            if ei % 2 == 0:
                                A("act", lambda e, ps=ps, hl=hl, s_=s_: e.activation(
                                    out=Kt[hl].t[0:64, s_ * W:(s_ + 1) * W], in_=ps.t[0:64, 0:W], func=AF.Copy),
                                  w=[ps.b, Kt[hl].b])
                            else:
                                A("dve", lambda e, ps=ps, hl=hl, s_=s_: e.tensor_copy(
                                    out=Kt[hl].t[0:64, s_ * W:(s_ + 1) * W], in_=ps.t[0:64, 0:W]),
                                  w=[ps.b, Kt[hl].b])
                            ei += 1
                            if ei % 4 == 0:
                                gdn_step()
                    for blk in range(NB):
                        ps = psum_m()
                        mm(ps, ps.t[:, 0:256], ckv.t[:, blk * 128:(blk + 1) * 128],
                           wuv_t[:, half * 256:(half + 1) * 256], True, True, [wb, ckv.b])
                        if blk % 2 == 0:
                            A("act", lambda e, ps=ps, blk=blk: e.activation(
                                out=Va.t[:, blk, :, 0:64], in_=ps.t[:, 0:256].rearrange("p (h d) -> p h d", h=4),
                                func=AF.Copy), w=[ps.b, Va.b])
                        else:
                            A("dve", lambda e, ps=ps, blk=blk: e.tensor_copy(
                                out=Va.t[:, blk, :, 0:64], in_=ps.t[:, 0:256].rearrange("p (h d) -> p h d", h=4)),
                              w=[ps.b, Va.b])
                        if blk % 4 == 3:
                            gdn_step()
                    for s_ in range(NST if nst_lim is None else nst_lim):
                        t0 = s_ * W
                        cq = cqr()
                        DMA("sp", cq.t[:], cqnT[:, t0:t0 + W].rearrange("(c p) t -> p c t", p=128), [], cq.b)
                        tc_ = tcr()
                        ts_ = tsr()
                        DMA("sp", tc_.t[64:96, :], cosT_d[:, t0:t0 + W], [], tc_.b)
                        DMA("sp", ts_.t[64:96, :], sinT_d[:, t0:t0 + W], [], ts_.b)
                        om = omr()

                        def prep_q(hl, cq=cq, tc_=tc_, ts_=ts_, half=half):
                            h = half * 4 + hl
                            psq = banks[7]
                            for c in range(2):
                                mm(psq, psq.t[0:96, 0:W], wuq_t[:, c, h * 96:(h + 1) * 96], cq.t[:, c, :], c == 0,
                                   c == 1, [wb, cq.b])
                            QT = qtr()
                            A("act", lambda e, psq=psq, QT=QT: e.activation(out=QT.t[0:64, :], in_=psq.t[0:64, 0:W],
                                                                            func=AF.Copy), w=[psq.b, QT.b])
                            t1 = t1r()
                            t2 = t2r()
                            A("dve", lambda e, psq=psq, t1=t1, tc_=tc_: e.tensor_tensor(
                                out=t1.t[64:96, :], in0=psq.t[64:96, 0:W], in1=tc_.t[64:96, :], op=ALU.mult),
                              r=[tc_.b], w=[psq.b, t1.b])
                            psq2 = banks[7]
                            for c in range(2):
                                mm(psq2, psq2.t[0:96, 0:W], wuqs_t[:, c, h * 96:(h + 1) * 96], cq.t[:, c, :], c == 0,
                                   c == 1, [wb, cq.b])
                            A("dve", lambda e, psq2=psq2, t2=t2, ts_=ts_: e.tensor_tensor(
                                out=t2.t[64:96, :], in0=psq2.t[64:96, 0:W], in1=ts_.t[64:96, :], op=ALU.mult),
                              r=[ts_.b], w=[psq2.b, t2.b])
                            A("dve", lambda e, t1=t1, t2=t2, QT=QT: e.tensor_tensor(
                                out=QT.t[64:96, :], in0=t1.t[64:96, :], in1=t2.t[64:96, :], op=ALU.add),
                              r=[t1.b, t2.b, QT.b], w=[QT.b])
                            return QT

                        QTn = prep_q(0)
                        for hl in range(4):
                            h = half * 4 + hl
                            QT = QTn
                            pso = banks[6]
                            mm(pso, pso.t[:, 0:195], zerob[:, 0:128], zerob[:, 0:195], True, True, [cst])
                            nkb = 3 * s_ + 3

                            def issue_pss(kb, hl=hl, QT=QT, s_=s_):
                                i0 = max(0, kb - 3 * s_)
                                ncols = (3 - i0) * 128
                                pss = psum_m()
                                diag = kb >= 3 * s_
                                mm(pss, pss.t[:, 0:ncols], Kt[hl].t[0:96, kb * 128:(kb + 1) * 128],
                                   QT.t[0:96, i0 * 128:W], True, not diag, [Kt[hl].b, QT.b])
                                if diag:
                                    mm(pss, pss.t[:, 0:128], identb[:], negmask[:], False, True, [cst])
                                return pss, i0, ncols

                            pend = [issue_pss(0)]
                            if nkb > 1:
                                pend.append(issue_pss(1))
                            for kb in range(nkb):
                                pss, i0, ncols = pend.pop(0)
                                if kb + 2 < nkb:
                                    pend.append(issue_pss(kb + 2))
                                if kb == min(2, nkb - 1) and hl + 1 < 4:
                                    QTn = prep_q(hl + 1)
                                PT = ptr()
                                A("act", lambda e, pss=pss, PT=PT, i0=i0, ncols=ncols: e.activation(
                                    out=PT.t[:, i0 * 128:W], in_=pss.t[:, 0:ncols], func=AF.Exp, scale=sm_scale),
                                  w=[pss.b, PT.b])
                                for i in range(i0, 3):
                                    mm(pso, pso.t[:, i * 65:(i + 1) * 65], PT.t[:, i * 128:(i + 1) * 128],
                                       Va.t[:, kb, hl, :], False, False, [PT.b, Va.b], skip=True)
                                kbc += 0.56
                                while kbc >= 1.0:
                                    gdn_step(1)
                                    kbc -= 1.0
                            rc = rcr()
                            A("dve", lambda e, pso=pso, rc=rc: e.reciprocal(
                                out=rc.t[:, 0:3], in_=pso.t[:, 0:195].rearrange("p (i d) -> p i d", i=3)[:, :, 64]),
                              w=[pso.b, rc.b])
                            A("dve", lambda e, pso=pso, rc=rc, om=om, hl=hl: e.tensor_tensor(
                                out=om.t[:, :, hl * 64:(hl + 1) * 64],
                                in0=pso.t[:, 0:195].rearrange("p (i d) -> p i d", i=3)[:, :, 0:64],
                                in1=rc.t[:, 0:3].unsqueeze(2).broadcast_to([128, 3, 64]), op=ALU.mult),
                              r=[rc.b], w=[pso.b, om.b])
                        DMA("pool", omla[t0:t0 + W, half * 256:(half + 1) * 256].rearrange("(t p) d -> p t d", p=128),
                            om.t[:], [om.b], omb, own=om.b)
                while not gstate["done"]:
                    gdn_step()
                S.barrier()
                S.emit()

        if 4 in stages:
            with ExitStack() as st:
                R = {
                    "xnT": ring(st, "xnT3", [128, 8, W], BF16, 2),
                    "junk": ring(st, "junk3", [128, D], BF16, 2),
                    "ss": ring(st, "ss3", [128, 4], F32, 4),
                    "xn": ring(st, "xn3", [128, D], BF16, 3),
                    "hid": ring(st, "hid3", [128, NF, W], BF16, 1),
                    "wslot": ring(st, "wslot3", [128, 8, 512], BF16, 6),
                    "sil": ring(st, "sil3", [128, W], F32, 2),
                    "hb": [S.buf(f"hid3_{f}") for f in range(NF)],
                }
                wmo_t = sbt(st, "wmo_t", [128, 4, D], BF16)
                wgo_t = sbt(st, "wgo_t", [128, 4, D], BF16)
                wout_t = sbt(st, "wout_t", [128, 8, D], BF16)
                fnorm_t = sbt(st, "fnorm_t", [128, D], F32)
                w3 = S.buf("w3")
                DMA("pool", wmo_t[:], wmo_f.rearrange("(c p) n -> p c n", p=128), [], w3)
                DMA("pool", wgo_t[:], wgo_f.rearrange("(c p) n -> p c n", p=128), [], w3)
                for c in range(8):
                    DMA("pool", wout_t[:, c, :], wout_f[c * 128:(c + 1) * 128, :], [], w3)
                DMA("sp", fnorm_t[:], fnorm_d, [], w3)
                htr = ring(st, "ht", [128, 3, D], F32, 2)
                omr3 = ring(st, "om3", [128, 3, 512], BF16, 2)
                ogr3 = ring(st, "og3", [128, 3, 512], BF16, 2)
                gtr3 = ring(st, "gt3", [128, 2048], BF16, 2)
                oTr = ring(st, "oT3", [128, 8, 128], BF16, 2)
                mar = ring(st, "ma3", [128, 512], F32, 2)
                mbr = ring(st, "mb3", [128, 512], F32, 2)
                mrr = ring(st, "mrg3", [128, D], BF16, 2)
                mTr = ring(st, "mT3", [128, 8, 128], BF16, 2)
                outr = ring(st, "ost3", [128, D], F32, 2)
                outb = S.buf("out")
                def merge(s_):
                    t0 = s_ * W
                    ht = htr()
                    DMA("sp", ht.t[:], h1s[t0:t0 + W, :].rearrange("(t p) d -> p t d", p=128), [], ht.b)
                    om = omr3()
                    og = ogr3()
                    DMA("sp", om.t[:], omla[t0:t0 + W, :].rearrange("(t p) d -> p t d", p=128), [], om.b)
                    DMA("sp", og.t[:], ogdn[t0:t0 + W, :].rearrange("(t p) d -> p t d", p=128), [], og.b)
                    for t in range(3):
                        gt_ = gtr3()
                        DMA("sp", gt_.t[:], gates[t0 + t * 128:t0 + (t + 1) * 128, :], [], gt_.b)
                        ps = psum()
                        pv = ps.t[:].bitcast(BF16)
                        for c in range(4):
                            tr(ps, pv[:, c * 128:(c + 1) * 128], om.t[:, t, c * 128:(c + 1) * 128], identb[:],
                               [om.b, cst])
                        for c in range(4):
                            tr(ps, pv[:, (4 + c) * 128:(5 + c) * 128], og.t[:, t, c * 128:(c + 1) * 128], identb[:],
                               [og.b, cst])
                        oT = oTr()
                        A("act", lambda e, pv=pv, oT=oT: e.activation(
                            out=oT.t[:], in_=pv.rearrange("p (c n) -> p c n", c=8), func=AF.Copy), w=[ps.b, oT.b])
                        mrg = mrr()
                        for half in range(2):
                            hs = slice(half * 512, (half + 1) * 512)
                            psm = psum()
                            for c in range(4):
                                mm(psm, psm.t[:, :], oT.t[:, c, :], wmo_t[:, c, hs], c == 0, c == 3, [oT.b, w3])
                            psg = psum()
                            for c in range(4):
                                mm(psg, psg.t[:, :], oT.t[:, 4 + c, :], wgo_t[:, c, hs], c == 0, c == 3, [oT.b, w3])
                            ma = mar()
                            mb = mbr()
                            A("dve", lambda e, psm=psm, ma=ma, gt_=gt_, half=half: e.tensor_tensor(
                                out=ma.t[:], in0=psm.t[:, :], in1=gt_.t[:, half * 512:(half + 1) * 512], op=ALU.mult),
                              r=[gt_.b], w=[psm.b, ma.b])
                            A("dve", lambda e, psg=psg, mb=mb, gt_=gt_, half=half: e.tensor_tensor(
                                out=mb.t[:], in0=psg.t[:, :], in1=gt_.t[:, 1024 + half * 512:1024 + (half + 1) * 512],
                                op=ALU.mult), r=[gt_.b], w=[psg.b, mb.b])
                            A("pool", lambda e, ma=ma, mb=mb, mrg=mrg, half=half: e.tensor_tensor(
                                out=mrg.t[:, half * 512:(half + 1) * 512], in0=ma.t[:], in1=mb.t[:], op=ALU.add),
                              r=[ma.b, mb.b], w=[mrg.b])
                        ps2 = psum()
                        pv2 = ps2.t[:].bitcast(BF16)
                        for c in range(8):
                            tr(ps2, pv2[:, c * 128:(c + 1) * 128], mrg.t[:, c * 128:(c + 1) * 128], identb[:],
                               [mrg.b, cst])
                        mT = mTr()
                        A("act", lambda e, pv2=pv2, mT=mT: e.activation(
                            out=mT.t[:], in_=pv2.rearrange("p (c n) -> p c n", c=8), func=AF.Copy), w=[ps2.b, mT.b])
                        for half in range(2):
                            pso = psum()
                            for c in range(8):
                                mm(pso, pso.t[:, :], mT.t[:, c, :], wout_t[:, c, half * 512:(half + 1) * 512], c == 0,
                                   c == 7, [mT.b, w3])
                            A("dve", lambda e, pso=pso, ht=ht, t=t, half=half: e.tensor_tensor(
                                out=ht.t[:, t, half * 512:(half + 1) * 512], in0=ht.t[:, t, half * 512:(half + 1) * 512],
                                in1=pso.t[:, :], op=ALU.add), r=[ht.b], w=[pso.b, ht.b])
                    return ht

                def final(s_, ht):
                    t0 = s_ * W
                    for t in range(3):
                        junk = R["junk"]()
                        ss = R["ss"]()
                        A("dve", lambda e, ss=ss: e.memset(ss.t[:], 0.0), w=[ss.b])
                        A("act", lambda e, t=t, junk=junk, ss=ss, ht=ht: e.activation(
                            out=junk.t[:], in_=ht.t[:, t, :], func=AF.Square, accum_out=ss.t[:, 0:1]),
                          r=[ht.b, ss.b], w=[junk.b, ss.b])
                        A("act", lambda e, ss=ss: e.activation(out=ss.t[:, 1:2], in_=ss.t[:, 0:1], func=AF.Sqrt,
                                                               scale=1.0 / D, bias=EPS), r=[ss.b], w=[ss.b])
                        A("dve", lambda e, ss=ss: e.reciprocal(out=ss.t[:, 2:3], in_=ss.t[:, 1:2]), r=[ss.b], w=[ss.b])
                        ot = outr()
                        A("dve", lambda e, ot=ot, ht=ht, t=t, ss=ss: e.scalar_tensor_tensor(
                            out=ot.t[:], in0=ht.t[:, t, :], scalar=ss.t[:, 2:3], in1=fnorm_t[:], op0=ALU.mult,
                            op1=ALU.mult), r=[ht.b, ss.b, w3], w=[ot.b])
                        DMA("pool", out_d[t0 + t * 128:t0 + (t + 1) * 128, :], ot.t[:], [ot.b], outb, own=ot.b)

                hts = {0: merge(0)}
                xnT2 = {0: norm_tr(R, norm_chain(R, hts[0]), nrm_t[2])}
                for s_ in range(NST):
                    hid = ffn_gu(R, xnT2[s_], 1)
                    axn = None
                    if s_ + 1 < NST:
                        hts[s_ + 1] = merge(s_ + 1)
                        axn = norm_chain(R, hts[s_ + 1])
                    ffn_down(R, hid, hts[s_], 1)
                    final(s_, hts[s_])
                    if axn is not None:
                        xnT2[s_ + 1] = norm_tr(R, axn, nrm_t[2])
                S.barrier()
                S.emit()

        S.barrier()
        S.emit()
    return nc


def host_inputs(inp, b):
    f32 = np.float32
    x = np.asarray(inp["x"], f32)
    xin = np.zeros((LP, D), f32)
    xin[:NMETA] = np.asarray(inp["meta_tokens"], f32)
    xin[NMETA:LREAL] = x[b]
    win = np.asarray(inp["w_in"], f32)[0]
    kr = win[:, 384:416]
    win_ext = np.concatenate([win, kr[:, 16:32], kr[:, 0:16]], axis=1)
    wuq = np.asarray(inp["w_uq"], f32)[0].reshape(256, 8, 96)
    wuqs = np.zeros_like(wuq)
    wuqs[:, :, 64:80] = wuq[:, :, 80:96]
    wuqs[:, :, 80:96] = wuq[:, :, 64:80]
    wukv = np.asarray(inp["w_ukv"], f32)[0].reshape(128, 8, 128)

    def pc(v, c):
        return np.ascontiguousarray(np.asarray(v, f32).reshape(c, 128).T)

    pos = np.arange(LP, dtype=f32)
    inv = (np.float32(10000.0) ** (-np.arange(0, 32, 2, dtype=f32) / np.float32(32))).astype(f32)
    ang = (pos[:, None] * inv[None, :]).astype(f32)
    cos = np.cos(ang).astype(f32).T
    sin = np.sin(ang).astype(f32).T
    cosT = np.concatenate([cos, cos], 0)
    sinT = np.concatenate([-sin, sin], 0)
    cwv = np.asarray(inp["conv_w"], f32)[0]
    cw = np.ascontiguousarray(cwv.reshape(4, 12, 128).transpose(2, 1, 0).reshape(128, 48))
    m = {
        "xin": xin,
        "wg1": np.asarray(inp["ffn1_w_gate"], f32)[0], "wu1": np.asarray(inp["ffn1_w_up"], f32)[0],
        "wd1": np.asarray(inp["ffn1_w_down"], f32)[0],
        "wg2": np.asarray(inp["ffn2_w_gate"], f32)[0], "wu2": np.asarray(inp["ffn2_w_up"], f32)[0],
        "wd2": np.asarray(inp["ffn2_w_down"], f32)[0],
        "win": win_ext,
        "wuq": wuq.reshape(256, 768), "wuqs": wuqs.reshape(256, 768),
        "wuk": np.ascontiguousarray(wukv[:, :, 0:64].reshape(128, 512)),
        "wuv": np.ascontiguousarray(wukv[:, :, 64:128].reshape(128, 512)),
        "wmo": np.asarray(inp["w_mla_o"], f32)[0], "wgo": np.asarray(inp["w_gdn_o"], f32)[0],
        "wout": np.asarray(inp["w_out"], f32)[0],
        "nrm1": pc(inp["ffn1_norm"][0], 8), "nrmm": pc(inp["mix_norm"][0], 8), "nrm2": pc(inp["ffn2_norm"][0], 8),
        "qn": pc(inp["q_norm"][0], 2), "kvn": pc(inp["kv_norm"][0], 1),
        "cw": cw,
        "alog": np.ascontiguousarray(np.broadcast_to(np.asarray(inp["a_log"], f32)[0][None, :], (128, 8))),
        "dtb": np.ascontiguousarray(np.broadcast_to(np.asarray(inp["dt_bias"], f32)[0][None, :], (128, 8))),
        "gnorm": np.ascontiguousarray(np.broadcast_to(np.asarray(inp["gdn_norm"], f32)[0][None, :], (128, 64))),
        "fnorm": np.ascontiguousarray(np.broadcast_to(np.asarray(inp["final_norm"], f32)[None, :], (128, D))),
        "cosT": np.ascontiguousarray(cosT), "sinT": np.ascontiguousarray(sinT),
    }
    return {k: np.ascontiguousarray(v, dtype=f32) for k, v in m.items()}


def kernel(**inputs):
    nc = build()
    in_maps = [host_inputs(inputs, b) for b in range(NCORES)]
    res = run_bass_kernel_spmd(nc, in_maps, core_ids=list(range(NCORES)))
    out = np.stack([np.asarray(res.results[b]["out"])[NMETA:LREAL] for b in range(NCORES)], axis=0)
    return out.astype(np.float32)
```

```python
import numpy as np
from contextlib import ExitStack
import concourse.bass as bass
import concourse.mybir as mybir
from concourse.bass_utils import run_bass_kernel_spmd

F32 = mybir.dt.float32
BF16 = mybir.dt.bfloat16
AF = mybir.ActivationFunctionType
ALU = mybir.AluOpType
AX = mybir.AxisListType

ENGS = ("pe", "act", "dve", "pool", "sp")
SEM_LIMIT = 30000

D = 1024
DFF = 2816
NF = 22
SEQ = 4096
NMETA = 16
LREAL = SEQ + NMETA
LP = 4224
NB = 33
W = 384
NST = 11
EPS = 1e-6
DIN_EXT = 4560
NCORES = 4
GDN_F32 = True
GDN_F32R = False
F32R = mybir.dt.float32r


class Buf:
    __slots__ = ("name", "writers", "readers", "dsem", "dcount")

    def __init__(self, name):
        self.name = name
        self.writers = []
        self.readers = []
        self.dsem = None
        self.dcount = 0


class Op:
    __slots__ = ("eng", "fn", "deps", "is_dma", "sem", "count", "needed", "seq")


class TB:
    __slots__ = ("t", "b")

    def __init__(self, t, b):
        self.t = t
        self.b = b


class Sched:
    def __init__(self, nc, es, max_dma_sems=120):
        self.nc = nc
        self.es = es
        self.ops = {e: [] for e in ENGS}
        self.bufs = []
        self.dma_sems = []
        self.max_dma_sems = max_dma_sems
        self.eng_sems = {e: [] for e in ENGS}
        self.eng_cnt = {e: 0 for e in ENGS}
        self.last_op = {e: None for e in ENGS}
        self.dma_since = []
        self.waited = {e: {} for e in ENGS}
        self.nops = 0

    def buf(self, name=None):
        b = Buf(name or f"b{len(self.bufs)}")
        self.bufs.append(b)
        return b

    def _dsem(self, b):
        if b.dsem is None:
            assert len(self.dma_sems) < self.max_dma_sems, "too many dma sems"
            s = self.es.enter_context(self.nc.semaphore(f"d{len(self.dma_sems)}"))
            self.dma_sems.append(s)
            b.dsem = s
        return b.dsem

    def add(self, eng, fn, reads=(), writes=(), dma=False, own=None):
        op = Op()
        op.eng = eng
        op.fn = fn
        op.is_dma = dma
        op.seq = self.nops
        op.needed = False
        op.sem = None
        op.count = None
        deps = []
        seen = set()

        def push(d):
            if id(d) in seen:
                return
            seen.add(id(d))
            deps.append(d)

        for b in reads:
            for d in b.writers:
                push(d)
        owner = None
        if dma:
            assert len(writes) == 1
            owner = own if own is not None else writes[0]
            osem = self._dsem(owner)
        for b in writes:
            for d in b.writers:
                if dma and d.is_dma and d.sem is osem:
                    continue
                push(d)
            for d in b.readers:
                push(d)
        if eng == "pe" and not dma:
            deps = [d for d in deps if not (d.eng == "pe" and not d.is_dma)]
        latest = {}
        rest = []
        for d in deps:
            if d.is_dma:
                rest.append(d)
            elif d.eng not in latest or latest[d.eng].seq < d.seq:
                latest[d.eng] = d
        deps = rest + list(latest.values())
        op.deps = deps
        for d in deps:
            d.needed = True
        if dma:
            op.sem = osem
            owner.dcount += 16
            op.count = owner.dcount
            self.dma_since.append(op)
        for b in reads:
            b.readers.append(op)
        for b in writes:
            if dma and owner is not b:
                b.writers = [w for w in b.writers if w.is_dma] + [op]
            else:
                b.writers = [op]
            b.readers = []
        self.ops[eng].append(op)
        if fn is not None and not dma:
            self.last_op[eng] = op
        self.nops += 1
        return op

    def barrier(self):
        deps = [self.last_op[e] for e in ENGS if self.last_op[e] is not None] + list(self.dma_since)
        for e in ENGS:
            op = Op()
            op.eng = e
            op.fn = None
            op.seq = self.nops
            op.is_dma = False
            op.needed = False
            op.sem = None
            op.count = None
            op.deps = [d for d in deps if not (d.eng == e and not d.is_dma)]
            for d in op.deps:
                d.needed = True
            self.ops[e].append(op)
        for b in self.bufs:
            b.writers = []
            b.readers = []
        self.dma_since = []

    def emit(self):
        nc = self.nc
        for e in ENGS:
            for op in self.ops[e]:
                if op.is_dma or op.fn is None:
                    continue
                if op.needed:
                    self.eng_cnt[e] += 1
                    c = self.eng_cnt[e]
                    k = (c - 1) // SEM_LIMIT
                    while len(self.eng_sems[e]) <= k:
                        self.eng_sems[e].append(
                            self.es.enter_context(nc.semaphore(f"e_{e}_{len(self.eng_sems[e])}")))
                    op.sem = self.eng_sems[e][k]
                    op.count = c - k * SEM_LIMIT

        def run(e, engine):
            waited = self.waited[e]
            for op in self.ops[e]:
                need = {}
                for d in op.deps:
                    key = id(d.sem)
                    if waited.get(key, 0) >= d.count:
                        continue
                    if key not in need or need[key][1] < d.count:
                        need[key] = (d.sem, d.count)
                for key, (s, c) in need.items():
                    engine.wait_ge(s, c)
                    waited[key] = c
                if op.fn is None:
                    continue
                ins = op.fn(engine)
                if op.is_dma:
                    ins.then_inc(op.sem, 16)
                elif op.needed:
                    ins.then_inc(op.sem, 1)
            self.ops[e] = []

        with nc.Block() as block:
            @block.tensor
            def _(eng):
                run("pe", eng)

            @block.scalar
            def _(eng):
                run("act", eng)

            @block.vector
            def _(eng):
                run("dve", eng)

            @block.gpsimd
            def _(eng):
                run("pool", eng)

            @block.sync
            def _(eng):
                run("sp", eng)


FFN_COLGROUPS = [(0, 512), (512, 512), (1024, 512), (1536, 512), (2048, 512), (2560, 256)]
FFN_ROWGROUPS = [(0, 8), (8, 8), (16, 6)]


def build(dbg=False, stages=(1, 2, 3, 4), nb_lim=None, nst_lim=None, lvl=9):
    nc = bass.Bass("TRN2", target_bir_lowering=False)

    def din(name, shape, dt=F32):
        return nc.dram_tensor(name, shape, dt, kind="ExternalInput").ap()

    def dscr(name, shape, dt):
        return nc.dram_tensor(name, shape, dt, kind=("ExternalOutput" if dbg else "Internal")).ap()

    xin = din("xin", [LP, D])
    wg_f = [din("wg1", [D, DFF]), din("wg2", [D, DFF])]
    wu_f = [din("wu1", [D, DFF]), din("wu2", [D, DFF])]
    wd_f = [din("wd1", [DFF, D]), din("wd2", [DFF, D])]
    win_f = din("win", [D, DIN_EXT])
    wuq_f = din("wuq", [256, 768])
    wuqs_f = din("wuqs", [256, 768])
    wuk_f = din("wuk", [128, 512])
    wuv_f = din("wuv", [128, 512])
    wmo_f = din("wmo", [512, D])
    wgo_f = din("wgo", [512, D])
    wout_f = din("wout", [D, D])
    nrm_d = [din("nrm1", [128, 8]), din("nrmm", [128, 8]), din("nrm2", [128, 8])]
    qn_d = din("qn", [128, 2])
    kvn_d = din("kvn", [128, 1])
    cw_d = din("cw", [128, 48])
    alog_d = din("alog", [128, 8])
    dtb_d = din("dtb", [128, 8])
    gnorm_d = din("gnorm", [128, 64])
    fnorm_d = din("fnorm", [128, D])
    cosT_d = din("cosT", [32, LP])
    sinT_d = din("sinT", [32, LP])
    out_d = nc.dram_tensor("out", [LP, D], F32, kind="ExternalOutput").ap()

    wg_b = [nc.dram_tensor(f"wgb{i}", [6, 128, 8, 512], BF16, kind="Internal").ap() for i in range(2)]
    wu_b = [nc.dram_tensor(f"wub{i}", [6, 128, 8, 512], BF16, kind="Internal").ap() for i in range(2)]
    wd_b = [nc.dram_tensor(f"wdb{i}", [6, 128, 8, 512], BF16, kind="Internal").ap() for i in range(2)]
    win_b = nc.dram_tensor("winb", [9, 128, 8, 512], BF16, kind="Internal").ap()

    h1s = dscr("h1s", [LP, D], F32)
    cqnT = dscr("cqnT", [256, LP], BF16)
    ckvnT = dscr("ckvnT", [128, LP], BF16)
    kropeT = dscr("kropeT", [32, LP], BF16)
    gqT = dscr("gqT", [512, LP], BF16)
    gkT = dscr("gkT", [512, LP], BF16)
    gktok = dscr("gktok", [LP, 512], BF16)
    gvtok = dscr("gvtok", [LP, 512], BF16)
    gbs = dscr("gbs", [LP, 8], F32)
    ggs = dscr("ggs", [LP, 8], F32)
    zs = dscr("zs", [LP, 512], BF16)
    gates = dscr("gates", [LP, 2048], BF16)
    omla = dscr("omla", [LP, 512], BF16)
    ogdn = dscr("ogdn", [LP, 512], BF16)

    with ExitStack() as es:
        S = Sched(nc, es)

        def sbt(st, name, shape, dt):
            return st.enter_context(nc.sbuf_tensor(name, shape, dt))

        def tb(st, name, shape, dt):
            return TB(sbt(st, name, shape, dt), S.buf(name))

        def ring(st, name, shape, dt, n):
            tiles = [tb(st, f"{name}{i}", shape, dt) for i in range(n)]
            state = {"i": 0}

            def nxt():
                r = tiles[state["i"] % n]
                state["i"] += 1
                return r
            return nxt

        def A(eng, fn, r=(), w=()):
            return S.add(eng, fn, reads=list(r), writes=list(w))

        def DMA(eng, out, in_, r, w, own=None):
            return S.add(eng, lambda e: e.dma_start(out=out, in_=in_), reads=list(r), writes=[w], dma=True, own=own)

        banks = [TB(es.enter_context(nc.psum_tensor(f"ps{i}", [128, 512], F32)), S.buf(f"ps{i}"))
                 for i in range(8)]
        pstate = {"i": 0}

        pcfg = {"n": 8}

        def psum():
            r = banks[pstate["i"] % pcfg["n"]]
            pstate["i"] += 1
            return r

        accst = {"i": 0}

        def psum_acc():
            r = banks[6 + accst["i"] % 2]
            accst["i"] += 1
            return r

        def mm(ps, out, lhsT, rhs, start, stop, r, skip=False):
            if skip:
                return A("pe", lambda e: e.matmul(out=out, lhsT=lhsT, rhs=rhs, start=start, stop=stop,
                                                  skip_group_check=True), r=r, w=[ps.b])
            return A("pe", lambda e: e.matmul(out=out, lhsT=lhsT, rhs=rhs, start=start, stop=stop),
                     r=r, w=[ps.b])

        def tr(ps, out, in_, ident, r):
            return A("pe", lambda e: e.transpose(out=out, in_=in_, identity=ident), r=r, w=[ps.b])

        cst = S.buf("consts")
        identf = sbt(es, "identf", [128, 128], F32)
        identb = sbt(es, "identb", [128, 128], BF16)
        onesf = sbt(es, "onesf", [128, 128], F32)
        onesb = sbt(es, "onesb", [128, 128], BF16)
        onesblk = sbt(es, "onesblk", [128, 128], BF16)
        Uf = sbt(es, "Uf", [128, 128], F32)
        maskI = sbt(es, "maskI", [128, 128], F32)
        maskS = sbt(es, "maskS", [128, 128], F32)
        causb = sbt(es, "causb", [128, 128], BF16)
        A("pool", lambda e: e.memset(identf[:], 0.0), w=[cst])
        A("pool", lambda e: e.affine_select(out=identf[:], in_=identf[:], pattern=[[-1, 128]],
                                            compare_op=ALU.not_equal, fill=1.0, base=0,
                                            channel_multiplier=1), r=[cst], w=[cst])
        A("pool", lambda e: e.tensor_copy(out=identb[:], in_=identf[:]), r=[cst], w=[cst])
        A("pool", lambda e: e.memset(onesf[:], 1.0), w=[cst])
        A("pool", lambda e: e.memset(onesb[:], 1.0), w=[cst])
        A("pool", lambda e: e.memset(onesblk[:], 0.0), w=[cst])
        A("pool", lambda e: e.memset(onesblk[0:64, 0:64], 1.0), r=[cst], w=[cst])
        A("pool", lambda e: e.memset(onesblk[64:128, 64:128], 1.0), r=[cst], w=[cst])
        A("pool", lambda e: e.affine_select(out=Uf[:], in_=onesf[:], pattern=[[1, 128]],
                                            compare_op=ALU.is_ge, fill=0.0, base=0,
                                            channel_multiplier=-1), r=[cst], w=[cst])
        A("pool", lambda e: e.tensor_copy(out=maskI[:], in_=Uf[:]), r=[cst], w=[cst])
        A("pool", lambda e: e.affine_select(out=maskS[:], in_=onesf[:], pattern=[[1, 128]],
                                            compare_op=ALU.is_gt, fill=0.0, base=0,
                                            channel_multiplier=-1), r=[cst], w=[cst])
        A("pool", lambda e: e.tensor_copy(out=causb[:], in_=Uf[:]), r=[cst], w=[cst])
        zerob = sbt(es, "zerob", [128, 256], BF16)
        A("pool", lambda e: e.memset(zerob[:], 0.0), w=[cst])
        negmask = sbt(es, "negmask", [128, 128], BF16)
        A("pool", lambda e: e.tensor_scalar(out=negmask[:], in0=Uf[:], scalar1=-1.0, scalar2=30000.0,
                                            op0=ALU.add, op1=ALU.mult), r=[cst], w=[cst])

        nrm_t = [sbt(es, f"nrmt{i}", [128, 8], F32) for i in range(3)]
        qn_t = sbt(es, "qn_t", [128, 2], F32)
        kvn_t = sbt(es, "kvn_t", [128, 1], F32)
        cw_t = sbt(es, "cw_t", [128, 48], F32)
        alog_t = sbt(es, "alog_t", [128, 8], F32)
        negA_t = sbt(es, "negA_t", [128, 8], F32)
        dtb_t = sbt(es, "dtb_t", [128, 8], F32)
        gnorm_t = sbt(es, "gnorm_t", [128, 64], F32)
        prm = S.buf("prm")
        for t_, d_ in [(nrm_t[0], nrm_d[0]), (nrm_t[1], nrm_d[1]), (nrm_t[2], nrm_d[2]), (qn_t, qn_d),
                       (kvn_t, kvn_d), (cw_t, cw_d), (alog_t, alog_d), (dtb_t, dtb_d), (gnorm_t, gnorm_d)]:
            DMA("sp", t_[:], d_, [], prm)
        A("act", lambda e: e.activation(out=negA_t[:], in_=alog_t[:], func=AF.Exp), r=[prm], w=[cst])
        A("dve", lambda e: e.tensor_scalar(out=negA_t[:], in0=negA_t[:], scalar1=-1.0, scalar2=None,
                                           op0=ALU.mult), r=[cst], w=[cst])

        wgc = [[S.buf(f"wgc{i}_{g}") for g in range(6)] for i in range(2)]
        wuc = [[S.buf(f"wuc{i}_{g}") for g in range(6)] for i in range(2)]
        wdc = [S.buf(f"wdc{i}") for i in range(2)]
        winc = S.buf("winc")

        def tiled_cast(dst_g, dcol, src, r0, nchunk, c0, ncol, b, late=False):
            for ca in range(0, nchunk, 4):
                cb = min(nchunk, ca + 4)
                d_ = dst_g[:, ca:cb, dcol:dcol + ncol]
                s__ = src[(r0 + ca) * 128:(r0 + cb) * 128, c0:c0 + ncol].rearrange("(c p) n -> p c n", p=128)
                if late:
                    late_casts.append((d_, s__, b))
                else:
                    DMA("pool", d_, s__, [], b)

        late_casts = []
        WIN_GROUPS = [[(0, 416, 0), (4528, 32, 416), (1952, 16, 448)]] + \
            [[(416 + g_ * 512, 512, 0)] for g_ in range(3)] + \
            [[(c_, 512, 0)] for c_ in [1968, 2480, 2992, 3504, 4016]]

        def cast_ffn(i, late=False):
            for g, (c0, ncol) in enumerate(FFN_COLGROUPS):
                tiled_cast(wg_b[i][g], 0, wg_f[i], 0, 8, c0, ncol, wgc[i][g], late)
                tiled_cast(wu_b[i][g], 0, wu_f[i], 0, 8, c0, ncol, wuc[i][g], late)
            for ri, (r0, nr) in enumerate(FFN_ROWGROUPS):
                for half in range(2):
                    tiled_cast(wd_b[i][ri * 2 + half], 0, wd_f[i], r0, nr, half * 512, 512, wdc[i], late)

        cast_ffn(0)
        for gi_, parts in enumerate(WIN_GROUPS):
            for (c0, ncol, dcol) in parts:
                tiled_cast(win_b[gi_], dcol, win_f, 0, 8, c0, ncol, winc)
        cast_ffn(1, late=True)

        def issue_late_casts(k):
            for _ in range(k):
                if late_casts:
                    d_, s__, b_ = late_casts.pop(0)
                    DMA("pool", d_, s__, [], b_)

        def norm_chain(st_r, xt, nblk=3):
            xns = []
            for t in range(nblk):
                junk = st_r["junk"]()
                ss = st_r["ss"]()
                A("dve", lambda e, ss=ss: e.memset(ss.t[:], 0.0), w=[ss.b])
                A("act", lambda e, t=t, junk=junk, ss=ss: e.activation(
                    out=junk.t[:], in_=xt.t[:, t, :], func=AF.Square, accum_out=ss.t[:, 0:1]),
                  r=[xt.b, ss.b], w=[junk.b, ss.b])
                A("act", lambda e, ss=ss: e.activation(out=ss.t[:, 1:2], in_=ss.t[:, 0:1], func=AF.Sqrt,
                                                       scale=1.0 / D, bias=EPS), r=[ss.b], w=[ss.b])
                A("dve", lambda e, ss=ss: e.reciprocal(out=ss.t[:, 2:3], in_=ss.t[:, 1:2]), r=[ss.b], w=[ss.b])
                xn = st_r["xn"]()
                A("dve", lambda e, t=t, xn=xn, ss=ss: e.tensor_scalar(
                    out=xn.t[:], in0=xt.t[:, t, :], scalar1=ss.t[:, 2:3], scalar2=None, op0=ALU.mult),
                  r=[xt.b, ss.b], w=[xn.b])
                xns.append(xn)
            return xns

        def norm_tr(st_r, xns, nrm):
            xnT = st_r["xnT"]()
            for t, xn in enumerate(xns):
                ps = psum()
                pv = ps.t[:].bitcast(BF16)
                for c in range(8):
                    tr(ps, pv[:, c * 128:(c + 1) * 128], xn.t[:, c * 128:(c + 1) * 128], identb[:], [xn.b, cst])
                A("dve", lambda e, t=t, pv=pv, xnT=xnT: e.tensor_tensor(
                    out=xnT.t[:, :, t * 128:(t + 1) * 128],
                    in0=pv.rearrange("p (c n) -> p c n", c=8),
                    in1=nrm[:, :].unsqueeze(2).broadcast_to([128, 8, 128]), op=ALU.mult),
                  r=[prm], w=[ps.b, xnT.b])
            return xnT

        def norm_T(st_r, xt, nrm, nblk=3):
            xnT = norm_tr(st_r, norm_chain(st_r, xt, nblk), nrm)
            return xnT, [xnT.b] * nblk

        def ffn_gu(st_r, xnT, i, nblk=3):
            ncol = nblk * 128
            xb = [xnT.b]
            hid = st_r["hid"]()
            hb = st_r["hb"]
            for g, (c0, ncg) in enumerate(FFN_COLGROUPS):
                sg = st_r["wslot"]()
                su = st_r["wslot"]()
                DMA("sp", sg.t[:, :, 0:ncg], wg_b[i][g, :, :, 0:ncg], [wgc[i][g]], sg.b)
                DMA("sp", su.t[:, :, 0:ncg], wu_b[i][g, :, :, 0:ncg], [wuc[i][g]], su.b)
                for f in range(ncg // 128):
                    fi = c0 // 128 + f
                    pg = psum()
                    for k in range(8):
                        mm(pg, pg.t[:, 0:ncol], sg.t[:, k, f * 128:(f + 1) * 128], xnT.t[:, k, 0:ncol],
                           k == 0, k == 7, [sg.b] + xb)
                    pu = psum()
                    for k in range(8):
                        mm(pu, pu.t[:, 0:ncol], su.t[:, k, f * 128:(f + 1) * 128], xnT.t[:, k, 0:ncol],
                           k == 0, k == 7, [su.b] + xb)
                    sil = st_r["sil"]()
                    A("act", lambda e, pg=pg, sil=sil: e.activation(out=sil.t[:, 0:ncol], in_=pg.t[:, 0:ncol],
                                                                     func=AF.Silu), w=[pg.b, sil.b])
                    A("dve", lambda e, pu=pu, sil=sil, fi=fi: e.tensor_tensor(
                        out=hid.t[:, fi, 0:ncol], in0=sil.t[:, 0:ncol], in1=pu.t[:, 0:ncol], op=ALU.mult),
                      r=[sil.b], w=[pu.b, hb[fi]])
            return hid

        def ffn_down(st_r, hid, xt, i, nblk=3):
            hb = st_r["hb"]
            accs = [[psum() for _ in range(2)] for _ in range(nblk)]
            for ri, (r0, nr) in enumerate(FFN_ROWGROUPS):
                for half in range(2):
                    sd = st_r["wslot"]()
                    DMA("sp", sd.t[:, 0:nr, :], wd_b[i][ri * 2 + half, :, 0:nr, :], [wdc[i]], sd.b)
                    for t in range(nblk):
                        for f in range(nr):
                            fi = r0 + f
                            mm(accs[t][half], accs[t][half].t[:, :], hid.t[:, fi, t * 128:(t + 1) * 128],
                               sd.t[:, f, :], fi == 0, fi == NF - 1, [sd.b, hb[fi]])
            for t in range(nblk):
                for half in range(2):
                    ps = accs[t][half]
                    A("dve", lambda e, ps=ps, t=t, half=half: e.scalar_tensor_tensor(
                        out=xt.t[:, t, half * 512:(half + 1) * 512], in0=ps.t[:, :], scalar=0.5,
                        in1=xt.t[:, t, half * 512:(half + 1) * 512], op0=ALU.mult, op1=ALU.add),
                      r=[], w=[ps.b, xt.b])

        if 1 in stages:
            with ExitStack() as st:
                R = {
                    "xnT": ring(st, "xnT", [128, 8, W], BF16, 3),
                    "junk": ring(st, "junk", [128, D], BF16, 2),
                    "ss": ring(st, "ss", [128, 4], F32, 4),
                    "xn": ring(st, "xn", [128, D], BF16, 6),
                    "hid": ring(st, "hid", [128, NF, W], BF16, 1),
                    "wslot": ring(st, "wslot", [128, 8, 512], BF16, 5),
                    "sil": ring(st, "sil", [128, W], F32, 2),
                    "hb": [S.buf(f"hid{f}") for f in range(NF)],
                }
                xtr = ring(st, "xt", [128, 3, D], F32, 2)
                cqf = tb(st, "cqf", [128, 2, W], F32)
                ckvf = tb(st, "ckvf", [128, 1, W], F32)
                sqb = ring(st, "sqb", [128, W], BF16, 6)
                rr = ring(st, "rr", [128, W], F32, 2)
                ob16 = ring(st, "ob16", [128, W], BF16, 8)
                tabc = ring(st, "tabc", [32, W], F32, 2)
                tabs = ring(st, "tabs", [32, W], F32, 2)
                kr1 = ring(st, "kr1", [32, W], F32, 2)
                kr2 = ring(st, "kr2", [32, W], F32, 2)
                smallr = ring(st, "small", [128, 32], F32, 4)
                pre = [tb(st, f"pre{c}", [128, W + 3], BF16) for c in range(12)]
                dcw = sbt(st, "dcw", [128, 48, 128], BF16)
                dcwb = [S.buf("dcw0"), S.buf("dcw1")]
                for c_ in range(48):
                    A("dve", lambda e, c_=c_: e.tensor_scalar(
                        out=dcw[:, c_, :], in0=identf[:], scalar1=cw_t[:, c_:c_ + 1], scalar2=None,
                        op0=ALU.mult), r=[prm, cst], w=[dcwb[c_ % 2]])
                yr = ring(st, "yy", [128, W], F32, 5)
                ktst = tb(st, "ktst", [128, 3, 512], BF16)
                vtst = tb(st, "vtst", [128, 3, 512], BF16)
                zst = ring(st, "zst", [128, 512], BF16, 2)
                gst = ring(st, "gst", [128, 2048], BF16, 3)
                for c in range(12):
                    A("pool", lambda e, c=c: e.memset(pre[c].t[:, 0:3], 0.0), w=[pre[c].b])

                h1b = S.buf("h1s")
                sc = {k: S.buf(k) for k in ["cqnT", "ckvnT", "kropeT", "gqT", "gkT", "gktok", "gvtok",
                                            "gbs", "ggs", "zs", "gates"]}

                def inproj(s_, uT):
                    t0 = s_ * W
                    ub = [uT.b] * 3
                    sA = R["wslot"]()
                    DMA("sp", sA.t[:, :, 0:464], win_b[0, :, :, 0:464], [winc], sA.b)

                    def lowrank(cols, nch, nrmw, dst, dstb, inv_n, cqf):
                        sq = []
                        for c in range(nch):
                            ps = psum()
                            for k in range(8):
                                mm(ps, ps.t[:, 0:W], sA.t[:, k, cols + c * 128:cols + (c + 1) * 128],
                                   uT.t[:, k, :], k == 0, k == 7, [sA.b] + ub)
                            q = sqb()
                            A("act", lambda e, ps=ps, q=q: e.activation(out=q.t[:], in_=ps.t[:, 0:W], func=AF.Square),
                              w=[ps.b, q.b])
                            A("dve", lambda e, ps=ps, c=c: e.tensor_copy(out=cqf.t[:, c, :], in_=ps.t[:, 0:W]),
                              w=[ps.b, cqf.b])
                            sq.append(q)
                        yield
                        ps2 = psum()
                        for c in range(nch):
                            mm(ps2, ps2.t[:, 0:W], onesb[:], sq[c].t[:], c == 0, c == nch - 1, [sq[c].b, cst])
                        r_ = rr()
                        A("act", lambda e, ps2=ps2, r_=r_: e.activation(out=r_.t[:], in_=ps2.t[:, 0:W], func=AF.Sqrt,
                                                                        scale=inv_n, bias=EPS), w=[ps2.b, r_.b])
                        A("dve", lambda e, r_=r_: e.reciprocal(out=r_.t[:], in_=r_.t[:]), r=[r_.b], w=[r_.b])
                        for c in range(nch):
                            o = ob16()
                            A("dve", lambda e, c=c, o=o, r_=r_: e.scalar_tensor_tensor(
                                out=o.t[:], in0=cqf.t[:, c, :], scalar=nrmw[:, c:c + 1], in1=r_.t[:],
                                op0=ALU.mult, op1=ALU.mult), r=[cqf.b, r_.b, prm], w=[o.b])
                            DMA("pool", dst[c * 128:(c + 1) * 128, t0:t0 + W], o.t[:], [o.b], dstb, own=o.b)

                    lr1 = lowrank(0, 2, qn_t, cqnT, sc["cqnT"], 1.0 / 256, cqf)
                    lr2 = lowrank(256, 1, kvn_t, ckvnT, sc["ckvnT"], 1.0 / 128, ckvf)
                    next(lr1)
                    next(lr2)

                    tc_ = tabc()
                    ts_ = tabs()
                    DMA("sp", tc_.t[:], cosT_d[:, t0:t0 + W], [], tc_.b)
                    DMA("sp", ts_.t[:], sinT_d[:, t0:t0 + W], [], ts_.b)
                    psa = psum()
                    for k in range(8):
                        mm(psa, psa.t[0:32, 0:W], sA.t[:, k, 384:416], uT.t[:, k, :], k == 0, k == 7, [sA.b] + ub)
                    psb = psum()
                    for k in range(8):
                        mm(psb, psb.t[0:32, 0:W], sA.t[:, k, 416:448], uT.t[:, k, :], k == 0, k == 7, [sA.b] + ub)
                    k1 = kr1()
                    k2 = kr2()
                    A("dve", lambda e, psa=psa, k1=k1, tc_=tc_: e.tensor_tensor(
                        out=k1.t[:], in0=psa.t[0:32, 0:W], in1=tc_.t[:], op=ALU.mult), r=[tc_.b], w=[psa.b, k1.b])
                    A("dve", lambda e, psb=psb, k2=k2, ts_=ts_: e.tensor_tensor(
                        out=k2.t[:], in0=psb.t[0:32, 0:W], in1=ts_.t[:], op=ALU.mult), r=[ts_.b], w=[psb.b, k2.b])
                    o = ob16()
                    A("dve", lambda e, k1=k1, k2=k2, o=o: e.tensor_tensor(
                        out=o.t[0:32, :], in0=k1.t[:], in1=k2.t[:], op=ALU.add), r=[k1.b, k2.b], w=[o.b])
                    DMA("pool", kropeT[:, t0:t0 + W], o.t[0:32, :], [o.b], sc["kropeT"], own=o.b)

                    for t in range(3):
                        ps = psum()
                        for k in range(8):
                            mm(ps, ps.t[:, 0:16], uT.t[:, k, t * 128:(t + 1) * 128], sA.t[:, k, 448:464],
                               k == 0, k == 7, [sA.b, ub[t]])
                        sm = smallr()
                        A("act", lambda e, ps=ps, sm=sm: e.activation(out=sm.t[:, 0:8], in_=ps.t[:, 0:8],
                                                                      func=AF.Sigmoid), w=[ps.b, sm.b])
                        A("dve", lambda e, ps=ps, sm=sm: e.tensor_tensor(out=sm.t[:, 8:16], in0=ps.t[:, 8:16],
                                                                         in1=dtb_t[:], op=ALU.add),
                          r=[prm, sm.b], w=[ps.b, sm.b])
                        A("act", lambda e, sm=sm: e.activation(out=sm.t[:, 8:16], in_=sm.t[:, 8:16], func=AF.Exp),
                          r=[sm.b], w=[sm.b])
                        A("act", lambda e, sm=sm: e.activation(out=sm.t[:, 8:16], in_=sm.t[:, 8:16], func=AF.Ln,
                                                               bias=1.0), r=[sm.b], w=[sm.b])
                        A("dve", lambda e, sm=sm: e.tensor_tensor(out=sm.t[:, 16:24], in0=sm.t[:, 8:16],
                                                                  in1=negA_t[:], op=ALU.mult),
                          r=[sm.b, cst], w=[sm.b])
                        DMA("pool", gbs[t0 + t * 128:t0 + (t + 1) * 128, :], sm.t[:, 0:8], [sm.b], sc["gbs"], own=sm.b)
                        DMA("pool", ggs[t0 + t * 128:t0 + (t + 1) * 128, :], sm.t[:, 16:24], [sm.b], sc["ggs"], own=sm.b)

                    for lr_ in (lr1, lr2):
                        try:
                            next(lr_)
                        except StopIteration:
                            pass
                    def qkv_chunk(grp, f, sl):
                        ci = grp * 4 + f
                        ps = psum()
                        for k in range(8):
                            mm(ps, ps.t[:, 0:W], sl.t[:, k, f * 128:(f + 1) * 128], uT.t[:, k, :],
                               k == 0, k == 7, [sl.b] + ub)
                        pc = pre[ci]
                        A("act", lambda e, ps=ps, pc=pc: e.activation(out=pc.t[:, 3:W + 3], in_=ps.t[:, 0:W],
                                                                      func=AF.Copy), w=[ps.b, pc.b])
                        yield
                        psc = psum()
                        for j in range(4):
                            mm(psc, psc.t[:, 0:W], dcw[:, ci * 4 + j, :], pc.t[:, j:j + W], j == 0, j == 3, [pc.b] + dcwb)
                        A("pool", lambda e, pc=pc: e.tensor_copy(out=pc.t[:, 0:3], in_=pc.t[:, W:W + 3]),
                          r=[pc.b], w=[pc.b])
                        y = yr()
                        A("act", lambda e, psc=psc, y=y: e.activation(out=y.t[:], in_=psc.t[:, 0:W], func=AF.Silu),
                          w=[psc.b, y.b])
                        if grp < 2:
                            q = sqb()
                            A("act", lambda e, y=y, q=q: e.activation(out=q.t[:], in_=y.t[:], func=AF.Square),
                              r=[y.b], w=[q.b])
                        yield
                        o = ob16()
                        if grp < 2:
                            ps2 = psum()
                            mm(ps2, ps2.t[:, 0:W], onesblk[:], q.t[:], True, True, [q.b, cst])
                            r_ = rr()
                            sc_, bi_ = (64.0, 64.0 * EPS) if grp == 0 else (1.0, EPS)
                            A("act", lambda e, ps2=ps2, r_=r_, sc_=sc_, bi_=bi_: e.activation(
                                out=r_.t[:], in_=ps2.t[:, 0:W], func=AF.Sqrt, scale=sc_, bias=bi_),
                              w=[ps2.b, r_.b])
                            A("dve", lambda e, r_=r_: e.reciprocal(out=r_.t[:], in_=r_.t[:]), r=[r_.b], w=[r_.b])
                            A("dve", lambda e, y=y, r_=r_, o=o: e.tensor_tensor(out=o.t[:], in0=y.t[:], in1=r_.t[:],
                                                                               op=ALU.mult),
                              r=[y.b, r_.b], w=[o.b])
                            dstT, dstb = (gqT, sc["gqT"]) if grp == 0 else (gkT, sc["gkT"])
                            DMA("pool", dstT[f * 128:(f + 1) * 128, t0:t0 + W], o.t[:], [o.b], dstb, own=o.b)
                        else:
                            A("pool", lambda e, y=y, o=o: e.tensor_copy(out=o.t[:], in_=y.t[:]), r=[y.b], w=[o.b])
                        yield
                        if grp >= 1:
                            stg = ktst if grp == 1 else vtst
                            ps3 = psum()
                            pv = ps3.t[:].bitcast(BF16)
                            for t in range(3):
                                tr(ps3, pv[:, t * 128:(t + 1) * 128], o.t[:, t * 128:(t + 1) * 128], identb[:],
                                   [o.b, cst])
                            A("act", lambda e, pv=pv, stg=stg, f=f: e.activation(
                                out=stg.t[:, :, f * 128:(f + 1) * 128],
                                in_=pv[:, 0:384].rearrange("p (t n) -> p t n", t=3), func=AF.Copy),
                              w=[ps3.b, stg.b])
                            if f == 3:
                                dd, db = (gktok, sc["gktok"]) if grp == 1 else (gvtok, sc["gvtok"])
                                DMA("pool", dd[t0:t0 + W, :].rearrange("(t p) d -> p t d", p=128), stg.t[:],
                                    [stg.b], db, own=stg.b)
                        yield

                    gens = []

                    def adv(g):
                        try:
                            next(g)
                        except StopIteration:
                            pass

                    def pump():
                        n_ = len(gens)
                        for back in (3, 5, 7):
                            if n_ >= back and n_ - back < 12 and gens[n_ - back] is not None:
                                adv(gens[n_ - back])

                    for grp in range(3):
                        sl = R["wslot"]()
                        c_lo = 416 + grp * 512
                        DMA("sp", sl.t[:], win_b[1 + grp], [winc], sl.b)
                        for f in range(4):
                            g_ = qkv_chunk(grp, f, sl)
                            gens.append(g_)
                            adv(g_)
                            pump()
                    qkv_tail = {"gens": gens, "k": 0}

                    def qkv_drain_step():
                        gens.append(None)
                        pump()

                    gts = [gst() for _ in range(3)]
                    ndrain = [0]
                    for gi, c_lo in enumerate([1968, 2480, 2992, 3504, 4016]):
                        sl = R["wslot"]()
                        DMA("sp", sl.t[:], win_b[4 + gi], [winc], sl.b)
                        for t in range(3):
                            ps = psum()
                            for k in range(8):
                                mm(ps, ps.t[:, :], uT.t[:, k, t * 128:(t + 1) * 128], sl.t[:, k, :],
                                   k == 0, k == 7, [sl.b, ub[t]])
                            if ndrain[0] < 6:
                                qkv_drain_step()
                                ndrain[0] += 1
                            if gi == 0:
                                z = zst()
                                A("act", lambda e, ps=ps, z=z: e.activation(out=z.t[:], in_=ps.t[:, :], func=AF.Silu),
                                  w=[ps.b, z.b])
                                DMA("pool", zs[t0 + t * 128:t0 + (t + 1) * 128, :], z.t[:], [z.b], sc["zs"], own=z.b)
                            else:
                                g_ = gts[t]
                                A("act", lambda e, ps=ps, g_=g_, gi=gi: e.activation(
                                    out=g_.t[:, (gi - 1) * 512:gi * 512], in_=ps.t[:, :], func=AF.Sigmoid),
                                  w=[ps.b, g_.b])
                                if gi == 4:
                                    DMA("pool", gates[t0 + t * 128:t0 + (t + 1) * 128, :], g_.t[:], [g_.b], sc["gates"], own=g_.b)

                def load_x(s_):
                    xt = xtr()
                    DMA("sp", xt.t[:], xin[s_ * W:(s_ + 1) * W, :].rearrange("(t p) d -> p t d", p=128), [], xt.b)
                    return xt

                xts = {0: load_x(0)}
                xnTs = {0: norm_tr(R, norm_chain(R, xts[0]), nrm_t[0])}
                dxn = None
                for s_ in range(NST):
                    if s_ >= 1:
                        t0p = (s_ - 1) * W
                        xp = xts[s_ - 1]
                        DMA("pool", h1s[t0p:t0p + W, :].rearrange("(t p) d -> p t d", p=128), xp.t[:], [xp.b], h1b,
                            own=xp.b)
                        dxn = norm_chain(R, xp)
                    hid = ffn_gu(R, xnTs[s_], 0)
                    if s_ + 1 < NST:
                        xts[s_ + 1] = load_x(s_ + 1)
                    axn = norm_chain(R, xts[s_ + 1]) if s_ + 1 < NST else None
                    if s_ >= 1:
                        uT = norm_tr(R, dxn, nrm_t[1])
                        inproj(s_ - 1, uT)
                    issue_late_casts(5)
                    if axn is not None:
                        xnTs[s_ + 1] = norm_tr(R, axn, nrm_t[0])
                    ffn_down(R, hid, xts[s_], 0)
                xp = xts[NST - 1]
                t0p = (NST - 1) * W
                DMA("pool", h1s[t0p:t0p + W, :].rearrange("(t p) d -> p t d", p=128), xp.t[:], [xp.b], h1b, own=xp.b)
                uT = norm_tr(R, norm_chain(R, xp), nrm_t[1])
                inproj(NST - 1, uT)
                issue_late_casts(1000)
                S.barrier()
                S.emit()

        if 2 in stages:
            with ExitStack() as st:
                mst = {"i": 0}
                gst_ = {"i": 0}

                def psum_m():
                    r = banks[mst["i"] % 3]
                    mst["i"] += 1
                    return r

                def psum_g():
                    r = banks[3 + gst_["i"] % 3]
                    gst_["i"] += 1
                    return r

                gsl = []
                for i in range(2):
                    gsl.append({
                        "qT": sbt(st, f"g_qT{i}", [64, 8, 128], BF16),
                        "kT": sbt(st, f"g_kT{i}", [64, 8, 128], BF16),
                        "kt": sbt(st, f"g_kt{i}", [128, 512], BF16),
                        "vt": sbt(st, f"g_vt{i}", [128, 512], BF16),
                        "zt": sbt(st, f"g_zt{i}", [128, 512], BF16),
                        "bt": sbt(st, f"g_bt{i}", [128, 8], F32),
                        "gt": sbt(st, f"g_gt{i}", [128, 8], F32),
                        "b": S.buf(f"gslot{i}"),
                    })
                GT = F32 if GDN_F32 else BF16
                gclr = ring(st, "gcl", [128, 48], F32, 2)
                kdr = ring(st, "kdec", [128, 512], BF16, 2)
                gonr = ring(st, "gon", [128, 8, 128], F32, 2)
                xgr = ring(st, "xg", [128, 4, 128], F32, 1)
                etsr = ring(st, "ets", [128, 4, 128], F32, 1)
                etir = ring(st, "eti", [128, 4, 128], F32, 1)
                GL = BF16
                mtr = ring(st, "mtp", [128, 8, 128], GL, 4)
                nnr = ring(st, "nnp", [128, 8, 128], GL, 4)
                mt32r = ring(st, "mt32", [128, 8, 128], F32, 2)
                n0fr = ring(st, "n0f", [128, 8, 128], F32, 1)
                g32r = ring(st, "g32", [128, 8, 128], F32, 1)
                x0tr = ring(st, "x0t", [128, 8, 128], F32, 1)
                z32r = ring(st, "z32", [128, 8, 128], F32, 2)
                zbr = ring(st, "zb", [128, 8, 128], GL, 4)
                tfr = ring(st, "tfin", [128, 8, 128], GT, 2)
                itr = ring(st, "intraT", [128, 8, 128], BF16, 2)
                rtr = ring(st, "rt", [128, 8, 64], F32, 1)
                rbr = ring(st, "Rb", [128, 8, 64], GT, 1)
                vnr = ring(st, "vnew", [128, 8, 64], BF16, 1)
                o32r = ring(st, "o32", [128, 8, 64], F32, 2)
                sqr_ = ring(st, "osq", [128, 8, 64], F32, 1)
                ssr = ring(st, "oss", [128, 16], F32, 2)
                obr = ring(st, "gob", [128, 512], BF16, 2)
                S32 = tb(st, "S32", [64, 8, 64], F32)
                Sb = tb(st, "Sb", [64, 8, 64], BF16)
                A("pool", lambda e: e.memset(S32.t[:], 0.0), w=[S32.b])
                A("pool", lambda e: e.memset(Sb.t[:], 0.0), w=[Sb.b])
                ogb = S.buf("ogdn")

                def bc3(ap2, n):
                    k = ap2.shape[1]
                    p = ap2.shape[0]
                    return ap2.unsqueeze(2).broadcast_to([p, k, n])

                def v4(ps):
                    return ps.t[:].rearrange("p (h n) -> p h n", h=4)

                def fr(ap):
                    return ap.bitcast(F32R) if (GDN_F32 and GDN_F32R) else ap

                def v8(ps):
                    return ps.t[:].rearrange("p (h d) -> p h d", h=8)

                def gdn_gen(par):
                    for n in range(par, NB if nb_lim is None else nb_lim, 2):
                        z32 = z32r()
                        r0 = n * 128
                        sl = gsl[n % 2]
                        sb_ = sl["b"]
                        DMA("sp", sl["qT"][:], gqT[:, r0:r0 + 128].rearrange("(h d) t -> d h t", d=64), [], sb_)
                        DMA("sp", sl["kT"][:], gkT[:, r0:r0 + 128].rearrange("(h d) t -> d h t", d=64), [], sb_)
                        DMA("sp", sl["kt"][:], gktok[r0:r0 + 128, :], [], sb_)
                        DMA("sp", sl["vt"][:], gvtok[r0:r0 + 128, :], [], sb_)
                        DMA("sp", sl["zt"][:], zs[r0:r0 + 128, :], [], sb_)
                        DMA("sp", sl["bt"][:], gbs[r0:r0 + 128, :], [], sb_)
                        DMA("sp", sl["gt"][:], ggs[r0:r0 + 128, :], [], sb_)
                        qT, kT, kt, vt, zt, bt, gt = (sl[k] for k in ["qT", "kT", "kt", "vt", "zt", "bt", "gt"])

                        psg = psum_g()
                        mm(psg, psg.t[:, 0:8], Uf[:], gt[:], True, True, [sb_, cst])
                        mm(psg, psg.t[:, 8:16], onesf[:], gt[:], True, True, [sb_, cst])
                        gcl = gclr()
                        A("dve", lambda e, psg=psg, gcl=gcl: e.tensor_copy(out=gcl.t[:, 0:16], in_=psg.t[:, 0:16]),
                          w=[psg.b, gcl.b])
                        A("act", lambda e, gcl=gcl: e.activation(out=gcl.t[:, 16:24], in_=gcl.t[:, 0:8], func=AF.Exp),
                          r=[gcl.b], w=[gcl.b])
                        A("dve", lambda e, gcl=gcl: e.tensor_tensor(out=gcl.t[:, 24:32], in0=gcl.t[:, 8:16],
                                                                    in1=gcl.t[:, 0:8], op=ALU.subtract),
                          r=[gcl.b], w=[gcl.b])
                        A("act", lambda e, gcl=gcl: e.activation(out=gcl.t[:, 24:32], in_=gcl.t[:, 24:32],
                                                                 func=AF.Exp), r=[gcl.b], w=[gcl.b])
                        A("act", lambda e, gcl=gcl: e.activation(out=gcl.t[:, 32:40], in_=gcl.t[:, 8:16], func=AF.Exp),
                          r=[gcl.b], w=[gcl.b])
                        kd = kdr()
                        A("pool", lambda e, kd=kd, kt=kt, gcl=gcl: e.tensor_tensor(
                            out=kd.t[:].rearrange("p (h d) -> p h d", h=8),
                            in0=kt[:].rearrange("p (h d) -> p h d", h=8),
                            in1=bc3(gcl.t[:, 24:32], 64), op=ALU.mult), r=[sb_, gcl.b], w=[kd.b])
                        yield

                        mt32 = mt32r()
                        mt = mtr()
                        it_ = itr()
                        gon = gonr()
                        A("dve", lambda e, gon=gon, gt=gt: e.tensor_tensor(
                            out=gon.t[:], in0=onesf[:, :].unsqueeze(1).broadcast_to([128, 8, 128]),
                            in1=bc3(gt[:, 0:8], 128), op=ALU.mult), r=[sb_, cst], w=[gon.b])
                        for g in range(2):
                            psb = psum_g()
                            for hh in range(4):
                                h = 4 * g + hh
                                mm(psb, psb.t[:, hh * 128:(hh + 1) * 128], gon.t[:, h, :], Uf[:], True, True,
                                   [gon.b, cst])
                            xg = xgr()
                            A("dve", lambda e, psb=psb, xg=xg, gcl=gcl, g=g: e.tensor_tensor(
                                out=xg.t[:], in0=v4(psb), in1=bc3(gcl.t[:, 4 * g:4 * g + 4], 128), op=ALU.subtract),
                              r=[gcl.b], w=[psb.b, xg.b])
                            A("dve", lambda e, xg=xg: e.tensor_tensor(
                                out=xg.t[:], in0=xg.t[:], in1=maskI[:, :].unsqueeze(1).broadcast_to([128, 4, 128]),
                                op=ALU.mult), r=[xg.b, cst], w=[xg.b])
                            yield
                            A("act", lambda e, xg=xg: e.activation(out=xg.t[:], in_=xg.t[:], func=AF.Exp),
                              r=[xg.b], w=[xg.b])
                            ets = etsr()
                            eti = etir()
                            A("dve", lambda e, xg=xg, ets=ets: e.tensor_tensor(
                                out=ets.t[:], in0=xg.t[:], in1=maskS[:, :].unsqueeze(1).broadcast_to([128, 4, 128]),
                                op=ALU.mult), r=[xg.b, cst], w=[ets.b])
                            A("pool", lambda e, xg=xg, eti=eti: e.tensor_tensor(
                                out=eti.t[:], in0=xg.t[:], in1=maskI[:, :].unsqueeze(1).broadcast_to([128, 4, 128]),
                                op=ALU.mult), r=[xg.b, cst], w=[eti.b])
                            A("pool", lambda e, ets=ets, bt=bt, g=g: e.tensor_tensor(
                                out=ets.t[:], in0=ets.t[:], in1=bc3(bt[:, 4 * g:4 * g + 4], 128), op=ALU.mult),
                              r=[ets.b, sb_], w=[ets.b])
                            yield
                            psk = psum_g()
                            psq = psum_g()
                            for hh in range(4):
                                h = 4 * g + hh
                                mm(psk, psk.t[:, hh * 128:(hh + 1) * 128], kT[:, h, :], kT[:, h, :], True, True, [sb_])
                                mm(psq, psq.t[:, hh * 128:(hh + 1) * 128], kT[:, h, :], qT[:, h, :], True, True, [sb_])
                            A("dve", lambda e, psk=psk, ets=ets, mt32=mt32, g=g: e.tensor_tensor(
                                out=mt32.t[:, 4 * g:4 * g + 4, :], in0=v4(psk), in1=ets.t[:], op=ALU.mult),
                              r=[ets.b], w=[psk.b, mt32.b])
                            A("dve", lambda e, mt32=mt32, mt=mt, g=g: e.tensor_copy(
                                out=mt.t[:, 4 * g:4 * g + 4, :], in_=mt32.t[:, 4 * g:4 * g + 4, :]),
                              r=[mt32.b], w=[mt.b])
                            A("dve", lambda e, psq=psq, eti=eti, it_=it_, g=g: e.tensor_tensor(
                                out=it_.t[:, 4 * g:4 * g + 4, :], in0=v4(psq), in1=eti.t[:], op=ALU.mult),
                              r=[eti.b], w=[psq.b, it_.b])
                            yield

                        nn = nnr()
                        pst = psum_g()
                        pvb = pst.t[:].bitcast(BF16)
                        for h in range(8):
                            tr(pst, pvb[:, h * 128:(h + 1) * 128], mt.t[:, h, :], identb[:], [mt.b, cst])
                        A("dve", lambda e, pvb=pvb, nn=nn, pst=pst: e.tensor_copy(
                            out=nn.t[:], in_=pvb.rearrange("p (h n) -> p h n", h=8)), w=[pst.b, nn.b])
                        zb = zbr()
                        A("dve", lambda e, zb=zb, mt32=mt32: e.tensor_tensor(
                            out=zb.t[:], in0=identf[:, :].unsqueeze(1).broadcast_to([128, 8, 128]), in1=mt32.t[:],
                            op=ALU.subtract), r=[mt32.b, cst], w=[zb.b])
                        yield
                        mtp, np_ = mt, nn
                        for l in range(1, 7):
                            n2 = nnr()
                            psn = [psum_g(), psum_g()]
                            for h in range(8):
                                mm(psn[h // 4], psn[h // 4].t[:, (h % 4) * 128:(h % 4 + 1) * 128], fr(mtp.t[:, h, :]),
                                   fr(np_.t[:, h, :]), True, True, [mtp.b, np_.b])
                            for g in range(2):
                                A("dve", lambda e, g=g, psn=psn, n2=n2: e.tensor_copy(
                                    out=n2.t[:, 4 * g:4 * g + 4, :], in_=v4(psn[g])),
                                  w=[psn[g].b, n2.b])
                            if l <= 5:
                                m2 = mtr()
                                psm = [psum_g(), psum_g()]
                                for h in range(8):
                                    mm(psm[h // 4], psm[h // 4].t[:, (h % 4) * 128:(h % 4 + 1) * 128], fr(np_.t[:, h, :]),
                                       fr(mtp.t[:, h, :]), True, True, [mtp.b, np_.b])
                                for g in range(2):
                                    A("dve", lambda e, g=g, psm=psm, m2=m2: e.tensor_copy(
                                        out=m2.t[:, 4 * g:4 * g + 4, :], in_=v4(psm[g])), w=[psm[g].b, m2.b])
                            else:
                                m2 = None
                            yield
                            psz = [psum_g(), psum_g()]
                            for h in range(8):
                                mm(psz[h // 4], psz[h // 4].t[:, (h % 4) * 128:(h % 4 + 1) * 128], fr(n2.t[:, h, :]),
                                   fr(zb.t[:, h, :]), True, True, [n2.b, zb.b])
                            zbn = zbr()
                            for g in range(2):
                                A("dve", lambda e, zb=zb, zbn=zbn, g=g, psz=psz: e.tensor_tensor(
                                    out=zbn.t[:, 4 * g:4 * g + 4, :], in0=zb.t[:, 4 * g:4 * g + 4, :],
                                    in1=v4(psz[g]), op=ALU.add), r=[zb.b], w=[psz[g].b, zbn.b])
                            zb = zbn
                            if l == 6:
                                A("act", lambda e, z32=z32, zb=zb: e.activation(out=z32.t[:], in_=zb.t[:],
                                                                                func=AF.Copy),
                                  r=[zb.b], w=[z32.b])
                            mtp, np_ = m2, n2
                            yield
                        n0f = n0fr()
                        for g in range(2):
                            pst = psum_g()
                            for hh in range(4):
                                tr(pst, pst.t[:, hh * 128:(hh + 1) * 128], mt32.t[:, 4 * g + hh, :], identf[:],
                                   [mt32.b, cst])
                            A("dve", lambda e, pst=pst, n0f=n0f, g=g: e.tensor_tensor(
                                out=n0f.t[:, 4 * g:4 * g + 4, :], in0=v4(pst),
                                in1=identf[:, :].unsqueeze(1).broadcast_to([128, 4, 128]), op=ALU.add),
                              r=[cst], w=[pst.b, n0f.b])
                        g32 = g32r()
                        yield
                        pse = [psum_g(), psum_g()]
                        for h in range(8):
                            mm(pse[h // 4], pse[h // 4].t[:, (h % 4) * 128:(h % 4 + 1) * 128], n0f.t[:, h, :],
                               z32.t[:, h, :], True, True, [n0f.b, z32.b])
                        for g in range(2):
                            A("dve", lambda e, g=g, pse=pse, g32=g32: e.tensor_tensor(
                                out=g32.t[:, 4 * g:4 * g + 4, :],
                                in0=identf[:, :].unsqueeze(1).broadcast_to([128, 4, 128]), in1=v4(pse[g]),
                                op=ALU.subtract), r=[cst], w=[pse[g].b, g32.b])
                        x0t = x0tr()
                        for g in range(2):
                            pst = psum_g()
                            for hh in range(4):
                                tr(pst, pst.t[:, hh * 128:(hh + 1) * 128], z32.t[:, 4 * g + hh, :], identf[:],
                                   [z32.b, cst])
                            A("dve", lambda e, pst=pst, x0t=x0t, g=g: e.tensor_copy(
                                out=x0t.t[:, 4 * g:4 * g + 4, :], in_=v4(pst)), w=[pst.b, x0t.b])
                        yield
                        tf = tfr()
                        psx = [psum_g(), psum_g()]
                        for h in range(8):
                            mm(psx[h // 4], psx[h // 4].t[:, (h % 4) * 128:(h % 4 + 1) * 128], x0t.t[:, h, :],
                               g32.t[:, h, :], True, True, [x0t.b, g32.b])
                        for g in range(2):
                            A("dve", lambda e, z32=z32, g=g, psx=psx, tf=tf: e.tensor_tensor(
                                out=tf.t[:, 4 * g:4 * g + 4, :], in0=z32.t[:, 4 * g:4 * g + 4, :], in1=v4(psx[g]),
                                op=ALU.add), r=[z32.b], w=[psx[g].b, tf.b])
                        yield

                        psks = psum_g()
                        for h in range(8):
                            mm(psks, psks.t[:, h * 64:(h + 1) * 64], kT[:, h, :], Sb.t[:, h, :], True, True,
                               [sb_, Sb.b])
                        rt = rtr()
                        A("dve", lambda e, psks=psks, rt=rt, gcl=gcl: e.tensor_tensor(
                            out=rt.t[:], in0=v8(psks), in1=bc3(gcl.t[:, 16:24], 64), op=ALU.mult),
                          r=[gcl.b], w=[psks.b, rt.b])
                        psqs = psum_g()
                        for h in range(8):
                            mm(psqs, psqs.t[:, h * 64:(h + 1) * 64], qT[:, h, :], Sb.t[:, h, :], True, True,
                               [sb_, Sb.b])
                        o32 = o32r()
                        A("dve", lambda e, psqs=psqs, o32=o32, gcl=gcl: e.tensor_tensor(
                            out=o32.t[:], in0=v8(psqs), in1=bc3(gcl.t[:, 16:24], 64), op=ALU.mult),
                          r=[gcl.b], w=[psqs.b, o32.b])
                        Rb = rbr()
                        A("dve", lambda e, rt=rt, Rb=Rb, vt=vt: e.tensor_tensor(
                            out=Rb.t[:], in0=vt[:].rearrange("p (h d) -> p h d", h=8), in1=rt.t[:], op=ALU.subtract),
                          r=[rt.b, sb_], w=[Rb.b])
                        yield
                        pstr = psum_g()
                        for h in range(8):
                            mm(pstr, pstr.t[:, h * 64:(h + 1) * 64], fr(tf.t[:, h, :]), fr(Rb.t[:, h, :]), True, True,
                               [tf.b, Rb.b])
                        vn = vnr()
                        A("dve", lambda e, pstr=pstr, vn=vn, bt=bt: e.tensor_tensor(
                            out=vn.t[:], in0=v8(pstr), in1=bc3(bt[:, 0:8], 64), op=ALU.mult),
                          r=[sb_], w=[pstr.b, vn.b])
                        yield
                        psd = psum_g()
                        for h in range(8):
                            mm(psd, psd.t[0:64, h * 64:(h + 1) * 64], kd.t[:, h * 64:(h + 1) * 64], vn.t[:, h, :],
                               True, True, [kd.b, vn.b])
                        A("dve", lambda e, gcl=gcl: e.tensor_tensor(
                            out=S32.t[:], in0=S32.t[:], in1=gcl.t[0:64, 32:40].unsqueeze(2).broadcast_to([64, 8, 64]),
                            op=ALU.mult), r=[gcl.b, S32.b], w=[S32.b])
                        A("dve", lambda e, psd=psd: e.tensor_tensor(
                            out=S32.t[:], in0=S32.t[:], in1=psd.t[0:64, :].rearrange("p (h d) -> p h d", h=8),
                            op=ALU.add), r=[S32.b], w=[psd.b, S32.b])
                        A("dve", lambda e: e.tensor_copy(out=Sb.t[:], in_=S32.t[:]), r=[S32.b], w=[Sb.b])
                        pso = psum_g()
                        for h in range(8):
                            mm(pso, pso.t[:, h * 64:(h + 1) * 64], it_.t[:, h, :], vn.t[:, h, :], True, True,
                               [it_.b, vn.b])
                        A("dve", lambda e, pso=pso, o32=o32: e.tensor_tensor(
                            out=o32.t[:], in0=o32.t[:], in1=v8(pso), op=ALU.add), r=[o32.b], w=[pso.b, o32.b])
                        yield
                        sq = sqr_()
                        A("pool", lambda e, o32=o32, sq=sq: e.tensor_tensor(out=sq.t[:], in0=o32.t[:], in1=o32.t[:],
                                                                            op=ALU.mult), r=[o32.b], w=[sq.b])
                        ss = ssr()
                        A("dve", lambda e, sq=sq, ss=ss: e.tensor_reduce(out=ss.t[:, 0:8], in_=sq.t[:], axis=AX.X,
                                                                         op=ALU.add), r=[sq.b], w=[ss.b])
                        A("act", lambda e, ss=ss: e.activation(out=ss.t[:, 8:16], in_=ss.t[:, 0:8], func=AF.Sqrt,
                                                               scale=1.0 / 64, bias=EPS), r=[ss.b], w=[ss.b])
                        A("dve", lambda e, ss=ss: e.reciprocal(out=ss.t[:, 8:16], in_=ss.t[:, 8:16]),
                          r=[ss.b], w=[ss.b])
                        A("dve", lambda e, o32=o32, ss=ss: e.tensor_tensor(
                            out=o32.t[:], in0=o32.t[:], in1=bc3(ss.t[:, 8:16], 64), op=ALU.mult),
                          r=[ss.b, o32.b], w=[o32.b])
                        A("pool", lambda e, o32=o32: e.tensor_tensor(
                            out=o32.t[:], in0=o32.t[:], in1=gnorm_t[:, :].unsqueeze(1).broadcast_to([128, 8, 64]),
                            op=ALU.mult), r=[o32.b, prm], w=[o32.b])
                        ob = obr()
                        A("pool", lambda e, o32=o32, ob=ob, zt=zt: e.tensor_tensor(
                            out=ob.t[:].rearrange("p (h d) -> p h d", h=8), in0=o32.t[:],
                            in1=zt[:].rearrange("p (h d) -> p h d", h=8), op=ALU.mult), r=[o32.b, sb_], w=[ob.b])
                        DMA("pool", ogdn[r0:r0 + 128, :], ob.t[:], [ob.b], ogb, own=ob.b)
                        yield

                ggs_ = [gdn_gen(0), gdn_gen(1)]
                gstate = {"done": False, "d": [False, False], "i": 0}

                def gdn_step(k=2):
                    for _ in range(k):
                        if gstate["done"]:
                            return
                        i = gstate["i"]
                        gstate["i"] += 1
                        w = 0 if i < 11 else (i - 11 + 1) % 2
                        if gstate["d"][w]:
                            w = 1 - w
                        try:
                            next(ggs_[w])
                        except StopIteration:
                            gstate["d"][w] = True
                            gstate["done"] = all(gstate["d"])

                Kt = [tb(st, f"Kh{h}", [96, LP], BF16) for h in range(4)]
                Va = tb(st, "Vaug", [128, NB, 4, 65], BF16)
                ckv = tb(st, "ckv", [128, LP], BF16)
                wuk_t = sbt(st, "wuk_t", [128, 512], BF16)
                wuv_t = sbt(st, "wuv_t", [128, 512], BF16)
                wuq_t = sbt(st, "wuq_t", [128, 2, 768], BF16)
                wuqs_t = sbt(st, "wuqs_t", [128, 2, 768], BF16)
                wb = S.buf("mlaw")
                DMA("pool", wuk_t[:], wuk_f, [], wb)
                DMA("pool", wuv_t[:], wuv_f, [], wb)
                DMA("pool", wuq_t[:], wuq_f.rearrange("(c p) n -> p c n", p=128), [], wb)
                DMA("pool", wuqs_t[:], wuqs_f.rearrange("(c p) n -> p c n", p=128), [], wb)
                for s_ in range(3):
                    DMA("sp", ckv.t[:, s_ * 1408:(s_ + 1) * 1408], ckvnT[:, s_ * 1408:(s_ + 1) * 1408], [], ckv.b)
                A("pool", lambda e: e.memset(Va.t[:, :, :, 64:65], 1.0), w=[Va.b])
                cqr = ring(st, "cqs", [128, 2, W], BF16, 2)
                tcr = ring(st, "mtabc", [96, W], F32, 2)
                tsr = ring(st, "mtabs", [96, W], F32, 2)
                qtr = ring(st, "QT", [96, W], BF16, 3)
                t1r = ring(st, "qt1", [96, W], F32, 2)
                t2r = ring(st, "qt2", [96, W], F32, 2)
                ptr = ring(st, "PT", [128, W], BF16, 5)
                rcr = ring(st, "rc", [128, 4], F32, 2)
                omr = ring(st, "omst", [128, 3, 256], BF16, 2)
                omb = S.buf("omla")
                sm_scale = float(96 ** -0.5)
                do_mla = 3 in stages
                kbc = 0
                for half in range(2 if do_mla else 0):
                    ei = 0
                    for hl in range(4):
                        h = half * 4 + hl
                        DMA("sp", Kt[hl].t[64:96, :], kropeT[:, :], [], Kt[hl].b)
                        for s_ in range(NST):
                            ps = psum_m()
                            mm(ps, ps.t[0:64, 0:W], wuk_t[:, h * 64:(h + 1) * 64], ckv.t[:, s_ * W:(s_ + 1) * W],
                               True, True, [wb, ckv.b])
                            if ei % 2 == 0:
                                A("act", lambda e, ps=ps, hl=hl, s_=s_: e.activation(
                                    out=Kt[hl].t[0:64, s_ * W:(s_ + 1) * W], in_=ps.t[0:64, 0:W], func=AF.Copy),
                                  w=[ps.b, Kt[hl].b])
                            else:
                                A("dve", lambda e, ps=ps, hl=hl, s_=s_: e.tensor_copy(
                                    out=Kt[hl].t[0:64, s_ * W:(s_ + 1) * W], in_=ps.t[0:64, 0:W]),
                                  w=[ps.b, Kt[hl].b])
                            ei += 1
                            if ei % 4 == 0:
                                gdn_step()
                    for blk in range(NB):
                        ps = psum_m()
                        mm(ps, ps.t[:, 0:256], ckv.t[:, blk * 128:(blk + 1) * 128],
                           wuv_t[:, half * 256:(half + 1) * 256], True, True, [wb, ckv.b])
                        if blk % 2 == 0:
                            A("act", lambda e, ps=ps, blk=blk: e.activation(
                                out=Va.t[:, blk, :, 0:64], in_=ps.t[:, 0:256].rearrange("p (h d) -> p h d", h=4),
                                func=AF.Copy), w=[ps.b, Va.b])
                        else:
                            A("dve", lambda e, ps=ps, blk=blk: e.tensor_copy(
                                out=Va.t[:, blk, :, 0:64], in_=ps.t[:, 0:256].rearrange("p (h d) -> p h d", h=4)),
                              w=[ps.b, Va.b])
                        if blk % 4 == 3:
                            gdn_step()
                    for s_ in range(NST if nst_lim is None else nst_lim):
                        t0 = s_ * W
                        cq = cqr()
                        DMA("sp", cq.t[:], cqnT[:, t0:t0 + W].rearrange("(c p) t -> p c t", p=128), [], cq.b)
                        tc_ = tcr()
                        ts_ = tsr()
                        DMA("sp", tc_.t[64:96, :], cosT_d[:, t0:t0 + W], [], tc_.b)
                        DMA("sp", ts_.t[64:96, :], sinT_d[:, t0:t0 + W], [], ts_.b)
                        om = omr()

                        def prep_q(hl, cq=cq, tc_=tc_, ts_=ts_, half=half):
                            h = half * 4 + hl
                            psq = banks[7]
                            for c in range(2):
                                mm(psq, psq.t[0:96, 0:W], wuq_t[:, c, h * 96:(h + 1) * 96], cq.t[:, c, :], c == 0,
                                   c == 1, [wb, cq.b])
                            QT = qtr()
                            A("act", lambda e, psq=psq, QT=QT: e.activation(out=QT.t[0:64, :], in_=psq.t[0:64, 0:W],
                                                                            func=AF.Copy), w=[psq.b, QT.b])
                            t1 = t1r()
                            t2 = t2r()
                            A("dve", lambda e, psq=psq, t1=t1, tc_=tc_: e.tensor_tensor(
                                out=t1.t[64:96, :], in0=psq.t[64:96, 0:W], in1=tc_.t[64:96, :], op=ALU.mult),
                              r=[tc_.b], w=[psq.b, t1.b])
                            psq2 = banks[7]
                            for c in range(2):
                                mm(psq2, psq2.t[0:96, 0:W], wuqs_t[:, c, h * 96:(h + 1) * 96], cq.t[:, c, :], c == 0,
                                   c == 1, [wb, cq.b])
                            A("dve", lambda e, psq2=psq2, t2=t2, ts_=ts_: e.tensor_tensor(
                                out=t2.t[64:96, :], in0=psq2.t[64:96, 0:W], in1=ts_.t[64:96, :], op=ALU.mult),
                              r=[ts_.b], w=[psq2.b, t2.b])
                            A("dve", lambda e, t1=t1, t2=t2, QT=QT: e.tensor_tensor(
                                out=QT.t[64:96, :], in0=t1.t[64:96, :], in1=t2.t[64:96, :], op=ALU.add),
                              r=[t1.b, t2.b, QT.b], w=[QT.b])
                            return QT

                        QTn = prep_q(0)
                        for hl in range(4):
                            h = half * 4 + hl
                            QT = QTn
                            pso = banks[6]
                            mm(pso, pso.t[:, 0:195], zerob[:, 0:128], zerob[:, 0:195], True, True, [cst])
                            nkb = 3 * s_ + 3

                            def issue_pss(kb, hl=hl, QT=QT, s_=s_):
                                i0 = max(0, kb - 3 * s_)
                                ncols = (3 - i0) * 128
                                pss = psum_m()
                                diag = kb >= 3 * s_
                                mm(pss, pss.t[:, 0:ncols], Kt[hl].t[0:96, kb * 128:(kb + 1) * 128],
                                   QT.t[0:96, i0 * 128:W], True, not diag, [Kt[hl].b, QT.b])
                                if diag:
                                    mm(pss, pss.t[:, 0:128], identb[:], negmask[:], False, True, [cst])
                                return pss, i0, ncols

                            pend = [issue_pss(0)]
                            if nkb > 1:
                                pend.append(issue_pss(1))
                            for kb in range(nkb):
                                pss, i0, ncols = pend.pop(0)
                                if kb + 2 < nkb:
                                    pend.append(issue_pss(kb + 2))
                                if kb == min(2, nkb - 1) and hl + 1 < 4:
                                    QTn = prep_q(hl + 1)
                                PT = ptr()
                                A("act", lambda e, pss=pss, PT=PT, i0=i0, ncols=ncols: e.activation(
                                    out=PT.t[:, i0 * 128:W], in_=pss.t[:, 0:ncols], func=AF.Exp, scale=sm_scale),
                                  w=[pss.b, PT.b])
                                for i in range(i0, 3):
                                    mm(pso, pso.t[:, i * 65:(i + 1) * 65], PT.t[:, i * 128:(i + 1) * 128],
                                       Va.t[:, kb, hl, :], False, False, [PT.b, Va.b], skip=True)
                                kbc += 0.56
                                while kbc >= 1.0:
                                    gdn_step(1)
                                    kbc -= 1.0
                            rc = rcr()
                            A("dve", lambda e, pso=pso, rc=rc: e.reciprocal(
                                out=rc.t[:, 0:3], in_=pso.t[:, 0:195].rearrange("p (i d) -> p i d", i=3)[:, :, 64]),
                              w=[pso.b, rc.b])
                            A("dve", lambda e, pso=pso, rc=rc, om=om, hl=hl: e.tensor_tensor(
                                out=om.t[:, :, hl * 64:(hl + 1) * 64],
                                in0=pso.t[:, 0:195].rearrange("p (i d) -> p i d", i=3)[:, :, 0:64],
                                in1=rc.t[:, 0:3].unsqueeze(2).broadcast_to([128, 3, 64]), op=ALU.mult),
                              r=[rc.b], w=[pso.b, om.b])
                        DMA("pool", omla[t0:t0 + W, half * 256:(half + 1) * 256].rearrange("(t p) d -> p t d", p=128),
                            om.t[:], [om.b], omb, own=om.b)
                while not gstate["done"]:
                    gdn_step()
                S.barrier()
                S.emit()

        if 4 in stages:
            with ExitStack() as st:
                R = {
                    "xnT": ring(st, "xnT3", [128, 8, W], BF16, 2),
                    "junk": ring(st, "junk3", [128, D], BF16, 2),
                    "ss": ring(st, "ss3", [128, 4], F32, 4),
                    "xn": ring(st, "xn3", [128, D], BF16, 3),
                    "hid": ring(st, "hid3", [128, NF, W], BF16, 1),
                    "wslot": ring(st, "wslot3", [128, 8, 512], BF16, 6),
                    "sil": ring(st, "sil3", [128, W], F32, 2),
                    "hb": [S.buf(f"hid3_{f}") for f in range(NF)],
                }
                wmo_t = sbt(st, "wmo_t", [128, 4, D], BF16)
                wgo_t = sbt(st, "wgo_t", [128, 4, D], BF16)
                wout_t = sbt(st, "wout_t", [128, 8, D], BF16)
                fnorm_t = sbt(st, "fnorm_t", [128, D], F32)
                w3 = S.buf("w3")
                DMA("pool", wmo_t[:], wmo_f.rearrange("(c p) n -> p c n", p=128), [], w3)
                DMA("pool", wgo_t[:], wgo_f.rearrange("(c p) n -> p c n", p=128), [], w3)
                for c in range(8):
                    DMA("pool", wout_t[:, c, :], wout_f[c * 128:(c + 1) * 128, :], [], w3)
                DMA("sp", fnorm_t[:], fnorm_d, [], w3)
                htr = ring(st, "ht", [128, 3, D], F32, 2)
                omr3 = ring(st, "om3", [128, 3, 512], BF16, 2)
                ogr3 = ring(st, "og3", [128, 3, 512], BF16, 2)
                gtr3 = ring(st, "gt3", [128, 2048], BF16, 2)
                oTr = ring(st, "oT3", [128, 8, 128], BF16, 2)
                mar = ring(st, "ma3", [128, 512], F32, 2)
                mbr = ring(st, "mb3", [128, 512], F32, 2)
                mrr = ring(st, "mrg3", [128, D], BF16, 2)
                mTr = ring(st, "mT3", [128, 8, 128], BF16, 2)
                outr = ring(st, "ost3", [128, D], F32, 2)
                outb = S.buf("out")
                def merge(s_):
                    t0 = s_ * W
                    ht = htr()
                    DMA("sp", ht.t[:], h1s[t0:t0 + W, :].rearrange("(t p) d -> p t d", p=128), [], ht.b)
                    om = omr3()
                    og = ogr3()
                    DMA("sp", om.t[:], omla[t0:t0 + W, :].rearrange("(t p) d -> p t d", p=128), [], om.b)
                    DMA("sp", og.t[:], ogdn[t0:t0 + W, :].rearrange("(t p) d -> p t d", p=128), [], og.b)
                    for t in range(3):
                        gt_ = gtr3()
                        DMA("sp", gt_.t[:], gates[t0 + t * 128:t0 + (t + 1) * 128, :], [], gt_.b)
                        ps = psum()
                        pv = ps.t[:].bitcast(BF16)
                        for c in range(4):
                            tr(ps, pv[:, c * 128:(c + 1) * 128], om.t[:, t, c * 128:(c + 1) * 128], identb[:],
                               [om.b, cst])
                        for c in range(4):
                            tr(ps, pv[:, (4 + c) * 128:(5 + c) * 128], og.t[:, t, c * 128:(c + 1) * 128], identb[:],
                               [og.b, cst])
                        oT = oTr()
                        A("act", lambda e, pv=pv, oT=oT: e.activation(
                            out=oT.t[:], in_=pv.rearrange("p (c n) -> p c n", c=8), func=AF.Copy), w=[ps.b, oT.b])
                        mrg = mrr()
                        for half in range(2):
                            hs = slice(half * 512, (half + 1) * 512)
                            psm = psum()
                            for c in range(4):
                                mm(psm, psm.t[:, :], oT.t[:, c, :], wmo_t[:, c, hs], c == 0, c == 3, [oT.b, w3])
                            psg = psum()
                            for c in range(4):
                                mm(psg, psg.t[:, :], oT.t[:, 4 + c, :], wgo_t[:, c, hs], c == 0, c == 3, [oT.b, w3])
                            ma = mar()
                            mb = mbr()
                            A("dve", lambda e, psm=psm, ma=ma, gt_=gt_, half=half: e.tensor_tensor(
                                out=ma.t[:], in0=psm.t[:, :], in1=gt_.t[:, half * 512:(half + 1) * 512], op=ALU.mult),
                              r=[gt_.b], w=[psm.b, ma.b])
                            A("dve", lambda e, psg=psg, mb=mb, gt_=gt_, half=half: e.tensor_tensor(
                                out=mb.t[:], in0=psg.t[:, :], in1=gt_.t[:, 1024 + half * 512:1024 + (half + 1) * 512],
                                op=ALU.mult), r=[gt_.b], w=[psg.b, mb.b])
                            A("pool", lambda e, ma=ma, mb=mb, mrg=mrg, half=half: e.tensor_tensor(
                                out=mrg.t[:, half * 512:(half + 1) * 512], in0=ma.t[:], in1=mb.t[:], op=ALU.add),
                              r=[ma.b, mb.b], w=[mrg.b])
                        ps2 = psum()
                        pv2 = ps2.t[:].bitcast(BF16)
                        for c in range(8):
                            tr(ps2, pv2[:, c * 128:(c + 1) * 128], mrg.t[:, c * 128:(c + 1) * 128], identb[:],
                               [mrg.b, cst])
                        mT = mTr()
                        A("act", lambda e, pv2=pv2, mT=mT: e.activation(
                            out=mT.t[:], in_=pv2.rearrange("p (c n) -> p c n", c=8), func=AF.Copy), w=[ps2.b, mT.b])
                        for half in range(2):
                            pso = psum()
                            for c in range(8):
                                mm(pso, pso.t[:, :], mT.t[:, c, :], wout_t[:, c, half * 512:(half + 1) * 512], c == 0,
                                   c == 7, [mT.b, w3])
                            A("dve", lambda e, pso=pso, ht=ht, t=t, half=half: e.tensor_tensor(
                                out=ht.t[:, t, half * 512:(half + 1) * 512], in0=ht.t[:, t, half * 512:(half + 1) * 512],
                                in1=pso.t[:, :], op=ALU.add), r=[ht.b], w=[pso.b, ht.b])
                    return ht

                def final(s_, ht):
                    t0 = s_ * W
                    for t in range(3):
                        junk = R["junk"]()
                        ss = R["ss"]()
                        A("dve", lambda e, ss=ss: e.memset(ss.t[:], 0.0), w=[ss.b])
                        A("act", lambda e, t=t, junk=junk, ss=ss, ht=ht: e.activation(
                            out=junk.t[:], in_=ht.t[:, t, :], func=AF.Square, accum_out=ss.t[:, 0:1]),
                          r=[ht.b, ss.b], w=[junk.b, ss.b])
                        A("act", lambda e, ss=ss: e.activation(out=ss.t[:, 1:2], in_=ss.t[:, 0:1], func=AF.Sqrt,
                                                               scale=1.0 / D, bias=EPS), r=[ss.b], w=[ss.b])
                        A("dve", lambda e, ss=ss: e.reciprocal(out=ss.t[:, 2:3], in_=ss.t[:, 1:2]), r=[ss.b], w=[ss.b])
                        ot = outr()
                        A("dve", lambda e, ot=ot, ht=ht, t=t, ss=ss: e.scalar_tensor_tensor(
                            out=ot.t[:], in0=ht.t[:, t, :], scalar=ss.t[:, 2:3], in1=fnorm_t[:], op0=ALU.mult,
                            op1=ALU.mult), r=[ht.b, ss.b, w3], w=[ot.b])
                        DMA("pool", out_d[t0 + t * 128:t0 + (t + 1) * 128, :], ot.t[:], [ot.b], outb, own=ot.b)

                hts = {0: merge(0)}
                xnT2 = {0: norm_tr(R, norm_chain(R, hts[0]), nrm_t[2])}
                for s_ in range(NST):
                    hid = ffn_gu(R, xnT2[s_], 1)
                    axn = None
                    if s_ + 1 < NST:
                        hts[s_ + 1] = merge(s_ + 1)
                        axn = norm_chain(R, hts[s_ + 1])
                    ffn_down(R, hid, hts[s_], 1)
                    final(s_, hts[s_])
                    if axn is not None:
                        xnT2[s_ + 1] = norm_tr(R, axn, nrm_t[2])
                S.barrier()
                S.emit()

        S.barrier()
        S.emit()
    return nc


def host_inputs(inp, b):
    f32 = np.float32
    x = np.asarray(inp["x"], f32)
    xin = np.zeros((LP, D), f32)
    xin[:NMETA] = np.asarray(inp["meta_tokens"], f32)
    xin[NMETA:LREAL] = x[b]
    win = np.asarray(inp["w_in"], f32)[0]
    kr = win[:, 384:416]
    win_ext = np.concatenate([win, kr[:, 16:32], kr[:, 0:16]], axis=1)
    wuq = np.asarray(inp["w_uq"], f32)[0].reshape(256, 8, 96)
    wuqs = np.zeros_like(wuq)
    wuqs[:, :, 64:80] = wuq[:, :, 80:96]
    wuqs[:, :, 80:96] = wuq[:, :, 64:80]
    wukv = np.asarray(inp["w_ukv"], f32)[0].reshape(128, 8, 128)

    def pc(v, c):
        return np.ascontiguousarray(np.asarray(v, f32).reshape(c, 128).T)

    pos = np.arange(LP, dtype=f32)
    inv = (np.float32(10000.0) ** (-np.arange(0, 32, 2, dtype=f32) / np.float32(32))).astype(f32)
    ang = (pos[:, None] * inv[None, :]).astype(f32)
    cos = np.cos(ang).astype(f32).T
    sin = np.sin(ang).astype(f32).T
    cosT = np.concatenate([cos, cos], 0)
    sinT = np.concatenate([-sin, sin], 0)
    cwv = np.asarray(inp["conv_w"], f32)[0]
    cw = np.ascontiguousarray(cwv.reshape(4, 12, 128).transpose(2, 1, 0).reshape(128, 48))
    m = {
        "xin": xin,
        "wg1": np.asarray(inp["ffn1_w_gate"], f32)[0], "wu1": np.asarray(inp["ffn1_w_up"], f32)[0],
        "wd1": np.asarray(inp["ffn1_w_down"], f32)[0],
        "wg2": np.asarray(inp["ffn2_w_gate"], f32)[0], "wu2": np.asarray(inp["ffn2_w_up"], f32)[0],
        "wd2": np.asarray(inp["ffn2_w_down"], f32)[0],
        "win": win_ext,
        "wuq": wuq.reshape(256, 768), "wuqs": wuqs.reshape(256, 768),
        "wuk": np.ascontiguousarray(wukv[:, :, 0:64].reshape(128, 512)),
        "wuv": np.ascontiguousarray(wukv[:, :, 64:128].reshape(128, 512)),
        "wmo": np.asarray(inp["w_mla_o"], f32)[0], "wgo": np.asarray(inp["w_gdn_o"], f32)[0],
        "wout": np.asarray(inp["w_out"], f32)[0],
        "nrm1": pc(inp["ffn1_norm"][0], 8), "nrmm": pc(inp["mix_norm"][0], 8), "nrm2": pc(inp["ffn2_norm"][0], 8),
        "qn": pc(inp["q_norm"][0], 2), "kvn": pc(inp["kv_norm"][0], 1),
        "cw": cw,
        "alog": np.ascontiguousarray(np.broadcast_to(np.asarray(inp["a_log"], f32)[0][None, :], (128, 8))),
        "dtb": np.ascontiguousarray(np.broadcast_to(np.asarray(inp["dt_bias"], f32)[0][None, :], (128, 8))),
        "gnorm": np.ascontiguousarray(np.broadcast_to(np.asarray(inp["gdn_norm"], f32)[0][None, :], (128, 64))),
        "fnorm": np.ascontiguousarray(np.broadcast_to(np.asarray(inp["final_norm"], f32)[None, :], (128, D))),
        "cosT": np.ascontiguousarray(cosT), "sinT": np.ascontiguousarray(sinT),
    }
    return {k: np.ascontiguousarray(v, dtype=f32) for k, v in m.items()}


def kernel(**inputs):
    nc = build()
    in_maps = [host_inputs(inputs, b) for b in range(NCORES)]
    res = run_bass_kernel_spmd(nc, in_maps, core_ids=list(range(NCORES)))
    out = np.stack([np.asarray(res.results[b]["out"])[NMETA:LREAL] for b in range(NCORES)], axis=0)
    return out.astype(np.float32)
```
